# Optimizing a Trainium2 kernel written in Bass

```python
import math
import jax
import jax.numpy as jnp
from jax import lax
import numpy as np

D_MODEL = 1024
BATCH = 16
SEQ = 4096
DEPTH = 1

GRID_W = 64
CTX_LEN = 256
NORM_EPS = 1e-6

POOL_WINDOWS = (2, 4, 8, 16)
N_POOL_GROUPS = 4
POOL_WIDTH = D_MODEL
POOL_GROUP = POOL_WIDTH // N_POOL_GROUPS

SSD_EXPAND = 2
D_INNER = SSD_EXPAND * D_MODEL
HEAD_DIM = 64
N_HEADS = D_INNER // HEAD_DIM
D_STATE = 128
N_BC_GROUPS = 4
CONV_K = 4
CONV_LEFT = CONV_K // 2
CHUNK = 128
N_DIR = 2
SSD_NORM_GROUPS = N_BC_GROUPS
CONV_DIM = D_INNER + 2 * N_BC_GROUPS * D_STATE

N_BRANCH = 2
OFF_POOL_V = 0
OFF_POOL_Z = OFF_POOL_V + POOL_WIDTH
OFF_SSD_Z = OFF_POOL_Z + POOL_WIDTH
OFF_GATE = OFF_SSD_Z + D_INNER
OFF_XBC = OFF_GATE + N_BRANCH * D_MODEL
OFF_DT = OFF_XBC + CONV_DIM
IN_COLS = OFF_DT + N_DIR * N_HEADS

kernel_name = 'hybrid_pool_ssd_diffusion_block'


def rmsnorm(x, w):
    xf = x.astype(jnp.float32)
    y = xf * lax.rsqrt(jnp.mean(xf * xf, axis=-1, keepdims=True) + NORM_EPS)
    return (y * w.astype(jnp.float32)).astype(x.dtype)


def adaln(cond, w_ada, b_ada):
    mod = jax.nn.silu(cond) @ w_ada + b_ada
    return jnp.split(mod, 3, axis=-1)


def centred_dwconv(u, w, b):
    l = u.shape[1]
    up = jnp.pad(u, ((0, 0), (CONV_LEFT, CONV_K - 1 - CONV_LEFT), (0, 0)))
    out = up[:, 0:l] * w[0]
    for k in range(1, CONV_K):
        out = out + up[:, k:k + l] * w[k]
    return out + b


def box_mean(v, k, axis):
    n = v.shape[axis]
    lo, hi = k // 2, k - 1 - k // 2
    cs = jnp.cumsum(v.astype(jnp.float32), axis=axis)
    pad = [(0, 0)] * v.ndim
    pad[axis] = (1, 0)
    cs = jnp.pad(cs, pad)
    t = jnp.arange(n)
    i_hi = jnp.minimum(t + hi + 1, n)
    i_lo = jnp.maximum(t - lo, 0)
    s = jnp.take(cs, i_hi, axis=axis) - jnp.take(cs, i_lo, axis=axis)
    shape = [1] * v.ndim
    shape[axis] = n
    cnt = (i_hi - i_lo).astype(jnp.float32).reshape(shape)
    return (s / cnt).astype(v.dtype)


def pool_mixer(v, pool_w, pool_scale, rows):
    b, l, _ = v.shape
    diffs = []
    for gi, k in enumerate(POOL_WINDOWS):
        vg = v[..., gi * POOL_GROUP:(gi + 1) * POOL_GROUP]
        if rows is None:
            m = box_mean(vg, k, 1)
        else:
            vg2 = vg.reshape(b, rows, GRID_W, POOL_GROUP)
            m = box_mean(box_mean(vg2, k, 1), k, 2).reshape(b, l, POOL_GROUP)
        diffs.append(m - vg)
    d = jnp.stack(diffs, axis=2)
    y = jnp.einsum('blgi,gio->blgo', d, pool_w).reshape(b, l, POOL_WIDTH)
    return y * pool_scale


def ssd_prep(xbc_raw, dt_raw, conv_w, conv_b, dt_bias, a_log):
    b, l, _ = xbc_raw.shape
    xbc = jax.nn.silu(centred_dwconv(xbc_raw, conv_w, conv_b))
    bc = N_BC_GROUPS * D_STATE
    xs = xbc[..., :D_INNER].reshape(b, l, N_HEADS, HEAD_DIM)
    Bm = xbc[..., D_INNER:D_INNER + bc].reshape(b, l, N_BC_GROUPS, D_STATE)
    Cm = xbc[..., D_INNER + bc:].reshape(b, l, N_BC_GROUPS, D_STATE)
    dt = jax.nn.softplus((dt_raw.reshape(b, l, N_DIR, N_HEADS) + dt_bias).astype(jnp.float32))
    A = -jnp.exp(a_log.astype(jnp.float32))
    return xs, Bm, Cm, dt, A


def ssd_scan(xs, dt, A, Bm, Cm, h0, with_output):
    b, l, h, p = xs.shape
    g, n = Bm.shape[2], Bm.shape[3]
    r = h // g
    nc = l // CHUNK
    a_cs = jnp.cumsum((dt * A).reshape(b, nc, CHUNK, g, r), axis=2)
    xdt = (xs * dt[..., None].astype(xs.dtype)).reshape(b, nc, CHUNK, g, r, p)
    Br = Bm.reshape(b, nc, CHUNK, g, n)
    a_last = a_cs[:, :, -1]
    to_end = jnp.exp(a_last[:, :, None] - a_cs).astype(xs.dtype)
    states = jnp.einsum('bcjgn,bcjgrp->bcgrpn', Br, xdt * to_end[..., None])
    chunk_decay = jnp.exp(a_last).astype(xs.dtype)
    if h0 is None:
        h0 = jnp.zeros((b, h, p, n), xs.dtype)

    def step(carry, inp):
        s, d = inp
        nxt = d[..., None, None] * carry + s
        return nxt, (carry if with_output else None)

    h_final, h_start = lax.scan(step, h0.reshape(b, g, r, p, n),
                                (jnp.moveaxis(states, 1, 0), jnp.moveaxis(chunk_decay, 1, 0)))
    h_final = h_final.reshape(b, h, p, n)
    if not with_output:
        return None, h_final
    h_start = jnp.moveaxis(h_start, 0, 1)
    Cr = Cm.reshape(b, nc, CHUNK, g, n)
    seg = a_cs[:, :, :, None] - a_cs[:, :, None]
    lower = jnp.tril(jnp.ones((CHUNK, CHUNK), dtype=bool))[:, :, None, None]
    L = jnp.exp(jnp.where(lower, seg, -jnp.inf)).astype(xs.dtype)
    cb = jnp.einsum('bcign,bcjgn->bcijg', Cr, Br)
    y_diag = jnp.einsum('bcijgr,bcjgrp->bcigrp', cb[..., None] * L, xdt)
    y_off = jnp.einsum('bcign,bcgrpn->bcigrp', Cr, h_start) * jnp.exp(a_cs).astype(xs.dtype)[..., None]
    return (y_diag + y_off).reshape(b, l, h, p), h_final


def bidir_ssd(xs, dt, A, Bm, Cm, h0_f, h0_b, with_output):
    fl = lambda t: jnp.flip(t, axis=1)
    y_f, st_f = ssd_scan(xs, dt[:, :, 0], A[0], Bm, Cm, h0_f, with_output)
    y_b, st_b = ssd_scan(fl(xs), fl(dt[:, :, 1]), A[1], fl(Bm), fl(Cm), h0_b, with_output)
    y = (y_f + fl(y_b)) if with_output else None
    return y, st_f, st_b


def gated_group_rmsnorm(y, z, w):
    b, l, d = y.shape
    u = (y * jax.nn.silu(z)).astype(jnp.float32).reshape(b, l, SSD_NORM_GROUPS, d // SSD_NORM_GROUPS)
    u = u * lax.rsqrt(jnp.mean(u * u, axis=-1, keepdims=True) + NORM_EPS)
    return (u.reshape(b, l, d) * w.astype(jnp.float32)).astype(y.dtype)


def mixer(h, w_in, b_merge, pool_w, pool_scale, conv_w, conv_b, dt_bias, a_log, d_skip, ssd_norm,
          w_proj_pool, w_proj_ssd, w_out, h0_f, h0_b, rows):
    b, l, _ = h.shape
    proj = h @ w_in
    v = proj[..., OFF_POOL_V:OFF_POOL_Z]
    z_pool = proj[..., OFF_POOL_Z:OFF_SSD_Z]
    z_ssd = proj[..., OFF_SSD_Z:OFF_GATE]
    gates = jax.nn.sigmoid(proj[..., OFF_GATE:OFF_XBC] + b_merge)
    xbc_raw = proj[..., OFF_XBC:OFF_DT]
    dt_raw = proj[..., OFF_DT:]
    y_pool = pool_mixer(v, pool_w, pool_scale, rows) * jax.nn.silu(z_pool)
    xs, Bm, Cm, dt, A = ssd_prep(xbc_raw, dt_raw, conv_w, conv_b, dt_bias, a_log)
    y_ssd, st_f, st_b = bidir_ssd(xs, dt, A, Bm, Cm, h0_f, h0_b, True)
    y_ssd = (y_ssd + d_skip[:, None] * xs).reshape(b, l, D_INNER)
    y_ssd = gated_group_rmsnorm(y_ssd, z_ssd, ssd_norm)
    merged = gates[..., :D_MODEL] * (y_pool @ w_proj_pool) + gates[..., D_MODEL:] * (y_ssd @ w_proj_ssd)
    return merged @ w_out, st_f, st_b


def context_states(hc, w_in, conv_w, conv_b, dt_bias, a_log):
    proj = hc @ w_in[:, OFF_XBC:]
    xs, Bm, Cm, dt, A = ssd_prep(proj[..., :CONV_DIM], proj[..., CONV_DIM:], conv_w, conv_b, dt_bias, a_log)
    _, st_f, st_b = bidir_ssd(xs, dt, A, Bm, Cm, None, None, False)
    return st_f, st_b


def _normal(key, shape, scale):
    return jax.random.normal(key, shape, jnp.float32) * scale


def setup_inputs(seed: int = 0) -> dict:
    key = jax.random.key(seed)
    ks = jax.random.split(key, 24)
    D = D_MODEL
    dt0 = jnp.exp(jax.random.uniform(ks[12], (DEPTH, N_DIR, N_HEADS), jnp.float32,
                                     minval=math.log(1e-3), maxval=math.log(1e-1)))
    return {
        'x': _normal(ks[0], (BATCH, SEQ, D), 1.0),
        'c': _normal(ks[1], (BATCH, D), 1.0),
        'ctx': _normal(ks[2], (BATCH, CTX_LEN, D), 1.0),
        'c_ctx': _normal(ks[3], (D,), 1.0),
        'w_ada': _normal(ks[4], (DEPTH, D, 3 * D), 0.5 * D ** -0.5),
        'b_ada': _normal(ks[5], (DEPTH, 3 * D), 0.02),
        'norm_pre': 1.0 + _normal(ks[6], (DEPTH, D), 0.05),
        'norm_post': 1.0 + _normal(ks[7], (DEPTH, D), 0.05),
        'w_in': _normal(ks[8], (DEPTH, D, IN_COLS), D ** -0.5),
        'b_merge': _normal(ks[9], (DEPTH, N_BRANCH * D), 0.02),
        'pool_w': _normal(ks[10], (DEPTH, N_POOL_GROUPS, POOL_GROUP, POOL_GROUP), POOL_GROUP ** -0.5),
        'pool_scale': 1.0 + _normal(ks[11], (DEPTH, POOL_WIDTH), 0.05),
        'conv_w': _normal(ks[13], (DEPTH, CONV_K, CONV_DIM), CONV_K ** -0.5),
        'conv_b': _normal(ks[14], (DEPTH, CONV_DIM), 0.02),
        'dt_bias': dt0 + jnp.log(-jnp.expm1(-dt0)),
        'a_log': jnp.log(jax.random.uniform(ks[15], (DEPTH, N_DIR, N_HEADS), jnp.float32, minval=1.0, maxval=16.0)),
        'd_skip': 1.0 + _normal(ks[16], (DEPTH, N_HEADS), 0.05),
        'ssd_norm': 1.0 + _normal(ks[17], (DEPTH, D_INNER), 0.05),
        'w_proj_pool': _normal(ks[18], (DEPTH, POOL_WIDTH, D), POOL_WIDTH ** -0.5),
        'w_proj_ssd': _normal(ks[19], (DEPTH, D_INNER, D), D_INNER ** -0.5),
        'w_out': _normal(ks[20], (DEPTH, D, D), D ** -0.5),
    }


def reference(x, c, ctx, c_ctx, w_ada, b_ada, norm_pre, norm_post, w_in, b_merge, pool_w, pool_scale,
              conv_w, conv_b, dt_bias, a_log, d_skip, ssd_norm, w_proj_pool, w_proj_ssd, w_out):
    rows = x.shape[1] // GRID_W
    for layer in range(DEPTH):
        shift, scale, gate = adaln(c[:, None, :], w_ada[layer], b_ada[layer])
        shift_c, scale_c, gate_c = adaln(c_ctx, w_ada[layer], b_ada[layer])
        hc = rmsnorm(ctx, norm_pre[layer]) * (1.0 + scale_c) + shift_c
        if layer + 1 < DEPTH:
            out_c, st_f, st_b = mixer(hc, w_in[layer], b_merge[layer], pool_w[layer], pool_scale[layer],
                                      conv_w[layer], conv_b[layer], dt_bias[layer], a_log[layer],
                                      d_skip[layer], ssd_norm[layer], w_proj_pool[layer],
                                      w_proj_ssd[layer], w_out[layer], None, None, None)
            ctx_next = ctx + gate_c * rmsnorm(out_c, norm_post[layer])
        else:
            st_f, st_b = context_states(hc, w_in[layer], conv_w[layer], conv_b[layer],
                                        dt_bias[layer], a_log[layer])
            ctx_next = ctx
        hx = rmsnorm(x, norm_pre[layer]) * (1.0 + scale) + shift
        out_x, _, _ = mixer(hx, w_in[layer], b_merge[layer], pool_w[layer], pool_scale[layer],
                            conv_w[layer], conv_b[layer], dt_bias[layer], a_log[layer],
                            d_skip[layer], ssd_norm[layer], w_proj_pool[layer],
                            w_proj_ssd[layer], w_out[layer], st_f, st_b, rows)
        x = x + gate * rmsnorm(out_x, norm_post[layer])
        ctx = ctx_next
    return x
```

```python
import contextlib
import numpy as np
import concourse.bass as bass
import concourse.mybir as mybir
from concourse.bass_utils import run_bass_kernel_spmd

F32 = mybir.dt.float32
BF16 = mybir.dt.bfloat16
AF = mybir.ActivationFunctionType
ALU = mybir.AluOpType

COMPUTE = ("pe", "act", "dve", "pool")
DMA_RING = 8
EPS = 1e-6


class Buf:
    __slots__ = ("name", "w", "r")

    def __init__(self, name):
        self.name = name
        self.w = None
        self.r = {}


class Prog:
    def __init__(self):
        self.streams = {e: [] for e in ("pe", "act", "dve", "pool", "sp")}
        self.count = {e: 0 for e in COMPUTE}
        self.dma_n = {e: 0 for e in self.streams}
        self.known = {e: {} for e in self.streams}

    def _need(self, stream, waits, tok):
        if tok is None:
            return
        key, val = tok
        if key == stream and stream == "pe":
            return
        if self.known[stream].get(key, 0) >= val:
            return
        if waits.get(key, 0) < val:
            waits[key] = val

    def op(self, stream, fn, reads=(), writes=(), dma=False):
        waits = {}
        for b in reads:
            self._need(stream, waits, b.w)
        for b in writes:
            self._need(stream, waits, b.w)
            for tok in b.r.values():
                self._need(stream, waits, tok)
        if dma:
            m = self.dma_n[stream]
            self.dma_n[stream] = m + 1
            key = ("dma", stream, m % DMA_RING)
            val = 16 * (m // DMA_RING + 1)
            if m >= DMA_RING:
                self._need(stream, waits, (key, val - 16))
            tok = (key, val)
            inc = 16
        else:
            self.count[stream] += 1
            tok = (stream, self.count[stream])
            inc = 1
        for k, v in waits.items():
            self.known[stream][k] = v
        self.streams[stream].append((list(waits.items()), fn, tok[0], inc))
        for b in writes:
            b.w = tok
            b.r = {}
        for b in reads:
            if b not in writes:
                b.r[tok[0]] = tok
        return tok

    def final_wait_all(self, stream="sp"):
        waits = {}
        for e in COMPUTE:
            if self.count[e]:
                waits[e] = self.count[e]
        for s, n in self.dma_n.items():
            for m in range(max(0, n - DMA_RING), n):
                key = ("dma", s, m % DMA_RING)
                val = 16 * (m // DMA_RING + 1)
                if waits.get(key, 0) < val:
                    waits[key] = val
        self.streams[stream].append((list(waits.items()), None, None, 0))

    def emit(self, nc):
        keys = set()
        for s, ops in self.streams.items():
            for waits, fn, key, inc in ops:
                if key is not None:
                    keys.add(key)
                for k, _ in waits:
                    keys.add(k)
        keys = sorted(keys, key=str)
        with contextlib.ExitStack() as st:
            sems = {}
            for i, k in enumerate(keys):
                sems[k] = st.enter_context(nc.semaphore(f"s{i}"))
            block = st.enter_context(nc.Block())

            def runner(stream):
                def body(eng):
                    for waits, fn, key, inc in self.streams[stream]:
                        for k, v in waits:
                            eng.wait_ge(sems[k], v)
                        if fn is not None:
                            fn(eng).then_inc(sems[key], inc)
                return body

            block.tensor(runner("pe"))
            block.scalar(runner("act"))
            block.vector(runner("dve"))
            block.gpsimd(runner("pool"))
            block.sync(runner("sp"))


D = 1024
KC = 8
GW = 64
NCOL = 9280
NH = 32
LC = 256
BLK_V = (0, 1)
BLK_ZP = (2, 3)
BLK_ZS = (4, 5, 6, 7)
BLK_G = (8, 9, 10, 11)
BLK_XS = (12, 13, 14, 15)
BLK_B = 16
BLK_C = 17
BLK_DT = 18
POOL_WINDOWS = (2, 4, 8, 16)
C_NPRE, C_BADA, C_BMERGE, C_PSCALE, C_CONVB, C_CONVW, C_SSDN = 0, 8, 32, 48, 56, 80, 176
R_NPOST, R_DTB, R_ALOG, R_DSKIP, R_TOT = 0, 1024, 1088, 1152, 1184


def build_nc(NB, L, debug=False, stop=None):
    NT = L // 512
    NCH = L // 128
    NCOND = NB + 1
    nc = bass.Bass("TRN2", target_bir_lowering=False)
    din = lambda n, s: nc.dram_tensor(n, s, F32, kind="ExternalInput").ap()
    x_d = din("x", [NB, L, D])
    ctx_d = din("ctx", [NB, LC, D])
    c3_d = din("c3T", [128, KC * NCOND])
    wada_d = din("w_ada", [D, 3 * D])
    win_d = din("w_in", [D, NCOL])
    poolw_d = din("pool_w", [D, 256])
    wpp_d = din("w_pp", [D, D])
    wps_d = din("w_ps", [2 * D, D])
    wout_d = din("w_out", [D, D])
    cols_d = din("cols", [128, 192])
    rows_d = din("rows", [1, R_TOT])
    consts_d = din("consts", [128, 768])
    ind_d = din("ind", [128, 1024])
    negm_d = din("negm", [128, 1024])
    rinv_d = din("rinv", [128, 256])
    out_d = nc.dram_tensor("out", [NB, L, D], F32, kind="ExternalOutput").ap()
    dbg_d = nc.dram_tensor("dbg", [128, 4096], F32, kind="ExternalOutput").ap() if debug else None

    def scratch(n, s, dt=BF16):
        return nc.dram_tensor(n, s, dt, kind="Internal").ap()

    win_s = scratch("win_s", [19, 128, 4096])
    wada_s = scratch("wada_s", [6, 128, 4096])
    poolw_s = scratch("poolw_s", [1, 128, 4096])
    wpp_s = scratch("wpp_s", [2, 128, 4096])
    wps_s = scratch("wps_s", [4, 128, 4096])
    wout_s = scratch("wout_s", [2, 128, 4096])
    v_s = scratch("v_s", [8, 128, L])
    d_s = scratch("d_s", [8, 128, L])
    hb_s = scratch("hb_s", [NCH * 4, 128, 512])

    P = Prog()
    with contextlib.ExitStack() as st:
        def sb(name, shape, dt=F32):
            return st.enter_context(nc.sbuf_tensor("sb_" + name, shape, dt))

        def ps(name, shape, dt=F32):
            return st.enter_context(nc.psum_tensor("ps_" + name, shape, dt))

        cf = sb("cf", [128, 768]); cf_b = Buf("cf")
        identf = cf[:, 0:128]; onesf = cf[:, 640:768]
        Uf32 = {0: cf[:, 128:256], 1: cf[:, 256:384]}
        SUf32 = {0: cf[:, 384:512], 1: cf[:, 512:640]}
        cb16 = sb("cb16", [128, 128], BF16); cb16_b = Buf("cb16")
        identb = cb16[:, 0:128]
        indb = sb("indb", [128, 1024], BF16); indb_b = Buf("indb")
        nmb = sb("nmb", [128, 1024], BF16); nmb_b = Buf("nmb")
        rinv = sb("rinv", [128, 256]); rinv_b = Buf("rinv")
        cols = sb("cols", [128, 192]); cols_b = Buf("cols")
        rowsb = sb("rowsb", [128, 160]); rowsb_b = Buf("rowsb")
        dtb_bc = rowsb[:, 0:64]; A_bc = rowsb[:, 64:128]; dsk_bc = rowsb[:, 128:160]
        sc3 = sb("sc3", [128, KC * NCOND], BF16); sc3_b = Buf("sc3")
        modT = sb("modT", [128, 24 * NCOND]); modT_b = Buf("modT")
        Acol = sb("Acol", [128, KC * NCOND]); Acol_b = Buf("Acol")
        gn_bc = sb("gn_bc", [128, D]); gn_b = Buf("gn")
        tiny = sb("tiny", [128, 64]); tiny_b = Buf("tiny")

        wring = [sb(f"wr{i}", [128, KC, 512], BF16) for i in range(3)]
        wring_b = [Buf(f"wr{i}") for i in range(3)]
        wr_n = [0]

        xt = [sb(f"xt{i}", [128, D]) for i in range(3)]; xt_b = [Buf(f"xt{i}") for i in range(3)]
        xn = [sb(f"xn{i}", [128, D], BF16) for i in range(2)]; xn_b = [Buf(f"xn{i}") for i in range(2)]
        hT = sb("hT", [128, KC, 512], BF16); hT_b = Buf("hT")
        hTh = sb("hTh", [128, KC, 4], BF16); hTh_b = Buf("hTh")
        ssq = sb("ssq", [128, 8]); ssq_b = Buf("ssq")
        rstd = sb("rstd", [128, 8]); rstd_b = Buf("rstd")

        rawT4 = sb("rawT4", [128, 4, 516], BF16); rawT4_b = Buf("rawT4")
        cacc = [sb("cacc0", [128, 512])] * 2; cacc_b = [Buf("cacc0")] * 2
        xbcT = [sb(f"xbcT{i}", [128, 512], BF16) for i in range(2)]; xbcT_b = [Buf(f"xbcT{i}") for i in range(2)]
        BT = sb("BT", [128, 4, 512], BF16); BT_b = Buf("BT")
        CT = sb("CT", [128, 4, 512], BF16); CT_b = Buf("CT")
        Btok = sb("Btok", [128, 4, 512], BF16); Btok_b = Buf("Btok")
        xs_g = [sb(f"xsg{i}", [128, 4, 512], BF16) for i in range(2)]; xs_g_b = [Buf(f"xsg{i}") for i in range(2)]
        zs_g = sb("zsg", [128, 4, 512], BF16); zs_g_b = Buf("zsg")
        dtc = sb("dtc", [128, 4, 64]); dtc_b = Buf("dtc")
        dtx = sb("dtx", [128, 64]); dtx_b = Buf("dtx")
        dtA = sb("dtA", [128, 8, 32]); dtA_b = Buf("dtA")
        eall = sb("eall", [128, 8, 96]); eall_b = Buf("eall")
        dts = sb("dts", [128, 8, 32]); dts_b = Buf("dts")

        dtA2 = [sb(f"dtA2{i}", [128, 64]) for i in range(2)]; dtA2_b = [Buf(f"dtA2{i}") for i in range(2)]
        posA = [sb(f"posA{i}", [128, 128], BF16) for i in range(2)]; posA_b = [Buf(f"posA{i}") for i in range(2)]
        negA = [sb(f"negA{i}", [128, 128], BF16) for i in range(2)]; negA_b = [Buf(f"negA{i}") for i in range(2)]
        LT = [sb(f"LT{i}", [128, 8, 128], BF16) for i in range(2)]; LT_b = [Buf(f"LT{i}") for i in range(2)]
        MT = [sb(f"MT{i}", [128, 8, 128], BF16) for i in range(2)]; MT_b = [Buf(f"MT{i}") for i in range(2)]
        cbm = [sb(f"cbm{i}", [128, 1, 128], BF16) for i in range(2)]; cbm_b = [Buf(f"cbm{i}") for i in range(2)]
        xdt = [sb(f"xdt{i}", [128, 512], BF16) for i in range(2)]; xdt_b = [Buf(f"xdt{i}") for i in range(2)]
        xdts = [sb(f"xdts{i}", [128, 512], BF16) for i in range(2)]; xdts_b = [Buf(f"xdts{i}") for i in range(2)]
        xsd = [sb(f"xsd{i}", [128, 512], BF16) for i in range(2)]; xsd_b = [Buf(f"xsd{i}") for i in range(2)]
        ysb = sb("ysb", [128, 2, 512]); ysb_b = Buf("ysb")
        ub = sb("ub", [128, 512]); ub_b = Buf("ub")
        un = sb("un", [128, 512], BF16); un_b = Buf("un")
        S = {0: [sb(f"Sf{g}", [128, 512]) for g in range(4)], 1: [sb(f"Sb{g}", [128, 512]) for g in range(4)]}
        S_b = {0: [Buf(f"Sf{g}") for g in range(4)], 1: [Buf(f"Sb{g}") for g in range(4)]}
        Sf16 = [sb(f"Sf16{g}", [128, 512], BF16) for g in range(4)]; Sf16_b = [Buf(f"Sf16{g}") for g in range(4)]
        Hb16 = [sb(f"Hb16{i}", [128, 512], BF16) for i in range(2)]; Hb16_b = [Buf(f"Hb16{i}") for i in range(2)]
        osb = sb("osb", [128, D]); osb_b = Buf("osb")

        pgf = sb("pgf", [128, 6 * 2048]); PG = [Buf(f"pg{i}") for i in range(7)]
        mTt = sb("mTt", [128, 8, 512], BF16)

        def pg16(page, npages=1):
            return pgf[:, page * 2048:(page + npages) * 2048].bitcast(BF16)

        uT = pg16(0, 2).rearrange("p (k t) -> p k t", k=16)
        gT = pg16(2).rearrange("p (k t) -> p k t", k=8)
        zpT = pg16(3).rearrange("p (k t) -> p k t", k=8)
        p1T = zpT
        dTt = pg16(4).rearrange("p (k t) -> p k t", k=8)
        ypT = pg16(5).rearrange("p (k t) -> p k t", k=8)
        mT = mTt

        PA = [ps(f"PA{i}", [128, 512]) for i in range(2)]; PA_b = [Buf(f"PA{i}") for i in range(2)]
        PT = ps("PT", [128, 8, 128], BF16); PT_b = Buf("PT")
        PL = ps("PL", [128, 1024]); PL_b = Buf("PL")
        PY = ps("PY", [128, 512]); PY_b = Buf("PY")
        PZ = ps("PZ", [128, 512]); PZ_b = Buf("PZ")
        PS = ps("PS", [128, 512]); PS_b = Buf("PS")
        pa_n = [0]

        def next_pa():
            i = pa_n[0] % 2
            pa_n[0] += 1
            return PA[i], PA_b[i]

        rr = [0]

        def evac_eng():
            rr[0] += 1
            return "act" if rr[0] % 2 else "dve"

        def dma(out, in_, reads=(), writes=(), stream="sp", **kw):
            return P.op(stream, lambda e: e.dma_start(out=out, in_=in_, **kw), reads=reads, writes=writes, dma=True)

        def act(out, in_, func, reads, writes, bias=0.0, scale=1.0, accum_out=None):
            if accum_out is None:
                return P.op("act", lambda e: e.activation(out=out, in_=in_, func=func, bias=bias, scale=scale), reads, writes)
            return P.op("act", lambda e: e.activation(out=out, in_=in_, func=func, bias=bias, scale=scale, accum_out=accum_out), reads, writes)

        def tt(eng, out, in0, in1, op, reads, writes):
            return P.op(eng, lambda e: e.tensor_tensor(out=out, in0=in0, in1=in1, op=op), reads, writes)

        def ts(eng, out, in0, s1, s2, op0, op1, reads, writes):
            return P.op(eng, lambda e: e.tensor_scalar(out=out, in0=in0, scalar1=s1, scalar2=s2, op0=op0, op1=op1), reads, writes)

        def stt(out, in0, scalar, in1, op0, op1, reads, writes):
            return P.op("dve", lambda e: e.scalar_tensor_tensor(out=out, in0=in0, scalar=scalar, in1=in1, op0=op0, op1=op1), reads, writes)

        def cp(eng, out, in_, reads, writes):
            if eng == "act":
                return act(out, in_, AF.Copy, reads, writes)
            return P.op(eng, lambda e: e.tensor_copy(out=out, in_=in_), reads, writes)

        def mm(out, lhsT, rhs, start, stop, reads, writes):
            return P.op("pe", lambda e: e.matmul(out, lhsT=lhsT, rhs=rhs, start=start, stop=stop), reads, writes)

        def tr(out, in_, ident, reads, writes):
            return P.op("pe", lambda e: e.transpose(out=out, in_=in_, identity=ident), reads, writes)

        def memset(eng, ap, val, writes):
            return P.op(eng, lambda e: e.memset(ap, val), (), writes)

        def bc_h(ap8, n=8, q=64):
            return ap8.unsqueeze(2).to_broadcast([128, n, q])

        def load_w(scr, blk):
            i = wr_n[0] % 3
            wr_n[0] += 1
            dma(wring[i][:].rearrange("p k c -> p (k c)"), scr[blk], writes=[wring_b[i]])
            return wring[i], wring_b[i]

        dma(cf[:], consts_d, writes=[cf_b])
        dma(cols[:], cols_d, writes=[cols_b])
        dma(rinv[:], rinv_d, writes=[rinv_b])
        dma(rowsb[:], rows_d[:, R_DTB:R_TOT].partition_broadcast(128), writes=[rowsb_b])
        cp("dve", cb16[:], cf[:, 0:128], [cf_b], [cb16_b])
        act(A_bc, A_bc, AF.Exp, [rowsb_b], [rowsb_b])
        ts("dve", A_bc, A_bc, -1.0, None, ALU.mult, ALU.bypass, [rowsb_b], [rowsb_b])
        dma(pgf[:, 0:1024], ind_d, writes=[PG[0]])
        cp("dve", indb[:], pgf[:, 0:1024], [PG[0]], [indb_b])
        dma(pgf[:, 2048:3072], negm_d, writes=[PG[1]])
        cp("dve", nmb[:], pgf[:, 2048:3072], [PG[1]], [nmb_b])
        for i in range(2):
            memset("pool", dtA2[i][:], 0.0, [dtA2_b[i]])
            memset("pool", posA[i][:], 0.0, [posA_b[i]])
            memset("pool", negA[i][:], 0.0, [negA_b[i]])
        dma(tiny[:, 0:KC * NCOND], c3_d, writes=[tiny_b])
        act(sc3[:], tiny[:, 0:KC * NCOND], AF.Silu, [tiny_b], [sc3_b])

        cv_n = [0]

        def convert(src, K, N, scr, scale_col0=None):
            nkh = K // 1024
            ncb = (N + 511) // 512
            for kh in range(nkh):
                for cb_ in range(ncb):
                    w = min(512, N - cb_ * 512)
                    blk = kh * ncb + cb_
                    for half in range(2):
                        i = cv_n[0] % 2
                        cv_n[0] += 1
                        stg = pgf[:, i * 2048:(i + 1) * 2048].rearrange("p (k c) -> p k c", k=4)
                        cst = pg16(2 + i)[:, 0:2048].rearrange("p (k c) -> p k c", k=4)
                        r0 = kh * 1024 + half * 512
                        dma(stg[:, :, 0:w], src[r0:r0 + 512, cb_ * 512:cb_ * 512 + w].rearrange("(k p) c -> p k c", p=128),
                            writes=[PG[i]])
                        eng = ("act", "dve", "pool")[cv_n[0] % 3]
                        if scale_col0 is None:
                            cp(eng, cst[:, :, 0:w], stg[:, :, 0:w], [PG[i]], [PG[2 + i]])
                        else:
                            for k in range(4):
                                c0 = scale_col0 + kh * 8 + half * 4 + k
                                ts("dve", cst[:, k, 0:w], stg[:, k, 0:w], cols[:, c0:c0 + 1], None, ALU.mult, ALU.bypass,
                                   [PG[i], cols_b], [PG[2 + i]])
                        dst = scr[blk].rearrange("p (k c) -> p k c", k=8)[:, half * 4:half * 4 + 4, 0:w]
                        dma(dst, cst[:, :, 0:w], reads=[PG[2 + i]], writes=[SCR_B[id(scr)]])

        SCR_B = {}
        for s_ in (win_s, wada_s, poolw_s, wpp_s, wps_s, wout_s):
            SCR_B[id(s_)] = Buf("scr")
        convert(wada_d, D, 3 * D, wada_s)
        convert(win_d, D, NCOL, win_s)
        convert(poolw_d, D, 256, poolw_s)
        convert(wpp_d, D, D, wpp_s)
        convert(wps_d, 2 * D, D, wps_s, scale_col0=C_SSDN)
        convert(wout_d, D, D, wout_s)
        W_B = lambda scr: SCR_B[id(scr)]

        def load_wb(scr, blk, ncols=512):
            i = wr_n[0] % 3
            wr_n[0] += 1
            if ncols == 512:
                dma(wring[i][:].rearrange("p k c -> p (k c)"), scr[blk], reads=[W_B(scr)], writes=[wring_b[i]])
            else:
                dma(wring[i][:, :, 0:ncols], scr[blk].rearrange("p (k c) -> p k c", k=8)[:, :, 0:ncols], reads=[W_B(scr)],
                    writes=[wring_b[i]])
            return wring[i], wring_b[i]

        for blk in range(6):
            w, wb = load_wb(wada_s, blk)
            for cc in range(4):
                j = blk * 4 + cc
                for kc in range(KC):
                    mm(PS[:, j * 4:j * 4 + NCOND], w[:, kc, cc * 128:(cc + 1) * 128], sc3[:, kc * NCOND:(kc + 1) * NCOND],
                       kc == 0, kc == KC - 1, [wb, sc3_b], [PS_b])
        tt("dve", modT[:].rearrange("p (j i) -> p j i", i=NCOND), PS[:, 0:96].rearrange("p (j i) -> p j i", i=4)[:, :, 0:NCOND],
           cols[:, C_BADA:C_BADA + 24].unsqueeze(2).to_broadcast([128, 24, NCOND]), ALU.add, [PS_b, cols_b], [modT_b])
        mod3 = modT[:].rearrange("p (j i) -> p j i", i=NCOND)
        ts("dve", Acol[:].rearrange("p (k i) -> p k i", i=NCOND), mod3[:, 8:16, :], 1.0, None, ALU.add, ALU.bypass, [modT_b], [Acol_b])
        tt("dve", Acol[:].rearrange("p (k i) -> p k i", i=NCOND), Acol[:].rearrange("p (k i) -> p k i", i=NCOND),
           cols[:, C_NPRE:C_NPRE + 8].unsqueeze(2).to_broadcast([128, 8, NCOND]), ALU.mult, [Acol_b, cols_b], [Acol_b])
        Acol3 = Acol[:].rearrange("p (k i) -> p k i", i=NCOND)

        def make_gn(b):
            dma(osb[:], rows_d[:, R_NPOST:R_NPOST + D].partition_broadcast(128), writes=[osb_b])
            for half in range(2):
                pa, pab = next_pa()
                for k4 in range(4):
                    kc = half * 4 + k4
                    dg = xt[0][:, k4 * 128:(k4 + 1) * 128]
                    ts("dve", dg, identf, mod3[:, 16 + kc, b:b + 1], None, ALU.mult, ALU.bypass, [cf_b, modT_b], [xt_b[0]])
                    mm(pa[:, k4 * 128:(k4 + 1) * 128], onesf, dg, True, True, [cf_b, xt_b[0]], [pab])
                tt("dve", gn_bc[:, half * 512:(half + 1) * 512], pa[:], osb[:, half * 512:(half + 1) * 512], ALU.mult,
                   [pab, osb_b], [gn_b])

        def front(src, t0, ntok, seqlen, ci):
            nsub = ntok // 128
            xts = []
            for s in range(nsub + 1):
                i = s % 3
                if s < nsub:
                    dma(xt[i][:], src[t0 + s * 128:t0 + (s + 1) * 128, :], writes=[xt_b[i]])
                    npart = 128
                else:
                    lo = max(t0 - 2, 0)
                    hi = min(t0 + ntok, seqlen - 1)
                    dma(xt[i][0:2, :], src[lo:lo + 2, :], writes=[xt_b[i]])
                    dma(xt[i][2:3, :], src[hi:hi + 1, :], writes=[xt_b[i]])
                    dma(xt[i][3:4, :], src[hi:hi + 1, :], writes=[xt_b[i]])
                    npart = 4
                j = s % 2
                act(xn[j][0:npart, :], xt[i][0:npart, :], AF.Square, [xt_b[i]], [xn_b[j], ssq_b], accum_out=ssq[0:npart, s:s + 1])
                act(rstd[0:npart, s:s + 1], ssq[0:npart, s:s + 1], AF.Ln, [ssq_b], [rstd_b], bias=EPS, scale=1.0 / D)
                act(rstd[0:npart, s:s + 1], rstd[0:npart, s:s + 1], AF.Exp, [rstd_b], [rstd_b], scale=-0.5)
                ts("pool", xn[j][0:npart, :], xt[i][0:npart, :], rstd[0:npart, s:s + 1], 1.0, ALU.mult, ALU.mult,
                   [xt_b[i], rstd_b], [xn_b[j]])
                for kc in range(KC):
                    tr(PT[:, kc, 0:npart], xn[j][0:npart, kc * 128:(kc + 1) * 128], identb[0:npart, 0:npart], [xn_b[j], cb16_b], [PT_b])
                for kc in range(KC):
                    if s < nsub:
                        o = hT[:, kc, s * 128:(s + 1) * 128]; ob = hT_b
                    else:
                        o = hTh[:, kc, 0:4]; ob = hTh_b
                    a_ap = Acol3[:, kc, ci:ci + 1]
                    s_ap = mod3[:, kc, ci:ci + 1]
                    if evac_eng() == "act":
                        act(o, PT[:, kc, 0:npart], AF.Identity, [PT_b, Acol_b, modT_b], [ob], bias=s_ap, scale=a_ap)
                    else:
                        ts("dve", o, PT[:, kc, 0:npart], a_ap, s_ap, ALU.mult, ALU.add, [PT_b, Acol_b, modT_b], [ob])
            if t0 == 0:
                memset("pool", hTh[:, :, 0:2], 0.0, [hTh_b])
            if t0 + ntok >= seqlen:
                memset("pool", hTh[:, :, 2:4], 0.0, [hTh_b])

        def proj_feat(w, wb, cc, ntok, halo_idx=None):
            pa, pab = next_pa()
            for kc in range(KC):
                mm(pa[:, 0:ntok], w[:, kc, cc * 128:(cc + 1) * 128], hT[:, kc, 0:ntok], kc == 0, kc == KC - 1, [wb, hT_b], [pab])
            if halo_idx is not None:
                for kc in range(KC):
                    mm(PS[:, halo_idx * 4:halo_idx * 4 + 4], w[:, kc, cc * 128:(cc + 1) * 128], hTh[:, kc, 0:4], kc == 0, kc == KC - 1,
                       [wb, hTh_b], [PS_b])
            return pa, pab

        cn = [0]

        def xbc_block(blk, ntok, sink):
            w, wb = load_wb(win_s, blk)
            pas = []
            for cc in range(4):
                pa, pab = proj_feat(w, wb, cc, ntok, halo_idx=cc)
                cp(evac_eng(), rawT4[:, cc, 2:2 + ntok], pa[:, 0:ntok], [pab], [rawT4_b])
            PSh = PS[:, 0:16].rearrange("p (c f) -> p c f", f=4)
            cp("dve", rawT4[:, :, 0:2], PSh[:, :, 0:2], [PS_b], [rawT4_b])
            cp("dve", rawT4[:, :, 2 + ntok:3 + ntok], PSh[:, :, 2:3], [PS_b], [rawT4_b])
            for cc in range(4):
                gcc = (blk - 12) * 4 + cc
                i = cn[0] % 2
                cn[0] += 1
                wcol = lambda k: cols[:, C_CONVW + gcc * 4 + k:C_CONVW + gcc * 4 + k + 1]
                ts("dve", cacc[i][:, 0:ntok], rawT4[:, cc, 0:ntok], wcol(0), None, ALU.mult, ALU.bypass, [rawT4_b, cols_b], [cacc_b[i]])
                for k in range(1, 4):
                    stt(cacc[i][:, 0:ntok], rawT4[:, cc, k:k + ntok], wcol(k), cacc[i][:, 0:ntok], ALU.mult, ALU.add,
                        [rawT4_b, cols_b, cacc_b[i]], [cacc_b[i]])
                sink(cc, cacc[i], cacc_b[i], cols[:, C_CONVB + gcc:C_CONVB + gcc + 1])

        def dt_block(ntok):
            w, wb = load_wb(win_s, BLK_DT, 64)
            for c in range(ntok // 128):
                pa, pab = next_pa()
                for kc in range(KC):
                    mm(pa[:, 0:64], hT[:, kc, c * 128:(c + 1) * 128], w[:, kc, 0:64], kc == 0, kc == KC - 1, [wb, hT_b], [pab])
                tt("dve", dtx[:], pa[:, 0:64], dtb_bc, ALU.add, [pab, rowsb_b], [dtx_b])
                act(dtx[:], dtx[:], AF.Exp, [dtx_b], [dtx_b])
                act(dtc[:, c, :], dtx[:], AF.Ln, [dtx_b], [dtc_b], bias=1.0)

        def cd_prep(c, d):
            cd = c * 2 + d
            tt("dve", dtA[:, cd, :], dtc[:, c, d * 32:(d + 1) * 32], A_bc[:, d * 32:(d + 1) * 32], ALU.mult, [dtc_b, rowsb_b], [dtA_b])
            mm(PS[:, 128:160], Uf32[d], dtA[:, cd, :], True, True, [cf_b, dtA_b], [PS_b])
            mm(PS[:, 160:192], onesf, dtA[:, cd, :], True, True, [cf_b, dtA_b], [PS_b])
            mm(PS[:, 192:224], SUf32[d], dtA[:, cd, :], True, True, [cf_b, dtA_b], [PS_b])
            act(eall[:, cd, :], PS[:, 128:224], AF.Exp, [PS_b], [eall_b])
            tt("dve", dts[:, cd, :], dtc[:, c, d * 32:(d + 1) * 32], eall[:, cd, 64:96], ALU.mult, [dtc_b, eall_b], [dts_b])

        un_ = [0]
        ch_n = [0]

        def state_unit(g, c, d, xs, xsb, store_ci=None):
            cd = c * 2 + d
            i = un_[0] % 2
            un_[0] += 1
            tt("pool", xdts[i][:].rearrange("p (h q) -> p h q", h=8), xs[:, c, :].rearrange("p (h q) -> p h q", h=8),
               bc_h(dts[:, cd, g * 8:(g + 1) * 8]), ALU.mult, [xsb, dts_b], [xdts_b[i]])
            pa, pab = next_pa()
            mm(pa[:], Btok[:, c, g * 128:(g + 1) * 128], xdts[i][:], True, True, [Btok_b, xdts_b[i]], [pab])
            Sg, Sgb = S[d][g], S_b[d][g]
            if store_ci is not None:
                j = un_[0] % 2
                cp("act", Hb16[j][:], Sg[:], [Sgb], [Hb16_b[j]])
                dma(hb_s[store_ci * 4 + g], Hb16[j][:], reads=[Hb16_b[j]], writes=[HB_B[store_ci * 4 + g]])
            tt("pool", Sg[:].rearrange("p (h q) -> p h q", h=8), Sg[:].rearrange("p (h q) -> p h q", h=8),
               bc_h(eall[:, cd, 32 + g * 8:32 + (g + 1) * 8]), ALU.mult, [Sgb, eall_b], [Sgb])
            tt("dve", Sg[:], Sg[:], pa[:], ALU.add, [Sgb, pab], [Sgb])

        HB_B = [Buf(f"hb{i}") for i in range(NCH * 4)]
        V_B = [Buf(f"v{i}") for i in range(8)]
        D_B = [Buf(f"d{i}") for i in range(8)]

        def xs_sink_factory(slot, ntok):
            xsl, xslb = xs_g[slot], xs_g_b[slot]

            def sink(cc, acc, accb, bias):
                j = cn[0] % 2
                act(xbcT[j][:, 0:ntok], acc[:, 0:ntok], AF.Silu, [accb, cols_b], [xbcT_b[j]], bias=bias)
                for c in range(ntok // 128):
                    tr(PT[:, c, :], xbcT[j][:, c * 128:(c + 1) * 128], identb, [xbcT_b[j], cb16_b], [PT_b])
                nchk = ntok // 128
                cp(evac_eng(), xsl[:, 0:nchk, cc * 128:(cc + 1) * 128], PT[:, 0:nchk, :], [PT_b], [xslb])
            return sink

        def b_sink_factory(ntok):
            def sink(cc, acc, accb, bias):
                act(BT[:, cc, 0:ntok], acc[:, 0:ntok], AF.Silu, [accb, cols_b], [BT_b], bias=bias)
                for c in range(ntok // 128):
                    tr(PT[:, c, :], BT[:, cc, c * 128:(c + 1) * 128], identb, [BT_b, cb16_b], [PT_b])
                nchk = ntok // 128
                cp(evac_eng(), Btok[:, 0:nchk, cc * 128:(cc + 1) * 128], PT[:, 0:nchk, :], [PT_b], [Btok_b])
            return sink

        def c_sink_factory(ntok):
            def sink(cc, acc, accb, bias):
                act(CT[:, cc, 0:ntok], acc[:, 0:ntok], AF.Silu, [accb, cols_b], [CT_b], bias=bias)
            return sink

        def dbg_dump(ap, ncols, rows=128, col0=0):
            if debug:
                cp("dve", osb[0:rows, 0:ncols], ap, [], [osb_b])
                dma(dbg_d[0:rows, col0:col0 + ncols], osb[0:rows, 0:ncols], reads=[osb_b], writes=[DBG_B])
        DBG_B = Buf("dbg")

        def context(b):
            for d in range(2):
                for g in range(4):
                    memset("pool", S[d][g][:], 0.0, [S_b[d][g]])
            front(ctx_d[b], 0, LC, LC, NB)
            xbc_block(BLK_B, LC, b_sink_factory(LC))
            dt_block(LC)
            for c in range(2):
                for d in range(2):
                    cd_prep(c, d)
            for g in range(4):
                slot = g % 2
                xbc_block(BLK_XS[g], LC, xs_sink_factory(slot, LC))
                for c in (0, 1):
                    state_unit(g, c, 0, xs_g[slot], xs_g_b[slot])
                for c in (1, 0):
                    state_unit(g, c, 1, xs_g[slot], xs_g_b[slot])

        def sweep1(b):
            for t in range(NT - 1, -1, -1):
                t0 = t * 512
                front(x_d[b], t0, 512, L, b)
                for bi, blk in enumerate(BLK_V):
                    w, wb = load_wb(win_s, blk)
                    for cc in range(4):
                        pa, pab = proj_feat(w, wb, cc, 512)
                        cp(evac_eng(), dTt[:, bi * 4 + cc, :], pa[:], [pab], [PG[4]])
                for j in range(8):
                    dma(v_s[j][:, t0:t0 + 512], dTt[:, j, :], reads=[PG[4]], writes=[V_B[j]])
                xbc_block(BLK_B, 512, b_sink_factory(512))
                dt_block(512)
                for c in range(4):
                    cd_prep(c, 1)
                for g in range(4):
                    slot = g % 2
                    xbc_block(BLK_XS[g], 512, xs_sink_factory(slot, 512))
                    for c in (3, 2, 1, 0):
                        state_unit(g, c, 1, xs_g[slot], xs_g_b[slot], store_ci=t * 4 + c)

        def pool_phase(b):
            R = L // GW
            PADN = 8
            vin = pg16(0)[:, 0:L].rearrange("p (r c) -> p r c", c=GW)
            bufs = [(pgf[:, 2048:2048 + 5120], [PG[1], PG[2], PG[3]]), (pgf[:, 2048 + 5120:2048 + 10240], [PG[3], PG[4], PG[5]])]

            def rview(i):
                return bufs[i][0][:, 0:(R + 2 * PADN) * GW].rearrange("p (r c) -> p r c", c=GW)

            def cview(i):
                return bufs[i][0][:, 0:R * (GW + 2 * PADN)].rearrange("p (r c) -> p r c", c=GW + 2 * PADN)

            eng = "dve"

            def step(axis, n, a, bsh, src, srcb, dst, dstb):
                def sl(ap, lo, hi):
                    return ap[:, lo:hi, :] if axis == 0 else ap[:, :, lo:hi]
                tt(eng, sl(dst, a, n - bsh), sl(src, 0, n - bsh - a), sl(src, a + bsh, n), ALU.add, srcb, dstb)
                if a > 0:
                    cp(eng, sl(dst, 0, a), sl(src, bsh, a + bsh), srcb, dstb)
                if bsh > 0:
                    cp(eng, sl(dst, n - bsh, n), sl(src, n - bsh - a, n - a), srcb, dstb)

            def run_steps(axis, n, k, cur, view):
                wdt = 1
                while wdt < k:
                    a, bsh = (1, 0) if wdt == 1 else (wdt // 2, wdt // 2)
                    step(axis, n, a, bsh, view(cur), bufs[cur][1], view(1 - cur), bufs[1 - cur][1])
                    cur = 1 - cur
                    wdt *= 2
                return cur

            for j in range(8):
                gi = j // 2
                k = POOL_WINDOWS[gi]
                dma(pg16(0)[:, 0:L], v_s[j], reads=[V_B[j]], writes=[PG[0]])
                A0 = rview(0)
                memset("pool", A0[:, 0:PADN, :], 0.0, bufs[0][1])
                memset("pool", A0[:, R + PADN:R + 2 * PADN, :], 0.0, bufs[0][1])
                cp(eng, A0[:, PADN:R + PADN, :], vin, [PG[0]], bufs[0][1])
                cur = run_steps(0, R + 2 * PADN, k, 0, rview)
                oth = 1 - cur
                Cv = cview(oth)
                memset("pool", Cv[:, :, 0:PADN], 0.0, bufs[oth][1])
                memset("pool", Cv[:, :, GW + PADN:GW + 2 * PADN], 0.0, bufs[oth][1])
                tt(eng, Cv[:, :, PADN:GW + PADN], rview(cur)[:, PADN:R + PADN, :],
                   rinvR[:, gi * 64:gi * 64 + R].unsqueeze(2).to_broadcast([128, R, GW]), ALU.mult,
                   bufs[cur][1] + [rinvR_b], bufs[oth][1])
                cur = run_steps(1, GW + 2 * PADN, k, oth, cview)
                oth = 1 - cur
                tmp = bufs[oth][0][:, 0:L].rearrange("p (r c) -> p r c", c=GW)
                tt(eng, tmp, cview(cur)[:, :, PADN:GW + PADN], rinv[:, gi * 64:gi * 64 + GW].unsqueeze(1).to_broadcast([128, R, GW]),
                   ALU.mult, bufs[cur][1] + [rinv_b], bufs[oth][1])
                tt(eng, vin, tmp, vin, ALU.subtract, bufs[oth][1] + [PG[0]], [PG[0]])
                dma(d_s[j], pg16(0)[:, 0:L], reads=[PG[0]], writes=[D_B[j]])

        rinvR = sb("rinvR", [128, 256]); rinvR_b = Buf("rinvR")
        rinvR_d = din("rinvR", [128, 256])
        dma(rinvR[:], rinvR_d, writes=[rinvR_b])

        def ssd_group(g, t, b):
            slot = g % 2
            xs, xsb = xs_g[slot], xs_g_b[slot]
            xbc_block(BLK_XS[g], 512, xs_sink_factory(slot, 512))
            w, wb = load_wb(win_s, BLK_ZS[g])
            for c in range(4):
                pa, pab = next_pa()
                for kc in range(KC):
                    mm(pa[:], hT[:, kc, c * 128:(c + 1) * 128], w[:, kc, :], kc == 0, kc == KC - 1, [wb, hT_b], [pab])
                act(zs_g[:, c, :], pa[:], AF.Silu, [pab], [zs_g_b])
            for c in range(4):
                ci = t * 4 + c
                ch_n[0] += 1
                i2 = ch_n[0] % 2
                mm(PS[:, 0:128], BT[:, g, c * 128:(c + 1) * 128], CT[:, g, c * 128:(c + 1) * 128], True, True, [BT_b, CT_b], [PS_b])
                cp("dve", cbm[i2][:, 0, :], PS[:, 0:128], [PS_b], [cbm_b[i2]])
                tt("pool", xsd[i2][:].rearrange("p (h q) -> p h q", h=8), xs[:, c, :].rearrange("p (h q) -> p h q", h=8),
                   bc_h(dsk_bc[:, g * 8:(g + 1) * 8]), ALU.mult, [xsb, rowsb_b], [xsd_b[i2]])
                mm(PY[:], identb, xsd[i2][:], True, False, [cb16_b, xsd_b[i2]], [PY_b])
                dma(Hb16[i2][:], hb_s[ci * 4 + g], reads=[HB_B[ci * 4 + g]], writes=[Hb16_b[i2]])
                for d in range(2):
                    cd = c * 2 + d
                    i = un_[0] % 2
                    un_[0] += 1
                    cp("pool", dtA2[i][:, 0:8], dtA[:, cd, g * 8:(g + 1) * 8], [dtA_b], [dtA2_b[i]])
                    cp("pool", dtA2[i][:, 32:40], dtA[:, cd, g * 8:(g + 1) * 8], [dtA_b], [dtA2_b[i]])
                    mm(PS[0:64, 256:384], dtA2[i][:, 0:64], Uf32[d], True, True, [dtA2_b[i], cf_b], [PS_b])
                    act(posA[i][0:64, :], PS[0:64, 256:384], AF.Copy, [PS_b], [posA_b[i]])
                    stt(posA[i][32:64, :], PS[32:64, 256:384], 1.0, posA[i][32:64, :], ALU.mult, ALU.subtract, [PS_b, posA_b[i]], [posA_b[i]])
                    ts("pool", negA[i][0:64, :], posA[i][0:64, :], -1.0, 1.0, ALU.mult, ALU.mult, [posA_b[i]], [negA_b[i]])
                    for half in range(2):
                        mm(PL[:, half * 512:(half + 1) * 512], negA[i][0:64, :], indb[0:64, half * 512:(half + 1) * 512], True, False,
                           [negA_b[i], indb_b], [PL_b])
                        mm(PL[:, half * 512:(half + 1) * 512], identb, nmb[:, d * 512:(d + 1) * 512], False, False,
                           [cb16_b, nmb_b], [PL_b])
                        for hh in range(4):
                            h = half * 4 + hh
                            mm(PL[:, h * 128:(h + 1) * 128], indb[0:64, h * 128:(h + 1) * 128], posA[i][0:64, :], False, hh == 3,
                               [indb_b, posA_b[i]], [PL_b])
                    act(LT[i][:].rearrange("p h t -> p (h t)"), PL[:], AF.Exp, [PL_b], [LT_b[i]])
                    tt("dve", MT[i][:], LT[i][:], cbm[i2][:, 0, :].unsqueeze(1).to_broadcast([128, 8, 128]), ALU.mult,
                       [LT_b[i], cbm_b[i2]], [MT_b[i]])
                    tt("pool", xdt[i][:].rearrange("p (h q) -> p h q", h=8), xs[:, c, :].rearrange("p (h q) -> p h q", h=8),
                       bc_h(dtc[:, c, d * 32 + g * 8:d * 32 + (g + 1) * 8]), ALU.mult, [xsb, dtc_b], [xdt_b[i]])
                    for h in range(8):
                        mm(PY[:, h * 64:(h + 1) * 64], MT[i][:, h, :], xdt[i][:, h * 64:(h + 1) * 64], False, (d == 1 and h == 7),
                           [MT_b[i], xdt_b[i]], [PY_b])
                    if d == 0:
                        mm(PZ[:], CT[:, g, c * 128:(c + 1) * 128], Sf16[g][:], True, True, [CT_b, Sf16_b[g]], [PZ_b])
                    else:
                        mm(PZ[:], CT[:, g, c * 128:(c + 1) * 128], Hb16[i2][:], True, True, [CT_b, Hb16_b[i2]], [PZ_b])
                    tt("dve", ysb[:, d, :].rearrange("p (h q) -> p h q", h=8), PZ[:].rearrange("p (h q) -> p h q", h=8),
                       bc_h(eall[:, cd, g * 8:(g + 1) * 8]), ALU.mult, [PZ_b, eall_b], [ysb_b])
                    if d == 0:
                        un_[0] -= 1
                        state_unit(g, c, 0, xs, xsb)
                        cp("act", Sf16[g][:], S[0][g][:], [S_b[0][g]], [Sf16_b[g]])
                tt("pool", ysb[:, 0, :], ysb[:, 0, :], ysb[:, 1, :], ALU.add, [ysb_b], [ysb_b])
                tt("dve", ub[:], PY[:], ysb[:, 0, :], ALU.add, [PY_b, ysb_b], [ub_b])
                tt("pool", ub[:], ub[:], zs_g[:, c, :], ALU.mult, [ub_b, zs_g_b], [ub_b])
                act(un[:], ub[:], AF.Square, [ub_b], [un_b, tiny_b], accum_out=tiny[:, 0:1])
                act(tiny[:, 1:2], tiny[:, 0:1], AF.Ln, [tiny_b], [tiny_b], bias=EPS, scale=1.0 / 512)
                act(tiny[:, 1:2], tiny[:, 1:2], AF.Exp, [tiny_b], [tiny_b], scale=-0.5)
                ts("dve", un[:], ub[:], tiny[:, 1:2], None, ALU.mult, ALU.bypass, [ub_b, tiny_b], [un_b])
                for q in range(4):
                    tr(PT[:, q, :], un[:, q * 128:(q + 1) * 128], identb, [un_b, cb16_b], [PT_b])
                cp("act", uT[:, g * 4:(g + 1) * 4, c * 128:(c + 1) * 128], PT[:, 0:4, :], [PT_b], [PG[0], PG[1]])

        class _Stop(Exception):
            pass

        def sweep2(b):
            make_gn(b)
            if stop == "gn":
                raise _Stop()
            for g in range(4):
                cp("act", Sf16[g][:], S[0][g][:], [S_b[0][g]], [Sf16_b[g]])
            for t in range(NT):
                t0 = t * 512
                front(x_d[b], t0, 512, L, b)
                xbc_block(BLK_B, 512, b_sink_factory(512))
                xbc_block(BLK_C, 512, c_sink_factory(512))
                dt_block(512)
                for c in range(4):
                    for d in range(2):
                        cd_prep(c, d)
                if stop == "front2":
                    raise _Stop()
                for g in range(4):
                    ssd_group(g, t, b)
                    if stop == "ssd1":
                        raise _Stop()
                if stop == "ssd":
                    raise _Stop()
                for bi, blk in enumerate(BLK_ZP):
                    w, wb = load_wb(win_s, blk)
                    for cc in range(4):
                        pa, pab = proj_feat(w, wb, cc, 512)
                        act(zpT[:, bi * 4 + cc, :], pa[:], AF.Silu, [pab], [PG[3]])
                for j in range(8):
                    dma(dTt[:, j, :], d_s[j][:, t0:t0 + 512], reads=[D_B[j]], writes=[PG[4]])
                w, wb = load_wb(poolw_s, 0, 256)
                pw = w
                for gi in range(4):
                    for oc in range(2):
                        pa, pab = next_pa()
                        for kc in range(2):
                            mm(pa[:], pw[:, gi * 2 + kc, oc * 128:(oc + 1) * 128], dTt[:, gi * 2 + kc, :], kc == 0, kc == 1, [wb, PG[4]], [pab])
                        j = gi * 2 + oc
                        stt(ypT[:, j, :], pa[:], cols[:, C_PSCALE + j:C_PSCALE + j + 1], zpT[:, j, :], ALU.mult, ALU.mult,
                            [pab, cols_b, PG[3]], [PG[5]])
                if stop == "tail1":
                    raise _Stop()
                for bi in range(2):
                    w, wb = load_wb(win_s, BLK_G[bi])
                    for cc in range(4):
                        pa, pab = proj_feat(w, wb, cc, 512)
                        j = bi * 4 + cc
                        act(gT[:, j, :], pa[:], AF.Sigmoid, [pab, cols_b], [PG[2]], bias=cols[:, C_BMERGE + j:C_BMERGE + j + 1])
                for cb_ in range(2):
                    w, wb = load_wb(wpp_s, cb_)
                    for cc in range(4):
                        pa, pab = next_pa()
                        for kc in range(KC):
                            mm(pa[:], w[:, kc, cc * 128:(cc + 1) * 128], ypT[:, kc, :], kc == 0, kc == KC - 1, [wb, PG[5]], [pab])
                        j = cb_ * 4 + cc
                        tt("dve", p1T[:, j, :], pa[:], gT[:, j, :], ALU.mult, [pab, PG[2]], [PG[3]])
                for bi in range(2):
                    w, wb = load_wb(win_s, BLK_G[2 + bi])
                    for cc in range(4):
                        pa, pab = proj_feat(w, wb, cc, 512)
                        j = bi * 4 + cc
                        act(gT[:, j, :], pa[:], AF.Sigmoid, [pab, cols_b], [PG[2]], bias=cols[:, C_BMERGE + 8 + j:C_BMERGE + 8 + j + 1])
                for cb_ in range(2):
                    w0, wb0 = load_wb(wps_s, 0 * 2 + cb_)
                    w1, wb1 = load_wb(wps_s, 1 * 2 + cb_)
                    for cc in range(4):
                        pa, pab = next_pa()
                        for kc in range(16):
                            w, wb = (w0, wb0) if kc < 8 else (w1, wb1)
                            mm(pa[:], w[:, kc % 8, cc * 128:(cc + 1) * 128], uT[:, kc, :], kc == 0, kc == 15, [wb, PG[0], PG[1]], [pab])
                        j = cb_ * 4 + cc
                        tt("dve", mT[:, j, :], pa[:], gT[:, j, :], ALU.mult, [pab, PG[2]], [PG[6]])
                        tt("pool", mT[:, j, :], mT[:, j, :], p1T[:, j, :], ALU.add, [PG[6], PG[3]], [PG[6]])
                if stop == "tail2":
                    raise _Stop()
                w0, wb0 = load_wb(wout_s, 0)
                w1, wb1 = load_wb(wout_s, 1)
                for s in range(4):
                    for cb_ in range(2):
                        w, wb = (w0, wb0) if cb_ == 0 else (w1, wb1)
                        pa, pab = next_pa()
                        for kc in range(KC):
                            mm(pa[:], mT[:, kc, s * 128:(s + 1) * 128], w[:, kc, :], kc == 0, kc == KC - 1, [wb, PG[6]], [pab])
                        cp(evac_eng(), osb[:, cb_ * 512:(cb_ + 1) * 512], pa[:], [pab], [osb_b])
                    act(xn[0][:], osb[:], AF.Square, [osb_b], [xn_b[0], tiny_b], accum_out=tiny[:, 6:7])
                    act(tiny[:, 7:8], tiny[:, 6:7], AF.Ln, [tiny_b], [tiny_b], bias=EPS, scale=1.0 / D)
                    act(tiny[:, 7:8], tiny[:, 7:8], AF.Exp, [tiny_b], [tiny_b], scale=-0.5)
                    i = s % 3
                    dma(xt[i][:], x_d[b][t0 + s * 128:t0 + (s + 1) * 128, :], writes=[xt_b[i]])
                    ts("dve", osb[:], osb[:], tiny[:, 7:8], None, ALU.mult, ALU.bypass, [osb_b, tiny_b], [osb_b])
                    tt("pool", osb[:], osb[:], gn_bc[:], ALU.mult, [osb_b, gn_b], [osb_b])
                    tt("dve", osb[:], osb[:], xt[i][:], ALU.add, [osb_b, xt_b[i]], [osb_b])
                    if stop == "o3":
                        raise _Stop()
                    if stop == "outx":
                        dma(out_d[b][t0 + s * 128:t0 + (s + 1) * 128, :], xt[i][:], reads=[xt_b[i], osb_b], writes=[OUT_B])
                    else:
                        dma(out_d[b][t0 + s * 128:t0 + (s + 1) * 128, :], osb[:], reads=[osb_b], writes=[OUT_B])

        OUT_B = Buf("out")

        for b in range(NB):
            if stop == "setup":
                break
            context(b)
            if stop == "context":
                break
            sweep1(b)
            if stop == "sweep1":
                break
            pool_phase(b)
            if stop == "pool":
                break
            try:
                sweep2(b)
            except _Stop:
                break

        P.final_wait_all("sp")
        P.emit(nc)
    return nc


def _consts(L):
    t = np.arange(128)
    ident = np.eye(128, dtype=np.float32)
    Uf = (t[:, None] <= t[None, :]).astype(np.float32)
    Ub = (t[:, None] >= t[None, :]).astype(np.float32)
    SUf = (t[:, None] > t[None, :]).astype(np.float32)
    SUb = (t[:, None] < t[None, :]).astype(np.float32)
    ones = np.ones((128, 128), np.float32)
    consts = np.concatenate([ident, Uf, Ub, SUf, SUb, ones], axis=1)
    ind = np.zeros((128, 8, 128), np.float32)
    for h in range(8):
        ind[h, h, :] = 1.0
        ind[32 + h, h, :] = 1.0
    ind = ind.reshape(128, 1024)
    BIG = 30000.0
    nm0 = np.tile(-BIG * (t[None, :] < t[:, None]).astype(np.float32), (1, 4))
    nm1 = np.tile(-BIG * (t[None, :] > t[:, None]).astype(np.float32), (1, 4))
    negm = np.concatenate([nm0, nm1], axis=1).astype(np.float32)

    def inv_counts(n):
        out = np.ones((4, 64), np.float32)
        for gi, k in enumerate(POOL_WINDOWS):
            lo, hi = k // 2, k - 1 - k // 2
            i = np.arange(n)
            cnt = np.minimum(i + hi + 1, n) - np.maximum(i - lo, 0)
            out[gi, :n] = 1.0 / cnt
        return np.broadcast_to(out.reshape(1, 256), (128, 256)).astype(np.float32).copy()

    return consts, ind, negm, inv_counts(GW), inv_counts(L // GW)


def _in_maps(inputs, n_cores, NB, L):
    f = lambda a: np.ascontiguousarray(np.asarray(a, dtype=np.float32))
    x = f(inputs["x"]); c = f(inputs["c"]); ctx = f(inputs["ctx"]); c_ctx = f(inputs["c_ctx"])
    consts, ind, negm, rinv, rinvR = _consts(L)
    colv = lambda v: f(v).reshape(-1, 128).T
    conv_w = f(inputs["conv_w"])[0]
    cw = conv_w.reshape(4, 24, 128).transpose(2, 1, 0).reshape(128, 96)
    cols = np.concatenate([
        colv(inputs["norm_pre"][0]), colv(inputs["b_ada"][0]), colv(inputs["b_merge"][0]), colv(inputs["pool_scale"][0]),
        colv(inputs["conv_b"][0]), cw, colv(inputs["ssd_norm"][0])], axis=1)
    assert cols.shape == (128, 192)
    rows = np.concatenate([f(inputs["norm_post"][0]), f(inputs["dt_bias"][0]).reshape(-1), f(inputs["a_log"][0]).reshape(-1),
                           f(inputs["d_skip"][0])]).reshape(1, R_TOT)
    shared = {
        "w_ada": f(inputs["w_ada"][0]), "w_in": f(inputs["w_in"][0]), "pool_w": f(inputs["pool_w"][0]).reshape(1024, 256),
        "w_pp": f(inputs["w_proj_pool"][0]), "w_ps": f(inputs["w_proj_ssd"][0]), "w_out": f(inputs["w_out"][0]),
        "cols": f(cols), "rows": f(rows), "consts": consts, "ind": ind, "negm": negm, "rinv": rinv, "rinvR": rinvR,
    }
    maps = []
    for i in range(n_cores):
        cc = np.concatenate([c[i * NB:(i + 1) * NB], c_ctx[None, :]], axis=0)
        c3T = cc.reshape(NB + 1, 8, 128).transpose(2, 1, 0).reshape(128, 8 * (NB + 1))
        m = dict(shared)
        m["x"] = f(x[i * NB:(i + 1) * NB]); m["ctx"] = f(ctx[i * NB:(i + 1) * NB]); m["c3T"] = f(c3T)
        maps.append(m)
    return maps


def kernel(**inputs):
    n_cores = 8
    x = inputs["x"]
    Bt, L, _ = x.shape
    NB = Bt // n_cores
    nc = build_nc(NB, L)
    maps = _in_maps(inputs, n_cores, NB, L)
    res = run_bass_kernel_spmd(nc, maps, core_ids=list(range(n_cores)))
    out = np.concatenate([r["out"] for r in res.results], axis=0)
    return out.astype(np.float32)
```

```python
import contextlib
import numpy as np
import concourse.bass as bass
import concourse.mybir as mybir
from concourse.bass_utils import run_bass_kernel_spmd

F32 = mybir.dt.float32
BF16 = mybir.dt.bfloat16
AF = mybir.ActivationFunctionType
ALU = mybir.AluOpType

COMPUTE = ("pe", "act", "dve", "pool")
DMA_RING = 8
EPS = 1e-6


class Buf:
    __slots__ = ("name", "w", "r")

    def __init__(self, name):
        self.name = name
        self.w = None
        self.r = {}


class Prog:
    def __init__(self):
        self.streams = {e: [] for e in ("pe", "act", "dve", "pool", "sp")}
        self.count = {e: 0 for e in COMPUTE}
        self.dma_n = {e: 0 for e in self.streams}
        self.known = {e: {} for e in self.streams}

    def _need(self, stream, waits, tok):
        if tok is None:
            return
        key, val = tok
        if key == stream and stream == "pe":
            return
        if self.known[stream].get(key, 0) >= val:
            return
        if waits.get(key, 0) < val:
            waits[key] = val

    def op(self, stream, fn, reads=(), writes=(), dma=False, noembed=False):
        waits = {}
        for b in reads:
            self._need(stream, waits, b.w)
        for b in writes:
            self._need(stream, waits, b.w)
            for tok in b.r.values():
                self._need(stream, waits, tok)
        if dma:
            m = self.dma_n[stream]
            self.dma_n[stream] = m + 1
            key = ("dma", stream, m % DMA_RING)
            val = 16 * (m // DMA_RING + 1)
            if m >= DMA_RING:
                self._need(stream, waits, (key, val - 16))
            tok = (key, val)
            inc = 16
        else:
            self.count[stream] += 1
            tok = (stream, self.count[stream])
            inc = 1
        for k, v in waits.items():
            self.known[stream][k] = v
        self.streams[stream].append((list(waits.items()), fn, tok[0], inc, noembed))
        for b in writes:
            b.w = tok
            b.r = {}
        for b in reads:
            if b not in writes:
                b.r[tok[0]] = tok
        return tok

    def final_wait_all(self, stream="sp"):
        waits = {}
        for e in COMPUTE:
            if self.count[e]:
                waits[e] = self.count[e]
        for s, n in self.dma_n.items():
            for m in range(max(0, n - DMA_RING), n):
                key = ("dma", s, m % DMA_RING)
                val = 16 * (m // DMA_RING + 1)
                if waits.get(key, 0) < val:
                    waits[key] = val
        self.streams[stream].append((list(waits.items()), None, None, 0, True))

    def emit(self, nc):
        keys = set()
        for s, ops in self.streams.items():
            for waits, fn, key, inc, _ne in ops:
                if key is not None:
                    keys.add(key)
                for k, _ in waits:
                    keys.add(k)
        keys = sorted(keys, key=str)
        with contextlib.ExitStack() as st:
            sems = {}
            for i, k in enumerate(keys):
                sems[k] = st.enter_context(nc.semaphore(f"s{i}"))
            block = st.enter_context(nc.Block())

            def runner(stream):
                embed = stream != "pe"

                def body(eng):
                    for waits, fn, key, inc, noembed in self.streams[stream]:
                        if fn is None or not embed or not waits or noembed:
                            for k, v in waits:
                                eng.wait_ge(sems[k], v)
                            if fn is not None:
                                fn(eng).then_inc(sems[key], inc)
                        else:
                            for k, v in waits[:-1]:
                                eng.wait_ge(sems[k], v)
                            ins = fn(eng)
                            k, v = waits[-1]
                            ins._wait_ge(sems[k], v)
                            ins.then_inc(sems[key], inc)
                return body

            block.tensor(runner("pe"))
            block.scalar(runner("act"))
            block.vector(runner("dve"))
            block.gpsimd(runner("pool"))
            block.sync(runner("sp"))


D = 1024
KC = 8
GW = 64
NCOL = 9280
NH = 32
LC = 256
BLK_V = (0, 1)
BLK_ZP = (2, 3)
BLK_ZS = (4, 5, 6, 7)
BLK_G = (8, 9, 10, 11)
BLK_XS = (12, 13, 14, 15)
BLK_B = 16
BLK_C = 17
BLK_DT = 18
POOL_WINDOWS = (2, 4, 8, 16)
C_NPRE, C_BADA, C_BMERGE, C_PSCALE, C_CONVB, C_CONVW, C_SSDN = 0, 8, 32, 48, 56, 80, 176
R_NPOST, R_DTB, R_ALOG, R_DSKIP, R_TOT = 0, 1024, 1088, 1152, 1184


def build_nc(NB, L, debug=False, stop=None):
    NT = L // 512
    NCH = L // 128
    NCOND = NB + 1
    nc = bass.Bass("TRN2", target_bir_lowering=False)
    din = lambda n, s: nc.dram_tensor(n, s, F32, kind="ExternalInput").ap()
    x_d = din("x", [NB, L, D])
    ctx_d = din("ctx", [NB, LC, D])
    c3_d = din("c3T", [128, KC * NCOND])
    wada_d = din("w_ada", [D, 3 * D])
    win_d = din("w_in", [D, NCOL])
    poolw_d = din("pool_w", [D, 256])
    wpp_d = din("w_pp", [D, D])
    wps_d = din("w_ps", [2 * D, D])
    wout_d = din("w_out", [D, D])
    cols_d = din("cols", [128, 192])
    rows_d = din("rows", [1, R_TOT])
    consts_d = din("consts", [128, 768])
    ind_d = din("ind", [128, 1024])
    negm_d = din("negm", [128, 1024])
    rinv_d = din("rinv", [128, 256])
    out_d = nc.dram_tensor("out", [NB, L, D], F32, kind="ExternalOutput").ap()
    dbg_d = nc.dram_tensor("dbg", [128, 4096], F32, kind="ExternalOutput").ap() if debug else None

    def scratch(n, s, dt=BF16):
        return nc.dram_tensor(n, s, dt, kind="Internal").ap()

    win_s = scratch("win_s", [19, 128, 4096])
    wada_s = scratch("wada_s", [6, 128, 4096])
    poolw_s = scratch("poolw_s", [1, 128, 4096])
    wpp_s = scratch("wpp_s", [2, 128, 4096])
    wps_s = scratch("wps_s", [4, 128, 4096])
    wout_s = scratch("wout_s", [2, 128, 4096])
    v_s = scratch("v_s", [8, 128, L])
    d_s = scratch("d_s", [8, 128, L])
    hb_s = scratch("hb_s", [NCH * 4, 128, 512])

    P = Prog()
    with contextlib.ExitStack() as st:
        def sb(name, shape, dt=F32):
            return st.enter_context(nc.sbuf_tensor("sb_" + name, shape, dt))

        def ps(name, shape, dt=F32):
            return st.enter_context(nc.psum_tensor("ps_" + name, shape, dt))

        cf = sb("cf", [128, 768]); cf_b = Buf("cf")
        identf = cf[:, 0:128]; onesf = cf[:, 640:768]
        Uf32 = {0: cf[:, 128:256], 1: cf[:, 256:384]}
        SUf32 = {0: cf[:, 384:512], 1: cf[:, 512:640]}
        cb16 = sb("cb16", [128, 128], BF16); cb16_b = Buf("cb16")
        identb = cb16[:, 0:128]
        indb = sb("indb", [128, 1024], BF16); indb_b = Buf("indb")
        nmb = sb("nmb", [128, 1024], BF16); nmb_b = Buf("nmb")
        rinv = sb("rinv", [128, 256]); rinv_b = Buf("rinv")
        cols = sb("cols", [128, 192]); cols_b = Buf("cols")
        rowsb = sb("rowsb", [128, 160]); rowsb_b = Buf("rowsb")
        dtb_bc = rowsb[:, 0:64]; A_bc = rowsb[:, 64:128]; dsk_bc = rowsb[:, 128:160]
        sc3 = sb("sc3", [128, KC * NCOND], BF16); sc3_b = Buf("sc3")
        modT = sb("modT", [128, 24 * NCOND]); modT_b = Buf("modT")
        Acol = sb("Acol", [128, KC * NCOND]); Acol_b = Buf("Acol")
        gn_bc = sb("gn_bc", [128, D]); gn_b = Buf("gn")
        tiny = sb("tiny", [128, 64]); tiny_b = Buf("tiny")

        wring = [sb(f"wr{i}", [128, KC, 512], BF16) for i in range(3)]
        wring_b = [Buf(f"wr{i}") for i in range(3)]
        wr_n = [0]

        xt = [sb(f"xt{i}", [128, D]) for i in range(3)]; xt_b = [Buf(f"xt{i}") for i in range(3)]
        xn = [sb(f"xn{i}", [128, D], BF16) for i in range(2)]; xn_b = [Buf(f"xn{i}") for i in range(2)]
        hT = sb("hT", [128, KC, 512], BF16); hT_b = Buf("hT")
        hTh = sb("hTh", [128, KC, 4], BF16); hTh_b = Buf("hTh")
        ssq = sb("ssq", [128, 8]); ssq_b = Buf("ssq")
        rstd = sb("rstd", [128, 8]); rstd_b = Buf("rstd")

        rawT4 = sb("rawT4", [128, 4, 516], BF16); rawT4_b = Buf("rawT4")
        cacc = [sb("cacc0", [128, 512])] * 2; cacc_b = [Buf("cacc0")] * 2
        xbcT = [sb(f"xbcT{i}", [128, 512], BF16) for i in range(2)]; xbcT_b = [Buf(f"xbcT{i}") for i in range(2)]
        BT = sb("BT", [128, 4, 512], BF16); BT_b = Buf("BT")
        CT = sb("CT", [128, 4, 512], BF16); CT_b = Buf("CT")
        Btok = sb("Btok", [128, 4, 512], BF16); Btok_b = Buf("Btok")
        xs_g = [sb(f"xsg{i}", [128, 4, 512], BF16) for i in range(2)]; xs_g_b = [Buf(f"xsg{i}") for i in range(2)]
        zs_g = sb("zsg", [128, 4, 512], BF16); zs_g_b = Buf("zsg")
        dtc = sb("dtc", [128, 4, 64]); dtc_b = Buf("dtc")
        dtA = sb("dtA", [128, 8, 32]); dtA_b = Buf("dtA")
        eall = sb("eall", [128, 8, 96]); eall_b = Buf("eall")
        dts = sb("dts", [128, 8, 32]); dts_b = Buf("dts")

        dtA2 = [sb(f"dtA2{i}", [128, 128]) for i in range(2)]; dtA2_b = [Buf(f"dtA2{i}") for i in range(2)]
        posA = [sb(f"posA{i}", [128, 128], BF16) for i in range(2)]; posA_b = [Buf(f"posA{i}") for i in range(2)]
        LHS = sb("LHS", [128, 128], BF16); LHS_b = Buf("LHS")
        RHS = sb("RHS", [128, 1024], BF16); RHS_b = Buf("RHS")
        nhalf = sb("nhalf", [128, 8]); nhalf_b = Buf("nhalf")
        LT = [sb(f"LT{i}", [128, 8, 128], BF16) for i in range(2)]; LT_b = [Buf(f"LT{i}") for i in range(2)]
        MT = [sb(f"MT{i}", [128, 8, 128], BF16) for i in range(2)]; MT_b = [Buf(f"MT{i}") for i in range(2)]
        cbm = [sb(f"cbm{i}", [128, 1, 128], BF16) for i in range(2)]; cbm_b = [Buf(f"cbm{i}") for i in range(2)]
        xdt = [sb(f"xdt{i}", [128, 512], BF16) for i in range(2)]; xdt_b = [Buf(f"xdt{i}") for i in range(2)]
        xdts = [sb(f"xdts{i}", [128, 512], BF16) for i in range(2)]; xdts_b = [Buf(f"xdts{i}") for i in range(2)]
        xsd = [sb(f"xsd{i}", [128, 512], BF16) for i in range(2)]; xsd_b = [Buf(f"xsd{i}") for i in range(2)]
        ysb = sb("ysb", [128, 2, 512], BF16); ysb_b = Buf("ysb")
        ub = sb("ub", [128, 512]); ub_b = Buf("ub")
        un = sb("un", [128, 512], BF16); un_b = Buf("un")
        S = {0: [sb(f"Sf{g}", [128, 512]) for g in range(4)], 1: [sb(f"Sb{g}", [128, 512]) for g in range(4)]}
        S_b = {0: [Buf(f"Sf{g}") for g in range(4)], 1: [Buf(f"Sb{g}") for g in range(4)]}
        Sf16 = [sb(f"Sf16{g}", [128, 512], BF16) for g in range(4)]; Sf16_b = [Buf(f"Sf16{g}") for g in range(4)]
        Hb16 = [sb(f"Hb16{i}", [128, 512], BF16) for i in range(2)]; Hb16_b = [Buf(f"Hb16{i}") for i in range(2)]
        osb = sb("osb", [128, D]); osb_b = Buf("osb")

        pgf = sb("pgf", [128, 6 * 2048]); PG = [Buf(f"pg{i}") for i in range(7)]
        mTt = sb("mTt", [128, 8, 512], BF16)

        def pg16(page, npages=1):
            return pgf[:, page * 2048:(page + npages) * 2048].bitcast(BF16)

        uT = pg16(0, 2).rearrange("p (k t) -> p k t", k=16)
        gT = pg16(2).rearrange("p (k t) -> p k t", k=8)
        zpT = pg16(3).rearrange("p (k t) -> p k t", k=8)
        p1T = zpT
        dTt = pg16(4).rearrange("p (k t) -> p k t", k=8)
        ypT = pg16(5).rearrange("p (k t) -> p k t", k=8)
        mT = mTt

        PA = [ps(f"PA{i}", [128, 512]) for i in range(2)]; PA_b = [Buf(f"PA{i}") for i in range(2)]
        PT = ps("PT", [128, 8, 128], BF16); PT_b = Buf("PT")
        PL = ps("PL", [128, 1024]); PL_b = Buf("PL")
        PY = ps("PY", [128, 512]); PY_b = Buf("PY")
        PZ = ps("PZ", [128, 512]); PZ_b = Buf("PZ")
        PS = ps("PS", [128, 512]); PS_b = Buf("PS")
        pa_n = [0]

        def next_pa():
            i = pa_n[0] % 2
            pa_n[0] += 1
            return PA[i], PA_b[i]

        rr = [0]

        def evac_eng():
            rr[0] += 1
            return "act" if rr[0] % 2 else "dve"

        def dma(out, in_, reads=(), writes=(), stream="sp", **kw):
            return P.op(stream, lambda e: e.dma_start(out=out, in_=in_, **kw), reads=reads, writes=writes, dma=True)

        def act(out, in_, func, reads, writes, bias=0.0, scale=1.0, accum_out=None):
            if accum_out is None:
                return P.op("act", lambda e: e.activation(out=out, in_=in_, func=func, bias=bias, scale=scale), reads, writes)
            return P.op("act", lambda e: e.activation(out=out, in_=in_, func=func, bias=bias, scale=scale, accum_out=accum_out), reads, writes,
                        noembed=True)

        def tt(eng, out, in0, in1, op, reads, writes):
            return P.op(eng, lambda e: e.tensor_tensor(out=out, in0=in0, in1=in1, op=op), reads, writes)

        def ts(eng, out, in0, s1, s2, op0, op1, reads, writes):
            return P.op(eng, lambda e: e.tensor_scalar(out=out, in0=in0, scalar1=s1, scalar2=s2, op0=op0, op1=op1), reads, writes)

        def stt(out, in0, scalar, in1, op0, op1, reads, writes):
            return P.op("dve", lambda e: e.scalar_tensor_tensor(out=out, in0=in0, scalar=scalar, in1=in1, op0=op0, op1=op1), reads, writes)

        def cp(eng, out, in_, reads, writes):
            if eng == "act":
                return act(out, in_, AF.Copy, reads, writes)
            return P.op(eng, lambda e: e.tensor_copy(out=out, in_=in_), reads, writes)

        def mm(out, lhsT, rhs, start, stop, reads, writes):
            return P.op("pe", lambda e: e.matmul(out, lhsT=lhsT, rhs=rhs, start=start, stop=stop), reads, writes)

        def tr(out, in_, ident, reads, writes):
            return P.op("pe", lambda e: e.transpose(out=out, in_=in_, identity=ident), reads, writes)

        def memset(eng, ap, val, writes):
            return P.op(eng, lambda e: e.memset(ap, val), (), writes)

        def bc_h(ap8, n=8, q=64):
            return ap8.unsqueeze(2).to_broadcast([128, n, q])

        def load_w(scr, blk):
            i = wr_n[0] % 3
            wr_n[0] += 1
            dma(wring[i][:].rearrange("p k c -> p (k c)"), scr[blk], writes=[wring_b[i]])
            return wring[i], wring_b[i]

        dma(cf[:], consts_d, writes=[cf_b])
        dma(cols[:], cols_d, writes=[cols_b])
        dma(rinv[:], rinv_d, writes=[rinv_b])
        dma(rowsb[:], rows_d[:, R_DTB:R_TOT].partition_broadcast(128), writes=[rowsb_b])
        cp("dve", cb16[:], cf[:, 0:128], [cf_b], [cb16_b])
        act(A_bc, A_bc, AF.Exp, [rowsb_b], [rowsb_b])
        ts("dve", A_bc, A_bc, -1.0, None, ALU.mult, ALU.bypass, [rowsb_b], [rowsb_b])
        dma(pgf[:, 0:1024], ind_d, writes=[PG[0]])
        cp("dve", indb[:], pgf[:, 0:1024], [PG[0]], [indb_b])
        dma(pgf[:, 2048:3072], negm_d, writes=[PG[1]])
        cp("dve", nmb[:], pgf[:, 2048:3072], [PG[1]], [nmb_b])
        for i in range(2):
            memset("pool", dtA2[i][:], 0.0, [dtA2_b[i]])
            memset("pool", posA[i][:], 0.0, [posA_b[i]])
        memset("pool", LHS[:], 1.0, [LHS_b])
        memset("pool", LHS[0:64, :], 0.0, [LHS_b])
        memset("pool", nhalf[:], -0.5, [nhalf_b])
        cp("pool", RHS[:], indb[:], [indb_b], [RHS_b])
        dma(tiny[:, 0:KC * NCOND], c3_d, writes=[tiny_b])
        act(sc3[:], tiny[:, 0:KC * NCOND], AF.Silu, [tiny_b], [sc3_b])

        cv_n = [0]

        def convert(src, K, N, scr, scale_col0=None):
            nkh = K // 1024
            ncb = (N + 511) // 512
            for kh in range(nkh):
                for cb_ in range(ncb):
                    w = min(512, N - cb_ * 512)
                    blk = kh * ncb + cb_
                    for half in range(2):
                        i = cv_n[0] % 2
                        cv_n[0] += 1
                        stg = pgf[:, i * 2048:(i + 1) * 2048].rearrange("p (k c) -> p k c", k=4)
                        cst = pg16(2 + i)[:, 0:2048].rearrange("p (k c) -> p k c", k=4)
                        r0 = kh * 1024 + half * 512
                        dma(stg[:, :, 0:w], src[r0:r0 + 512, cb_ * 512:cb_ * 512 + w].rearrange("(k p) c -> p k c", p=128),
                            writes=[PG[i]])
                        eng = ("act", "dve", "pool")[cv_n[0] % 3]
                        if scale_col0 is None:
                            cp(eng, cst[:, :, 0:w], stg[:, :, 0:w], [PG[i]], [PG[2 + i]])
                        else:
                            for k in range(4):
                                c0 = scale_col0 + kh * 8 + half * 4 + k
                                ts("dve", cst[:, k, 0:w], stg[:, k, 0:w], cols[:, c0:c0 + 1], None, ALU.mult, ALU.bypass,
                                   [PG[i], cols_b], [PG[2 + i]])
                        dst = scr[blk].rearrange("p (k c) -> p k c", k=8)[:, half * 4:half * 4 + 4, 0:w]
                        dma(dst, cst[:, :, 0:w], reads=[PG[2 + i]], writes=[SCR_B[id(scr)]])

        SCR_B = {}
        for s_ in (win_s, wada_s, poolw_s, wpp_s, wps_s, wout_s):
            SCR_B[id(s_)] = Buf("scr")
        convert(wada_d, D, 3 * D, wada_s)
        convert(win_d, D, NCOL, win_s)
        convert(poolw_d, D, 256, poolw_s)
        convert(wpp_d, D, D, wpp_s)
        convert(wps_d, 2 * D, D, wps_s, scale_col0=C_SSDN)
        convert(wout_d, D, D, wout_s)
        W_B = lambda scr: SCR_B[id(scr)]

        def load_wb(scr, blk, ncols=512):
            i = wr_n[0] % 3
            wr_n[0] += 1
            if ncols == 512:
                dma(wring[i][:].rearrange("p k c -> p (k c)"), scr[blk], reads=[W_B(scr)], writes=[wring_b[i]])
            else:
                dma(wring[i][:, :, 0:ncols], scr[blk].rearrange("p (k c) -> p k c", k=8)[:, :, 0:ncols], reads=[W_B(scr)],
                    writes=[wring_b[i]])
            return wring[i], wring_b[i]

        for blk in range(6):
            w, wb = load_wb(wada_s, blk)
            for cc in range(4):
                j = blk * 4 + cc
                for kc in range(KC):
                    mm(PS[:, j * 4:j * 4 + NCOND], w[:, kc, cc * 128:(cc + 1) * 128], sc3[:, kc * NCOND:(kc + 1) * NCOND],
                       kc == 0, kc == KC - 1, [wb, sc3_b], [PS_b])
        tt("dve", modT[:].rearrange("p (j i) -> p j i", i=NCOND), PS[:, 0:96].rearrange("p (j i) -> p j i", i=4)[:, :, 0:NCOND],
           cols[:, C_BADA:C_BADA + 24].unsqueeze(2).to_broadcast([128, 24, NCOND]), ALU.add, [PS_b, cols_b], [modT_b])
        mod3 = modT[:].rearrange("p (j i) -> p j i", i=NCOND)
        ts("dve", Acol[:].rearrange("p (k i) -> p k i", i=NCOND), mod3[:, 8:16, :], 1.0, None, ALU.add, ALU.bypass, [modT_b], [Acol_b])
        tt("dve", Acol[:].rearrange("p (k i) -> p k i", i=NCOND), Acol[:].rearrange("p (k i) -> p k i", i=NCOND),
           cols[:, C_NPRE:C_NPRE + 8].unsqueeze(2).to_broadcast([128, 8, NCOND]), ALU.mult, [Acol_b, cols_b], [Acol_b])
        Acol3 = Acol[:].rearrange("p (k i) -> p k i", i=NCOND)

        def make_gn(b):
            dma(osb[:], rows_d[:, R_NPOST:R_NPOST + D].partition_broadcast(128), writes=[osb_b])
            for half in range(2):
                pa, pab = next_pa()
                for k4 in range(4):
                    kc = half * 4 + k4
                    dg = xt[0][:, k4 * 128:(k4 + 1) * 128]
                    ts("dve", dg, identf, mod3[:, 16 + kc, b:b + 1], None, ALU.mult, ALU.bypass, [cf_b, modT_b], [xt_b[0]])
                    mm(pa[:, k4 * 128:(k4 + 1) * 128], onesf, dg, True, True, [cf_b, xt_b[0]], [pab])
                tt("dve", gn_bc[:, half * 512:(half + 1) * 512], pa[:], osb[:, half * 512:(half + 1) * 512], ALU.mult,
                   [pab, osb_b], [gn_b])

        def front(src, t0, ntok, seqlen, ci):
            nsub = ntok // 128
            xts = []
            for s in range(nsub + 1):
                i = s % 3
                if s < nsub:
                    dma(xt[i][:], src[t0 + s * 128:t0 + (s + 1) * 128, :], writes=[xt_b[i]])
                    npart = 128
                else:
                    lo = max(t0 - 2, 0)
                    hi = min(t0 + ntok, seqlen - 1)
                    dma(xt[i][0:2, :], src[lo:lo + 2, :], writes=[xt_b[i]])
                    dma(xt[i][2:3, :], src[hi:hi + 1, :], writes=[xt_b[i]])
                    dma(xt[i][3:4, :], src[hi:hi + 1, :], writes=[xt_b[i]])
                    npart = 4
                j = s % 2
                act(xn[j][0:npart, :], xt[i][0:npart, :], AF.Square, [xt_b[i]], [xn_b[j], ssq_b], accum_out=ssq[0:npart, s:s + 1])
                ts("pool", rstd[0:npart, s:s + 1], ssq[0:npart, s:s + 1], 1.0 / D, EPS, ALU.mult, ALU.add, [ssq_b], [rstd_b])
                tt("pool", rstd[0:npart, s:s + 1], rstd[0:npart, s:s + 1], nhalf[0:npart, 0:1], ALU.pow, [rstd_b, nhalf_b], [rstd_b])
                ts("pool", xn[j][0:npart, :], xt[i][0:npart, :], rstd[0:npart, s:s + 1], 1.0, ALU.mult, ALU.mult,
                   [xt_b[i], rstd_b], [xn_b[j]])
                for kc in range(KC):
                    tr(PT[:, kc, 0:npart], xn[j][0:npart, kc * 128:(kc + 1) * 128], identb[0:npart, 0:npart], [xn_b[j], cb16_b], [PT_b])
                for kc in range(KC):
                    if s < nsub:
                        o = hT[:, kc, s * 128:(s + 1) * 128]; ob = hT_b
                    else:
                        o = hTh[:, kc, 0:4]; ob = hTh_b
                    a_ap = Acol3[:, kc, ci:ci + 1]
                    s_ap = mod3[:, kc, ci:ci + 1]
                    if evac_eng() == "act":
                        act(o, PT[:, kc, 0:npart], AF.Identity, [PT_b, Acol_b, modT_b], [ob], bias=s_ap, scale=a_ap)
                    else:
                        ts("dve", o, PT[:, kc, 0:npart], a_ap, s_ap, ALU.mult, ALU.add, [PT_b, Acol_b, modT_b], [ob])
            if t0 == 0:
                memset("pool", hTh[:, :, 0:2], 0.0, [hTh_b])
            if t0 + ntok >= seqlen:
                memset("pool", hTh[:, :, 2:4], 0.0, [hTh_b])

        def proj_feat(w, wb, cc, ntok, halo_idx=None):
            pa, pab = next_pa()
            for kc in range(KC):
                mm(pa[:, 0:ntok], w[:, kc, cc * 128:(cc + 1) * 128], hT[:, kc, 0:ntok], kc == 0, kc == KC - 1, [wb, hT_b], [pab])
            if halo_idx is not None:
                for kc in range(KC):
                    mm(PS[:, halo_idx * 4:halo_idx * 4 + 4], w[:, kc, cc * 128:(cc + 1) * 128], hTh[:, kc, 0:4], kc == 0, kc == KC - 1,
                       [wb, hTh_b], [PS_b])
            return pa, pab

        cn = [0]

        def xbc_block(blk, ntok, sink):
            w, wb = load_wb(win_s, blk)
            pas = []
            for cc in range(4):
                pa, pab = proj_feat(w, wb, cc, ntok, halo_idx=cc)
                cp(evac_eng(), rawT4[:, cc, 2:2 + ntok], pa[:, 0:ntok], [pab], [rawT4_b])
            PSh = PS[:, 0:16].rearrange("p (c f) -> p c f", f=4)
            cp("dve", rawT4[:, :, 0:2], PSh[:, :, 0:2], [PS_b], [rawT4_b])
            cp("dve", rawT4[:, :, 2 + ntok:3 + ntok], PSh[:, :, 2:3], [PS_b], [rawT4_b])
            for cc in range(4):
                gcc = (blk - 12) * 4 + cc
                i = cn[0] % 2
                cn[0] += 1
                wcol = lambda k: cols[:, C_CONVW + gcc * 4 + k:C_CONVW + gcc * 4 + k + 1]
                ts("dve", cacc[i][:, 0:ntok], rawT4[:, cc, 0:ntok], wcol(0), None, ALU.mult, ALU.bypass, [rawT4_b, cols_b], [cacc_b[i]])
                for k in range(1, 4):
                    stt(cacc[i][:, 0:ntok], rawT4[:, cc, k:k + ntok], wcol(k), cacc[i][:, 0:ntok], ALU.mult, ALU.add,
                        [rawT4_b, cols_b, cacc_b[i]], [cacc_b[i]])
                sink(cc, cacc[i], cacc_b[i], cols[:, C_CONVB + gcc:C_CONVB + gcc + 1])

        def dt_block(ntok):
            w, wb = load_wb(win_s, BLK_DT, 64)
            nchk = ntok // 128
            for c in range(nchk):
                pa, pab = next_pa()
                for kc in range(KC):
                    mm(pa[:, 0:64], hT[:, kc, c * 128:(c + 1) * 128], w[:, kc, 0:64], kc == 0, kc == KC - 1, [wb, hT_b], [pab])
                tt("dve", dtc[:, c, :], pa[:, 0:64], dtb_bc, ALU.add, [pab, rowsb_b], [dtc_b])
            act(dtc[:, 0:nchk, :], dtc[:, 0:nchk, :], AF.Exp, [dtc_b], [dtc_b])
            act(dtc[:, 0:nchk, :], dtc[:, 0:nchk, :], AF.Ln, [dtc_b], [dtc_b], bias=1.0)

        def cd_prep(c, d):
            cd = c * 2 + d
            tt("dve", dtA[:, cd, :], dtc[:, c, d * 32:(d + 1) * 32], A_bc[:, d * 32:(d + 1) * 32], ALU.mult, [dtc_b, rowsb_b], [dtA_b])
            mm(PS[:, 128:160], Uf32[d], dtA[:, cd, :], True, True, [cf_b, dtA_b], [PS_b])
            mm(PS[:, 160:192], onesf, dtA[:, cd, :], True, True, [cf_b, dtA_b], [PS_b])
            mm(PS[:, 192:224], SUf32[d], dtA[:, cd, :], True, True, [cf_b, dtA_b], [PS_b])
            act(eall[:, cd, :], PS[:, 128:224], AF.Exp, [PS_b], [eall_b])
            tt("dve", dts[:, cd, :], dtc[:, c, d * 32:(d + 1) * 32], eall[:, cd, 64:96], ALU.mult, [dtc_b, eall_b], [dts_b])

        un_ = [0]
        ch_n = [0]

        def state_unit(g, c, d, xs, xsb, store_ci=None):
            cd = c * 2 + d
            i = un_[0] % 2
            un_[0] += 1
            tt("pool", xdts[i][:].rearrange("p (h q) -> p h q", h=8), xs[:, c, :].rearrange("p (h q) -> p h q", h=8),
               bc_h(dts[:, cd, g * 8:(g + 1) * 8]), ALU.mult, [xsb, dts_b], [xdts_b[i]])
            pa, pab = next_pa()
            mm(pa[:], Btok[:, c, g * 128:(g + 1) * 128], xdts[i][:], True, True, [Btok_b, xdts_b[i]], [pab])
            Sg, Sgb = S[d][g], S_b[d][g]
            if store_ci is not None:
                j = un_[0] % 2
                cp("act", Hb16[j][:], Sg[:], [Sgb], [Hb16_b[j]])
                dma(hb_s[store_ci * 4 + g], Hb16[j][:], reads=[Hb16_b[j]], writes=[HB_B[store_ci * 4 + g]])
            tt("pool", Sg[:].rearrange("p (h q) -> p h q", h=8), Sg[:].rearrange("p (h q) -> p h q", h=8),
               bc_h(eall[:, cd, 32 + g * 8:32 + (g + 1) * 8]), ALU.mult, [Sgb, eall_b], [Sgb])
            tt("dve", Sg[:], Sg[:], pa[:], ALU.add, [Sgb, pab], [Sgb])

        HB_B = [Buf(f"hb{i}") for i in range(NCH * 4)]
        V_B = [Buf(f"v{i}") for i in range(8)]
        D_B = [Buf(f"d{i}") for i in range(8)]

        def xs_sink_factory(slot, ntok):
            xsl, xslb = xs_g[slot], xs_g_b[slot]

            def sink(cc, acc, accb, bias):
                j = cn[0] % 2
                act(xbcT[j][:, 0:ntok], acc[:, 0:ntok], AF.Silu, [accb, cols_b], [xbcT_b[j]], bias=bias)
                for c in range(ntok // 128):
                    tr(PT[:, c, :], xbcT[j][:, c * 128:(c + 1) * 128], identb, [xbcT_b[j], cb16_b], [PT_b])
                nchk = ntok // 128
                cp(evac_eng(), xsl[:, 0:nchk, cc * 128:(cc + 1) * 128], PT[:, 0:nchk, :], [PT_b], [xslb])
            return sink

        def b_sink_factory(ntok):
            def sink(cc, acc, accb, bias):
                act(BT[:, cc, 0:ntok], acc[:, 0:ntok], AF.Silu, [accb, cols_b], [BT_b], bias=bias)
                for c in range(ntok // 128):
                    tr(PT[:, c, :], BT[:, cc, c * 128:(c + 1) * 128], identb, [BT_b, cb16_b], [PT_b])
                nchk = ntok // 128
                cp(evac_eng(), Btok[:, 0:nchk, cc * 128:(cc + 1) * 128], PT[:, 0:nchk, :], [PT_b], [Btok_b])
            return sink

        def c_sink_factory(ntok):
            def sink(cc, acc, accb, bias):
                act(CT[:, cc, 0:ntok], acc[:, 0:ntok], AF.Silu, [accb, cols_b], [CT_b], bias=bias)
            return sink

        def dbg_dump(ap, ncols, rows=128, col0=0):
            if debug:
                cp("dve", osb[0:rows, 0:ncols], ap, [], [osb_b])
                dma(dbg_d[0:rows, col0:col0 + ncols], osb[0:rows, 0:ncols], reads=[osb_b], writes=[DBG_B])
        DBG_B = Buf("dbg")

        def context(b):
            for d in range(2):
                for g in range(4):
                    memset("pool", S[d][g][:], 0.0, [S_b[d][g]])
            front(ctx_d[b], 0, LC, LC, NB)
            xbc_block(BLK_B, LC, b_sink_factory(LC))
            dt_block(LC)
            for c in range(2):
                for d in range(2):
                    cd_prep(c, d)
            for g in range(4):
                slot = g % 2
                xbc_block(BLK_XS[g], LC, xs_sink_factory(slot, LC))
                for c in (0, 1):
                    state_unit(g, c, 0, xs_g[slot], xs_g_b[slot])
                for c in (1, 0):
                    state_unit(g, c, 1, xs_g[slot], xs_g_b[slot])

        def sweep1(b):
            for t in range(NT - 1, -1, -1):
                t0 = t * 512
                front(x_d[b], t0, 512, L, b)
                for bi, blk in enumerate(BLK_V):
                    w, wb = load_wb(win_s, blk)
                    for cc in range(4):
                        pa, pab = proj_feat(w, wb, cc, 512)
                        cp(evac_eng(), dTt[:, bi * 4 + cc, :], pa[:], [pab], [PG[4]])
                for j in range(8):
                    dma(v_s[j][:, t0:t0 + 512], dTt[:, j, :], reads=[PG[4]], writes=[V_B[j]])
                xbc_block(BLK_B, 512, b_sink_factory(512))
                dt_block(512)
                for c in range(4):
                    cd_prep(c, 1)
                for g in range(4):
                    slot = g % 2
                    xbc_block(BLK_XS[g], 512, xs_sink_factory(slot, 512))
                    for c in (3, 2, 1, 0):
                        state_unit(g, c, 1, xs_g[slot], xs_g_b[slot], store_ci=t * 4 + c)

        def pool_phase(b):
            R = L // GW
            PADN = 8
            vin = pg16(0)[:, 0:L].rearrange("p (r c) -> p r c", c=GW)
            bufs = [(pgf[:, 2048:2048 + 5120], [PG[1], PG[2], PG[3]]), (pgf[:, 2048 + 5120:2048 + 10240], [PG[3], PG[4], PG[5]])]

            def rview(i):
                return bufs[i][0][:, 0:(R + 2 * PADN) * GW].rearrange("p (r c) -> p r c", c=GW)

            def cview(i):
                return bufs[i][0][:, 0:R * (GW + 2 * PADN)].rearrange("p (r c) -> p r c", c=GW + 2 * PADN)

            eng = "dve"

            def step(axis, n, a, bsh, src, srcb, dst, dstb):
                def sl(ap, lo, hi):
                    return ap[:, lo:hi, :] if axis == 0 else ap[:, :, lo:hi]
                tt(eng, sl(dst, a, n - bsh), sl(src, 0, n - bsh - a), sl(src, a + bsh, n), ALU.add, srcb, dstb)
                if a > 0:
                    cp(eng, sl(dst, 0, a), sl(src, bsh, a + bsh), srcb, dstb)
                if bsh > 0:
                    cp(eng, sl(dst, n - bsh, n), sl(src, n - bsh - a, n - a), srcb, dstb)

            def run_steps(axis, n, k, cur, view):
                wdt = 1
                while wdt < k:
                    a, bsh = (1, 0) if wdt == 1 else (wdt // 2, wdt // 2)
                    step(axis, n, a, bsh, view(cur), bufs[cur][1], view(1 - cur), bufs[1 - cur][1])
                    cur = 1 - cur
                    wdt *= 2
                return cur

            for j in range(8):
                gi = j // 2
                k = POOL_WINDOWS[gi]
                dma(pg16(0)[:, 0:L], v_s[j], reads=[V_B[j]], writes=[PG[0]])
                A0 = rview(0)
                memset("pool", A0[:, 0:PADN, :], 0.0, bufs[0][1])
                memset("pool", A0[:, R + PADN:R + 2 * PADN, :], 0.0, bufs[0][1])
                cp(eng, A0[:, PADN:R + PADN, :], vin, [PG[0]], bufs[0][1])
                cur = run_steps(0, R + 2 * PADN, k, 0, rview)
                oth = 1 - cur
                Cv = cview(oth)
                memset("pool", Cv[:, :, 0:PADN], 0.0, bufs[oth][1])
                memset("pool", Cv[:, :, GW + PADN:GW + 2 * PADN], 0.0, bufs[oth][1])
                tt(eng, Cv[:, :, PADN:GW + PADN], rview(cur)[:, PADN:R + PADN, :],
                   rinvR[:, gi * 64:gi * 64 + R].unsqueeze(2).to_broadcast([128, R, GW]), ALU.mult,
                   bufs[cur][1] + [rinvR_b], bufs[oth][1])
                cur = run_steps(1, GW + 2 * PADN, k, oth, cview)
                oth = 1 - cur
                tmp = bufs[oth][0][:, 0:L].rearrange("p (r c) -> p r c", c=GW)
                tt(eng, tmp, cview(cur)[:, :, PADN:GW + PADN], rinv[:, gi * 64:gi * 64 + GW].unsqueeze(1).to_broadcast([128, R, GW]),
                   ALU.mult, bufs[cur][1] + [rinv_b], bufs[oth][1])
                tt(eng, vin, tmp, vin, ALU.subtract, bufs[oth][1] + [PG[0]], [PG[0]])
                dma(d_s[j], pg16(0)[:, 0:L], reads=[PG[0]], writes=[D_B[j]])

        rinvR = sb("rinvR", [128, 256]); rinvR_b = Buf("rinvR")
        rinvR_d = din("rinvR", [128, 256])
        dma(rinvR[:], rinvR_d, writes=[rinvR_b])

        def ssd_group(g, t, b):
            slot = g % 2
            xs, xsb = xs_g[slot], xs_g_b[slot]
            xbc_block(BLK_XS[g], 512, xs_sink_factory(slot, 512))
            w, wb = load_wb(win_s, BLK_ZS[g])
            for c in range(4):
                pa, pab = next_pa()
                for kc in range(KC):
                    mm(pa[:], hT[:, kc, c * 128:(c + 1) * 128], w[:, kc, :], kc == 0, kc == KC - 1, [wb, hT_b], [pab])
                act(zs_g[:, c, :], pa[:], AF.Silu, [pab], [zs_g_b])
            for c in range(4):
                ci = t * 4 + c
                ch_n[0] += 1
                i2 = ch_n[0] % 2
                mm(PS[:, 0:128], BT[:, g, c * 128:(c + 1) * 128], CT[:, g, c * 128:(c + 1) * 128], True, True, [BT_b, CT_b], [PS_b])
                cp("dve", cbm[i2][:, 0, :], PS[:, 0:128], [PS_b], [cbm_b[i2]])
                tt("pool", xsd[i2][:].rearrange("p (h q) -> p h q", h=8), xs[:, c, :].rearrange("p (h q) -> p h q", h=8),
                   bc_h(dsk_bc[:, g * 8:(g + 1) * 8]), ALU.mult, [xsb, rowsb_b], [xsd_b[i2]])
                mm(PY[:], identb, xsd[i2][:], True, False, [cb16_b, xsd_b[i2]], [PY_b])
                dma(Hb16[i2][:], hb_s[ci * 4 + g], reads=[HB_B[ci * 4 + g]], writes=[Hb16_b[i2]])
                for d in range(2):
                    cd = c * 2 + d
                    i = un_[0] % 2
                    un_[0] += 1
                    cp("pool", dtA2[i][:].rearrange("p (a q) -> p a q", q=32)[:, :, 0:8],
                       dtA[:, cd, g * 8:(g + 1) * 8].unsqueeze(1).to_broadcast([128, 4, 8]), [dtA_b], [dtA2_b[i]])
                    mm(PS[:, 256:384], dtA2[i][:], Uf32[d], True, True, [dtA2_b[i], cf_b], [PS_b])
                    act(posA[i][:], PS[:, 256:384], AF.Copy, [PS_b], [posA_b[i]])
                    for r0 in (32, 96):
                        stt(posA[i][r0:r0 + 32, :], PS[r0:r0 + 32, 256:384], 1.0, posA[i][r0:r0 + 32, :], ALU.mult, ALU.subtract,
                            [PS_b, posA_b[i]], [posA_b[i]])
                    ts("pool", LHS[0:64, :], posA[i][0:64, :], -1.0, 1.0, ALU.mult, ALU.mult, [posA_b[i]], [LHS_b])
                    tt("pool", RHS[64:128, :].rearrange("p (h t) -> p h t", h=8), indb[64:128, :].rearrange("p (h t) -> p h t", h=8),
                       posA[i][64:128, :].unsqueeze(1).to_broadcast([64, 8, 128]), ALU.mult, [indb_b, posA_b[i]], [RHS_b])
                    for half in range(2):
                        mm(PL[:, half * 512:(half + 1) * 512], LHS[:], RHS[:, half * 512:(half + 1) * 512], True, False,
                           [LHS_b, RHS_b], [PL_b])
                        mm(PL[:, half * 512:(half + 1) * 512], identb, nmb[:, d * 512:(d + 1) * 512], False, True,
                           [cb16_b, nmb_b], [PL_b])
                    act(LT[i][:].rearrange("p h t -> p (h t)"), PL[:], AF.Exp, [PL_b], [LT_b[i]])
                    tt("dve", MT[i][:], LT[i][:], cbm[i2][:, 0, :].unsqueeze(1).to_broadcast([128, 8, 128]), ALU.mult,
                       [LT_b[i], cbm_b[i2]], [MT_b[i]])
                    tt("pool", xdt[i][:].rearrange("p (h q) -> p h q", h=8), xs[:, c, :].rearrange("p (h q) -> p h q", h=8),
                       bc_h(dtc[:, c, d * 32 + g * 8:d * 32 + (g + 1) * 8]), ALU.mult, [xsb, dtc_b], [xdt_b[i]])
                    for h in range(8):
                        mm(PY[:, h * 64:(h + 1) * 64], MT[i][:, h, :], xdt[i][:, h * 64:(h + 1) * 64], False, (d == 1 and h == 7),
                           [MT_b[i], xdt_b[i]], [PY_b])
                    if d == 0:
                        mm(PZ[:], CT[:, g, c * 128:(c + 1) * 128], Sf16[g][:], True, True, [CT_b, Sf16_b[g]], [PZ_b])
                    else:
                        mm(PZ[:], CT[:, g, c * 128:(c + 1) * 128], Hb16[i2][:], True, True, [CT_b, Hb16_b[i2]], [PZ_b])
                    tt("dve", ysb[:, d, :].rearrange("p (h q) -> p h q", h=8), PZ[:].rearrange("p (h q) -> p h q", h=8),
                       bc_h(eall[:, cd, g * 8:(g + 1) * 8]), ALU.mult, [PZ_b, eall_b], [ysb_b])
                    if d == 0:
                        un_[0] -= 1
                        state_unit(g, c, 0, xs, xsb)
                        cp("act", Sf16[g][:], S[0][g][:], [S_b[0][g]], [Sf16_b[g]])
                tt("pool", ysb[:, 0, :], ysb[:, 0, :], ysb[:, 1, :], ALU.add, [ysb_b], [ysb_b])
                tt("dve", ub[:], PY[:], ysb[:, 0, :], ALU.add, [PY_b, ysb_b], [ub_b])
                tt("pool", ub[:], ub[:], zs_g[:, c, :], ALU.mult, [ub_b, zs_g_b], [ub_b])
                act(un[:], ub[:], AF.Square, [ub_b], [un_b, tiny_b], accum_out=tiny[:, 0:1])
                ts("pool", tiny[:, 1:2], tiny[:, 0:1], 1.0 / 512, EPS, ALU.mult, ALU.add, [tiny_b], [tiny_b])
                tt("pool", tiny[:, 1:2], tiny[:, 1:2], nhalf[:, 0:1], ALU.pow, [tiny_b, nhalf_b], [tiny_b])
                ts("dve", un[:], ub[:], tiny[:, 1:2], None, ALU.mult, ALU.bypass, [ub_b, tiny_b], [un_b])
                for q in range(4):
                    tr(PT[:, q, :], un[:, q * 128:(q + 1) * 128], identb, [un_b, cb16_b], [PT_b])
                cp("act", uT[:, g * 4:(g + 1) * 4, c * 128:(c + 1) * 128], PT[:, 0:4, :], [PT_b], [PG[0], PG[1]])

        class _Stop(Exception):
            pass

        def sweep2(b):
            make_gn(b)
            if stop == "gn":
                raise _Stop()
            for g in range(4):
                cp("act", Sf16[g][:], S[0][g][:], [S_b[0][g]], [Sf16_b[g]])
            for t in range(NT):
                t0 = t * 512
                front(x_d[b], t0, 512, L, b)
                xbc_block(BLK_B, 512, b_sink_factory(512))
                xbc_block(BLK_C, 512, c_sink_factory(512))
                dt_block(512)
                for c in range(4):
                    for d in range(2):
                        cd_prep(c, d)
                if stop == "front2":
                    raise _Stop()
                for g in range(4):
                    ssd_group(g, t, b)
                    if stop == "ssd1":
                        raise _Stop()
                if stop == "ssd":
                    raise _Stop()
                for bi, blk in enumerate(BLK_ZP):
                    w, wb = load_wb(win_s, blk)
                    for cc in range(4):
                        pa, pab = proj_feat(w, wb, cc, 512)
                        act(zpT[:, bi * 4 + cc, :], pa[:], AF.Silu, [pab], [PG[3]])
                for j in range(8):
                    dma(dTt[:, j, :], d_s[j][:, t0:t0 + 512], reads=[D_B[j]], writes=[PG[4]])
                w, wb = load_wb(poolw_s, 0, 256)
                pw = w
                for gi in range(4):
                    for oc in range(2):
                        pa, pab = next_pa()
                        for kc in range(2):
                            mm(pa[:], pw[:, gi * 2 + kc, oc * 128:(oc + 1) * 128], dTt[:, gi * 2 + kc, :], kc == 0, kc == 1, [wb, PG[4]], [pab])
                        j = gi * 2 + oc
                        stt(ypT[:, j, :], pa[:], cols[:, C_PSCALE + j:C_PSCALE + j + 1], zpT[:, j, :], ALU.mult, ALU.mult,
                            [pab, cols_b, PG[3]], [PG[5]])
                if stop == "tail1":
                    raise _Stop()
                for bi in range(2):
                    w, wb = load_wb(win_s, BLK_G[bi])
                    for cc in range(4):
                        pa, pab = proj_feat(w, wb, cc, 512)
                        j = bi * 4 + cc
                        act(gT[:, j, :], pa[:], AF.Sigmoid, [pab, cols_b], [PG[2]], bias=cols[:, C_BMERGE + j:C_BMERGE + j + 1])
                for cb_ in range(2):
                    w, wb = load_wb(wpp_s, cb_)
                    for cc in range(4):
                        pa, pab = next_pa()
                        for kc in range(KC):
                            mm(pa[:], w[:, kc, cc * 128:(cc + 1) * 128], ypT[:, kc, :], kc == 0, kc == KC - 1, [wb, PG[5]], [pab])
                        j = cb_ * 4 + cc
                        tt("dve", p1T[:, j, :], pa[:], gT[:, j, :], ALU.mult, [pab, PG[2]], [PG[3]])
                for bi in range(2):
                    w, wb = load_wb(win_s, BLK_G[2 + bi])
                    for cc in range(4):
                        pa, pab = proj_feat(w, wb, cc, 512)
                        j = bi * 4 + cc
                        act(gT[:, j, :], pa[:], AF.Sigmoid, [pab, cols_b], [PG[2]], bias=cols[:, C_BMERGE + 8 + j:C_BMERGE + 8 + j + 1])
                for cb_ in range(2):
                    w0, wb0 = load_wb(wps_s, 0 * 2 + cb_)
                    w1, wb1 = load_wb(wps_s, 1 * 2 + cb_)
                    for cc in range(4):
                        pa, pab = next_pa()
                        for kc in range(16):
                            w, wb = (w0, wb0) if kc < 8 else (w1, wb1)
                            mm(pa[:], w[:, kc % 8, cc * 128:(cc + 1) * 128], uT[:, kc, :], kc == 0, kc == 15, [wb, PG[0], PG[1]], [pab])
                        j = cb_ * 4 + cc
                        tt("dve", mT[:, j, :], pa[:], gT[:, j, :], ALU.mult, [pab, PG[2]], [PG[6]])
                        tt("pool", mT[:, j, :], mT[:, j, :], p1T[:, j, :], ALU.add, [PG[6], PG[3]], [PG[6]])
                if stop == "tail2":
                    raise _Stop()
                w0, wb0 = load_wb(wout_s, 0)
                w1, wb1 = load_wb(wout_s, 1)
                for s in range(4):
                    for cb_ in range(2):
                        w, wb = (w0, wb0) if cb_ == 0 else (w1, wb1)
                        pa, pab = next_pa()
                        for kc in range(KC):
                            mm(pa[:], mT[:, kc, s * 128:(s + 1) * 128], w[:, kc, :], kc == 0, kc == KC - 1, [wb, PG[6]], [pab])
                        cp(evac_eng(), osb[:, cb_ * 512:(cb_ + 1) * 512], pa[:], [pab], [osb_b])
                    act(xn[0][:], osb[:], AF.Square, [osb_b], [xn_b[0], tiny_b], accum_out=tiny[:, 6:7])
                    ts("pool", tiny[:, 7:8], tiny[:, 6:7], 1.0 / D, EPS, ALU.mult, ALU.add, [tiny_b], [tiny_b])
                    tt("pool", tiny[:, 7:8], tiny[:, 7:8], nhalf[:, 0:1], ALU.pow, [tiny_b, nhalf_b], [tiny_b])
                    i = s % 3
                    dma(xt[i][:], x_d[b][t0 + s * 128:t0 + (s + 1) * 128, :], writes=[xt_b[i]])
                    ts("dve", osb[:], osb[:], tiny[:, 7:8], None, ALU.mult, ALU.bypass, [osb_b, tiny_b], [osb_b])
                    tt("pool", osb[:], osb[:], gn_bc[:], ALU.mult, [osb_b, gn_b], [osb_b])
                    tt("dve", osb[:], osb[:], xt[i][:], ALU.add, [osb_b, xt_b[i]], [osb_b])
                    if stop == "o3":
                        raise _Stop()
                    if stop == "outx":
                        dma(out_d[b][t0 + s * 128:t0 + (s + 1) * 128, :], xt[i][:], reads=[xt_b[i], osb_b], writes=[OUT_B])
                    else:
                        dma(out_d[b][t0 + s * 128:t0 + (s + 1) * 128, :], osb[:], reads=[osb_b], writes=[OUT_B])

        OUT_B = Buf("out")

        for b in range(NB):
            if stop == "setup":
                break
            context(b)
            if stop == "context":
                break
            sweep1(b)
            if stop == "sweep1":
                break
            pool_phase(b)
            if stop == "pool":
                break
            try:
                sweep2(b)
            except _Stop:
                break

        P.final_wait_all("sp")
        P.emit(nc)
    return nc


def _consts(L):
    t = np.arange(128)
    ident = np.eye(128, dtype=np.float32)
    Uf = (t[:, None] <= t[None, :]).astype(np.float32)
    Ub = (t[:, None] >= t[None, :]).astype(np.float32)
    SUf = (t[:, None] > t[None, :]).astype(np.float32)
    SUb = (t[:, None] < t[None, :]).astype(np.float32)
    ones = np.ones((128, 128), np.float32)
    consts = np.concatenate([ident, Uf, Ub, SUf, SUb, ones], axis=1)
    ind = np.zeros((128, 8, 128), np.float32)
    for h in range(8):
        for grp in range(4):
            ind[32 * grp + h, h, :] = 1.0
    ind = ind.reshape(128, 1024)
    BIG = 30000.0
    nm0 = np.tile(-BIG * (t[None, :] < t[:, None]).astype(np.float32), (1, 4))
    nm1 = np.tile(-BIG * (t[None, :] > t[:, None]).astype(np.float32), (1, 4))
    negm = np.concatenate([nm0, nm1], axis=1).astype(np.float32)

    def inv_counts(n):
        out = np.ones((4, 64), np.float32)
        for gi, k in enumerate(POOL_WINDOWS):
            lo, hi = k // 2, k - 1 - k // 2
            i = np.arange(n)
            cnt = np.minimum(i + hi + 1, n) - np.maximum(i - lo, 0)
            out[gi, :n] = 1.0 / cnt
        return np.broadcast_to(out.reshape(1, 256), (128, 256)).astype(np.float32).copy()

    return consts, ind, negm, inv_counts(GW), inv_counts(L // GW)


def _in_maps(inputs, n_cores, NB, L):
    f = lambda a: np.ascontiguousarray(np.asarray(a, dtype=np.float32))
    x = f(inputs["x"]); c = f(inputs["c"]); ctx = f(inputs["ctx"]); c_ctx = f(inputs["c_ctx"])
    consts, ind, negm, rinv, rinvR = _consts(L)
    colv = lambda v: f(v).reshape(-1, 128).T
    conv_w = f(inputs["conv_w"])[0]
    cw = conv_w.reshape(4, 24, 128).transpose(2, 1, 0).reshape(128, 96)
    cols = np.concatenate([
        colv(inputs["norm_pre"][0]), colv(inputs["b_ada"][0]), colv(inputs["b_merge"][0]), colv(inputs["pool_scale"][0]),
        colv(inputs["conv_b"][0]), cw, colv(inputs["ssd_norm"][0])], axis=1)
    assert cols.shape == (128, 192)
    rows = np.concatenate([f(inputs["norm_post"][0]), f(inputs["dt_bias"][0]).reshape(-1), f(inputs["a_log"][0]).reshape(-1),
                           f(inputs["d_skip"][0])]).reshape(1, R_TOT)
    shared = {
        "w_ada": f(inputs["w_ada"][0]), "w_in": f(inputs["w_in"][0]), "pool_w": f(inputs["pool_w"][0]).reshape(1024, 256),
        "w_pp": f(inputs["w_proj_pool"][0]), "w_ps": f(inputs["w_proj_ssd"][0]), "w_out": f(inputs["w_out"][0]),
        "cols": f(cols), "rows": f(rows), "consts": consts, "ind": ind, "negm": negm, "rinv": rinv, "rinvR": rinvR,
    }
    maps = []
    for i in range(n_cores):
        cc = np.concatenate([c[i * NB:(i + 1) * NB], c_ctx[None, :]], axis=0)
        c3T = cc.reshape(NB + 1, 8, 128).transpose(2, 1, 0).reshape(128, 8 * (NB + 1))
        m = dict(shared)
        m["x"] = f(x[i * NB:(i + 1) * NB]); m["ctx"] = f(ctx[i * NB:(i + 1) * NB]); m["c3T"] = f(c3T)
        maps.append(m)
    return maps


def kernel(**inputs):
    n_cores = 8
    x = inputs["x"]
    Bt, L, _ = x.shape
    NB = Bt // n_cores
    nc = build_nc(NB, L)
    maps = _in_maps(inputs, n_cores, NB, L)
    res = run_bass_kernel_spmd(nc, maps, core_ids=list(range(n_cores)))
    out = np.concatenate([r["out"] for r in res.results], axis=0)
    return out.astype(np.float32)
```

```python
import contextlib
import numpy as np
import concourse.bass as bass
import concourse.mybir as mybir
from concourse.bass_utils import run_bass_kernel_spmd

F32 = mybir.dt.float32
BF16 = mybir.dt.bfloat16
AF = mybir.ActivationFunctionType
ALU = mybir.AluOpType

COMPUTE = ("pe", "act", "dve", "pool")
DMA_RING = 8
EPS = 1e-6


class Buf:
    __slots__ = ("name", "w", "r")

    def __init__(self, name):
        self.name = name
        self.w = None
        self.r = {}


class Prog:
    def __init__(self):
        self.streams = {e: [] for e in ("pe", "act", "dve", "pool", "sp")}
        self.count = {e: 0 for e in COMPUTE}
        self.dma_n = {e: 0 for e in self.streams}
        self.known = {e: {} for e in self.streams}

    def _need(self, stream, waits, tok):
        if tok is None:
            return
        key, val = tok
        if key == stream and stream == "pe":
            return
        if self.known[stream].get(key, 0) >= val:
            return
        if waits.get(key, 0) < val:
            waits[key] = val

    def op(self, stream, fn, reads=(), writes=(), dma=False, noembed=False):
        waits = {}
        for b in reads:
            self._need(stream, waits, b.w)
        for b in writes:
            self._need(stream, waits, b.w)
            for tok in b.r.values():
                self._need(stream, waits, tok)
        if dma:
            m = self.dma_n[stream]
            self.dma_n[stream] = m + 1
            key = ("dma", stream, m % DMA_RING)
            val = 16 * (m // DMA_RING + 1)
            if m >= DMA_RING:
                self._need(stream, waits, (key, val - 16))
            tok = (key, val)
            inc = 16
        else:
            self.count[stream] += 1
            tok = (stream, self.count[stream])
            inc = 1
        for k, v in waits.items():
            self.known[stream][k] = v
        self.streams[stream].append((list(waits.items()), fn, tok[0], inc, noembed))
        for b in writes:
            b.w = tok
            b.r = {}
        for b in reads:
            if b not in writes:
                b.r[tok[0]] = tok
        return tok

    def final_wait_all(self, stream="sp"):
        waits = {}
        for e in COMPUTE:
            if self.count[e]:
                waits[e] = self.count[e]
        for s, n in self.dma_n.items():
            for m in range(max(0, n - DMA_RING), n):
                key = ("dma", s, m % DMA_RING)
                val = 16 * (m // DMA_RING + 1)
                if waits.get(key, 0) < val:
                    waits[key] = val
        self.streams[stream].append((list(waits.items()), None, None, 0, True))

    def emit(self, nc):
        keys = set()
        for s, ops in self.streams.items():
            for waits, fn, key, inc, _ne in ops:
                if key is not None:
                    keys.add(key)
                for k, _ in waits:
                    keys.add(k)
        keys = sorted(keys, key=str)
        with contextlib.ExitStack() as st:
            sems = {}
            for i, k in enumerate(keys):
                sems[k] = st.enter_context(nc.semaphore(f"s{i}"))
            block = st.enter_context(nc.Block())

            def runner(stream):
                embed = stream != "pe"

                def body(eng):
                    for waits, fn, key, inc, noembed in self.streams[stream]:
                        if fn is None or not embed or not waits or noembed:
                            for k, v in waits:
                                eng.wait_ge(sems[k], v)
                            if fn is not None:
                                fn(eng).then_inc(sems[key], inc)
                        else:
                            for k, v in waits[:-1]:
                                eng.wait_ge(sems[k], v)
                            ins = fn(eng)
                            k, v = waits[-1]
                            ins._wait_ge(sems[k], v)
                            ins.then_inc(sems[key], inc)
                return body

            block.tensor(runner("pe"))
            block.scalar(runner("act"))
            block.vector(runner("dve"))
            block.gpsimd(runner("pool"))
            block.sync(runner("sp"))


D = 1024
KC = 8
GW = 64
NCOL = 9280
NH = 32
LC = 256
BLK_V = (0, 1)
BLK_ZP = (2, 3)
BLK_ZS = (4, 5, 6, 7)
BLK_G = (8, 9, 10, 11)
BLK_XS = (12, 13, 14, 15)
BLK_B = 16
BLK_C = 17
BLK_DT = 18
POOL_WINDOWS = (2, 4, 8, 16)
C_NPRE, C_BADA, C_BMERGE, C_PSCALE, C_CONVB, C_CONVW, C_SSDN = 0, 8, 32, 48, 56, 80, 176
R_NPOST, R_DTB, R_ALOG, R_DSKIP, R_TOT = 0, 1024, 1088, 1152, 1184


def build_nc(NB, L, debug=False, stop=None):
    NT = L // 512
    NCH = L // 128
    NCOND = NB + 1
    nc = bass.Bass("TRN2", target_bir_lowering=False)
    din = lambda n, s: nc.dram_tensor(n, s, F32, kind="ExternalInput").ap()
    x_d = din("x", [NB, L, D])
    ctx_d = din("ctx", [NB, LC, D])
    c3_d = din("c3T", [128, KC * NCOND])
    wada_d = din("w_ada", [D, 3 * D])
    win_d = din("w_in", [D, NCOL])
    poolw_d = din("pool_w", [D, 256])
    wpp_d = din("w_pp", [D, D])
    wps_d = din("w_ps", [2 * D, D])
    wout_d = din("w_out", [D, D])
    cols_d = din("cols", [128, 192])
    rows_d = din("rows", [1, R_TOT])
    consts_d = din("consts", [128, 768])
    ind_d = din("ind", [128, 1024])
    negm_d = din("negm", [128, 1024])
    rinv_d = din("rinv", [128, 256])
    out_d = nc.dram_tensor("out", [NB, L, D], F32, kind="ExternalOutput").ap()
    dbg_d = nc.dram_tensor("dbg", [128, 4096], F32, kind="ExternalOutput").ap() if debug else None

    def scratch(n, s, dt=BF16):
        return nc.dram_tensor(n, s, dt, kind="Internal").ap()

    win_s = scratch("win_s", [19, 128, 4096])
    wada_s = scratch("wada_s", [6, 128, 4096])
    poolw_s = scratch("poolw_s", [1, 128, 4096])
    wpp_s = scratch("wpp_s", [2, 128, 4096])
    wps_s = scratch("wps_s", [4, 128, 4096])
    wout_s = scratch("wout_s", [2, 128, 4096])
    v_s = scratch("v_s", [8, 128, L])
    d_s = scratch("d_s", [8, 128, L])
    hb_s = scratch("hb_s", [NCH * 4, 128, 512])

    P = Prog()
    with contextlib.ExitStack() as st:
        def sb(name, shape, dt=F32):
            return st.enter_context(nc.sbuf_tensor("sb_" + name, shape, dt))

        def ps(name, shape, dt=F32):
            return st.enter_context(nc.psum_tensor("ps_" + name, shape, dt))

        cf = sb("cf", [128, 768]); cf_b = Buf("cf")
        identf = cf[:, 0:128]; onesf = cf[:, 640:768]
        Uf32 = {0: cf[:, 128:256], 1: cf[:, 256:384]}
        SUf32 = {0: cf[:, 384:512], 1: cf[:, 512:640]}
        cb16 = sb("cb16", [128, 128], BF16); cb16_b = Buf("cb16")
        identb = cb16[:, 0:128]
        indb = sb("indb", [128, 1024], BF16); indb_b = Buf("indb")
        nmb = sb("nmb", [128, 1024], BF16); nmb_b = Buf("nmb")
        rinv = sb("rinv", [128, 256]); rinv_b = Buf("rinv")
        cols = sb("cols", [128, 192]); cols_b = Buf("cols")
        rowsb = sb("rowsb", [128, 160]); rowsb_b = Buf("rowsb")
        dtb_bc = rowsb[:, 0:64]; A_bc = rowsb[:, 64:128]; dsk_bc = rowsb[:, 128:160]
        sc3 = sb("sc3", [128, KC * NCOND], BF16); sc3_b = Buf("sc3")
        modT = sb("modT", [128, 24 * NCOND]); modT_b = Buf("modT")
        Acol = sb("Acol", [128, KC * NCOND]); Acol_b = Buf("Acol")
        gn_bc = sb("gn_bc", [128, D]); gn_b = Buf("gn")
        tiny = sb("tiny", [128, 64]); tiny_b = Buf("tiny")

        wring = [sb(f"wr{i}", [128, KC, 512], BF16) for i in range(3)]
        wring_b = [Buf(f"wr{i}") for i in range(3)]
        wr_n = [0]

        xt = [sb(f"xt{i}", [128, D]) for i in range(3)]; xt_b = [Buf(f"xt{i}") for i in range(3)]
        xn = [sb(f"xn{i}", [128, D], BF16) for i in range(2)]; xn_b = [Buf(f"xn{i}") for i in range(2)]
        hT = sb("hT", [128, KC, 512], BF16); hT_b = Buf("hT")
        hTh = sb("hTh", [128, KC, 4], BF16); hTh_b = Buf("hTh")
        ssq = sb("ssq", [128, 8]); ssq_b = Buf("ssq")
        rstd = sb("rstd", [128, 8]); rstd_b = Buf("rstd")

        rawT4 = sb("rawT4", [128, 4, 516], BF16); rawT4_b = Buf("rawT4")
        cacc = [sb("cacc0", [128, 512])] * 2; cacc_b = [Buf("cacc0")] * 2
        xbcT = [sb("xbcT0", [128, 512], BF16)] * 2; xbcT_b = [Buf("xbcT0")] * 2
        BT = sb("BT", [128, 4, 512], BF16); BT_b = Buf("BT")
        CT = sb("CT", [128, 4, 512], BF16); CT_b = Buf("CT")
        Btok = sb("Btok", [128, 4, 512], BF16); Btok_b = Buf("Btok")
        xs_g = [sb(f"xsg{i}", [128, 4, 512], BF16) for i in range(2)]; xs_g_b = [Buf(f"xsg{i}") for i in range(2)]
        zs_g = sb("zsg", [128, 4, 512], BF16); zs_g_b = Buf("zsg")
        dtc = sb("dtc", [128, 4, 64]); dtc_b = Buf("dtc")
        dtA = sb("dtA", [128, 8, 32]); dtA_b = Buf("dtA")
        eall = sb("eall", [128, 8, 96]); eall_b = Buf("eall")
        dts = sb("dts", [128, 8, 32]); dts_b = Buf("dts")

        dtA2a = sb("dtA2a", [128, 4, 128]); dtA2a_b = Buf("dtA2a")
        posAa = sb("posAa", [128, 1024], BF16); posAa_b = Buf("posAa")
        LHSa = sb("LHSa", [128, 1024], BF16); LHSa_b = Buf("LHSa")
        RHS = [sb(f"RHS{i}", [128, 1024], BF16) for i in range(2)]; RHS_b = [Buf(f"RHS{i}") for i in range(2)]
        nhalf = sb("nhalf", [128, 8]); nhalf_b = Buf("nhalf")
        LT = [sb(f"LT{i}", [128, 8, 128], BF16) for i in range(2)]; LT_b = [Buf(f"LT{i}") for i in range(2)]
        MT = [sb("MT0", [128, 8, 128], BF16)] * 2; MT_b = [Buf("MT0")] * 2
        cbm = [sb(f"cbm{i}", [128, 1, 128], BF16) for i in range(2)]; cbm_b = [Buf(f"cbm{i}") for i in range(2)]
        xdt2 = [sb(f"xdt2{i}", [128, 2, 512], BF16) for i in range(2)]; xdt2_b = [Buf(f"xdt2{i}") for i in range(2)]
        xdts = [sb("xdts0", [128, 512], BF16)] * 2; xdts_b = [Buf("xdts0")] * 2
        xsd = [sb("xsd0", [128, 512], BF16)] * 2; xsd_b = [Buf("xsd0")] * 2
        ysb = [sb(f"ysb{i}", [128, 512], BF16) for i in range(2)]; ysb_b = [Buf(f"ysb{i}") for i in range(2)]
        un = [sb(f"un{i}", [128, 512], BF16) for i in range(2)]; un_b = [Buf(f"un{i}") for i in range(2)]
        S = {0: [sb(f"Sf{g}", [128, 512]) for g in range(4)], 1: [sb(f"Sb{g}", [128, 512]) for g in range(4)]}
        S_b = {0: [Buf(f"Sf{g}") for g in range(4)], 1: [Buf(f"Sb{g}") for g in range(4)]}
        Sf16 = [sb("Sf16s", [128, 512], BF16)] * 4; Sf16_b = [Buf("Sf16s")] * 4
        Hb16 = [sb(f"Hb16{i}", [128, 512], BF16) for i in range(2)]; Hb16_b = [Buf(f"Hb16{i}") for i in range(2)]
        osb = sb("osb", [128, D])
        OSB = [Buf("osb0"), Buf("osb1")]
        ub = [osb[:, 0:512], osb[:, 512:1024]]; ub_b = OSB

        pgf = sb("pgf", [128, 6 * 2048]); PG = [Buf(f"pg{i}") for i in range(7)]
        mTt = sb("mTt", [128, 8, 512], BF16)

        def pg16(page, npages=1):
            return pgf[:, page * 2048:(page + npages) * 2048].bitcast(BF16)

        uT = pg16(0, 2).rearrange("p (k t) -> p k t", k=16)
        gT = pg16(2).rearrange("p (k t) -> p k t", k=8)
        zpT = pg16(3).rearrange("p (k t) -> p k t", k=8)
        p1T = zpT
        dTt = pg16(4).rearrange("p (k t) -> p k t", k=8)
        ypT = pg16(5).rearrange("p (k t) -> p k t", k=8)
        mT = mTt

        PA = [ps(f"PA{i}", [128, 512]) for i in range(2)]; PA_b = [Buf(f"PA{i}") for i in range(2)]
        PT = ps("PT", [128, 8, 128], BF16); PT_b = Buf("PT")
        PL = ps("PL", [128, 1024]); PL_b = Buf("PL")
        PY = ps("PY", [128, 512]); PY_b = Buf("PY")
        PZ = ps("PZ", [128, 512]); PZ_b = Buf("PZ")
        PS = ps("PS", [128, 512]); PS_b = Buf("PS")
        PTalt = PL[:, 0:512].bitcast(BF16).rearrange("p (k t) -> p k t", k=8)
        PYs = [PY, PS]; PYs_b = [PY_b, PS_b]
        pa_n = [0]

        pa_wide = [True]

        def next_pa():
            ring = ((PA[0], PA_b[0]), (PA[1], PA_b[1]), (PY, PY_b), (PS, PS_b)) if pa_wide[0] else ((PA[0], PA_b[0]), (PA[1], PA_b[1]))
            i = pa_n[0] % len(ring)
            pa_n[0] += 1
            return ring[i]

        rr = [0]

        def evac_eng():
            rr[0] += 1
            return "act" if rr[0] % 2 else "dve"

        def dma(out, in_, reads=(), writes=(), stream="sp", **kw):
            return P.op(stream, lambda e: e.dma_start(out=out, in_=in_, **kw), reads=reads, writes=writes, dma=True)

        def act(out, in_, func, reads, writes, bias=0.0, scale=1.0, accum_out=None):
            if accum_out is None:
                return P.op("act", lambda e: e.activation(out=out, in_=in_, func=func, bias=bias, scale=scale), reads, writes)
            return P.op("act", lambda e: e.activation(out=out, in_=in_, func=func, bias=bias, scale=scale, accum_out=accum_out), reads, writes,
                        noembed=True)

        def tt(eng, out, in0, in1, op, reads, writes):
            return P.op(eng, lambda e: e.tensor_tensor(out=out, in0=in0, in1=in1, op=op), reads, writes)

        def ts(eng, out, in0, s1, s2, op0, op1, reads, writes):
            return P.op(eng, lambda e: e.tensor_scalar(out=out, in0=in0, scalar1=s1, scalar2=s2, op0=op0, op1=op1), reads, writes)

        def stt(out, in0, scalar, in1, op0, op1, reads, writes):
            return P.op("dve", lambda e: e.scalar_tensor_tensor(out=out, in0=in0, scalar=scalar, in1=in1, op0=op0, op1=op1), reads, writes)

        def cp(eng, out, in_, reads, writes):
            if eng == "act":
                return act(out, in_, AF.Copy, reads, writes)
            return P.op(eng, lambda e: e.tensor_copy(out=out, in_=in_), reads, writes)

        def mm(out, lhsT, rhs, start, stop, reads, writes):
            return P.op("pe", lambda e: e.matmul(out, lhsT=lhsT, rhs=rhs, start=start, stop=stop), reads, writes)

        def tr(out, in_, ident, reads, writes):
            return P.op("pe", lambda e: e.transpose(out=out, in_=in_, identity=ident), reads, writes)

        def memset(eng, ap, val, writes):
            return P.op(eng, lambda e: e.memset(ap, val), (), writes)

        def bc_h(ap8, n=8, q=64):
            return ap8.unsqueeze(2).to_broadcast([128, n, q])

        def load_w(scr, blk):
            i = wr_n[0] % 3
            wr_n[0] += 1
            dma(wring[i][:].rearrange("p k c -> p (k c)"), scr[blk], writes=[wring_b[i]])
            return wring[i], wring_b[i]

        dma(cf[:], consts_d, writes=[cf_b])
        dma(cols[:], cols_d, writes=[cols_b])
        dma(rinv[:], rinv_d, writes=[rinv_b])
        dma(rowsb[:], rows_d[:, R_DTB:R_TOT].partition_broadcast(128), writes=[rowsb_b])
        cp("dve", cb16[:], cf[:, 0:128], [cf_b], [cb16_b])
        act(A_bc, A_bc, AF.Exp, [rowsb_b], [rowsb_b])
        ts("dve", A_bc, A_bc, -1.0, None, ALU.mult, ALU.bypass, [rowsb_b], [rowsb_b])
        dma(pgf[:, 0:1024], ind_d, writes=[PG[0]])
        cp("dve", indb[:], pgf[:, 0:1024], [PG[0]], [indb_b])
        dma(pgf[:, 2048:3072], negm_d, writes=[PG[1]])
        cp("dve", nmb[:], pgf[:, 2048:3072], [PG[1]], [nmb_b])
        memset("pool", dtA2a[:], 0.0, [dtA2a_b])
        memset("pool", LHSa[:], 1.0, [LHSa_b])
        memset("pool", nhalf[:], -0.5, [nhalf_b])
        for i in range(2):
            cp("pool", RHS[i][:], indb[:], [indb_b], [RHS_b[i]])
        dma(tiny[:, 0:KC * NCOND], c3_d, writes=[tiny_b])
        act(sc3[:], tiny[:, 0:KC * NCOND], AF.Silu, [tiny_b], [sc3_b])

        cv_n = [0]

        def convert(src, K, N, scr, scale_col0=None):
            nkh = K // 1024
            ncb = (N + 511) // 512
            for kh in range(nkh):
                for cb_ in range(ncb):
                    w = min(512, N - cb_ * 512)
                    blk = kh * ncb + cb_
                    for half in range(2):
                        i = cv_n[0] % 2
                        cv_n[0] += 1
                        stg = pgf[:, i * 2048:(i + 1) * 2048].rearrange("p (k c) -> p k c", k=4)
                        cst = pg16(2 + i)[:, 0:2048].rearrange("p (k c) -> p k c", k=4)
                        r0 = kh * 1024 + half * 512
                        dma(stg[:, :, 0:w], src[r0:r0 + 512, cb_ * 512:cb_ * 512 + w].rearrange("(k p) c -> p k c", p=128),
                            writes=[PG[i]])
                        eng = ("act", "dve", "pool")[cv_n[0] % 3]
                        if scale_col0 is None:
                            cp(eng, cst[:, :, 0:w], stg[:, :, 0:w], [PG[i]], [PG[2 + i]])
                        else:
                            for k in range(4):
                                c0 = scale_col0 + kh * 8 + half * 4 + k
                                ts("dve", cst[:, k, 0:w], stg[:, k, 0:w], cols[:, c0:c0 + 1], None, ALU.mult, ALU.bypass,
                                   [PG[i], cols_b], [PG[2 + i]])
                        dst = scr[blk].rearrange("p (k c) -> p k c", k=8)[:, half * 4:half * 4 + 4, 0:w]
                        dma(dst, cst[:, :, 0:w], reads=[PG[2 + i]], writes=[SCR_B[id(scr)]])

        SCR_B = {}
        for s_ in (win_s, wada_s, poolw_s, wpp_s, wps_s, wout_s):
            SCR_B[id(s_)] = Buf("scr")
        convert(wada_d, D, 3 * D, wada_s)
        convert(win_d, D, NCOL, win_s)
        convert(poolw_d, D, 256, poolw_s)
        convert(wpp_d, D, D, wpp_s)
        convert(wps_d, 2 * D, D, wps_s, scale_col0=C_SSDN)
        convert(wout_d, D, D, wout_s)
        W_B = lambda scr: SCR_B[id(scr)]

        def load_wb(scr, blk, ncols=512):
            i = wr_n[0] % 3
            wr_n[0] += 1
            if ncols == 512:
                dma(wring[i][:].rearrange("p k c -> p (k c)"), scr[blk], reads=[W_B(scr)], writes=[wring_b[i]])
            else:
                dma(wring[i][:, :, 0:ncols], scr[blk].rearrange("p (k c) -> p k c", k=8)[:, :, 0:ncols], reads=[W_B(scr)],
                    writes=[wring_b[i]])
            return wring[i], wring_b[i]

        for blk in range(6):
            w, wb = load_wb(wada_s, blk)
            for cc in range(4):
                j = blk * 4 + cc
                for kc in range(KC):
                    mm(PL[:, j * 4:j * 4 + NCOND], w[:, kc, cc * 128:(cc + 1) * 128], sc3[:, kc * NCOND:(kc + 1) * NCOND],
                       kc == 0, kc == KC - 1, [wb, sc3_b], [PL_b])
        tt("dve", modT[:].rearrange("p (j i) -> p j i", i=NCOND), PL[:, 0:96].rearrange("p (j i) -> p j i", i=4)[:, :, 0:NCOND],
           cols[:, C_BADA:C_BADA + 24].unsqueeze(2).to_broadcast([128, 24, NCOND]), ALU.add, [PL_b, cols_b], [modT_b])
        mod3 = modT[:].rearrange("p (j i) -> p j i", i=NCOND)
        ts("dve", Acol[:].rearrange("p (k i) -> p k i", i=NCOND), mod3[:, 8:16, :], 1.0, None, ALU.add, ALU.bypass, [modT_b], [Acol_b])
        tt("dve", Acol[:].rearrange("p (k i) -> p k i", i=NCOND), Acol[:].rearrange("p (k i) -> p k i", i=NCOND),
           cols[:, C_NPRE:C_NPRE + 8].unsqueeze(2).to_broadcast([128, 8, NCOND]), ALU.mult, [Acol_b, cols_b], [Acol_b])
        Acol3 = Acol[:].rearrange("p (k i) -> p k i", i=NCOND)

        def make_gn(b):
            dma(osb[:], rows_d[:, R_NPOST:R_NPOST + D].partition_broadcast(128), writes=[*OSB])
            for half in range(2):
                pa, pab = next_pa()
                for k4 in range(4):
                    kc = half * 4 + k4
                    dg = xt[0][:, k4 * 128:(k4 + 1) * 128]
                    ts("dve", dg, identf, mod3[:, 16 + kc, b:b + 1], None, ALU.mult, ALU.bypass, [cf_b, modT_b], [xt_b[0]])
                    mm(pa[:, k4 * 128:(k4 + 1) * 128], onesf, dg, True, True, [cf_b, xt_b[0]], [pab])
                tt("dve", gn_bc[:, half * 512:(half + 1) * 512], pa[:], osb[:, half * 512:(half + 1) * 512], ALU.mult,
                   [pab, *OSB], [gn_b])

        def front(src, t0, ntok, seqlen, ci):
            nsub = ntok // 128
            xts = []
            for s in range(nsub + 1):
                i = s % 3
                if s < nsub:
                    dma(xt[i][:], src[t0 + s * 128:t0 + (s + 1) * 128, :], writes=[xt_b[i]])
                    npart = 128
                else:
                    lo = max(t0 - 2, 0)
                    hi = min(t0 + ntok, seqlen - 1)
                    dma(xt[i][0:2, :], src[lo:lo + 2, :], writes=[xt_b[i]])
                    dma(xt[i][2:3, :], src[hi:hi + 1, :], writes=[xt_b[i]])
                    dma(xt[i][3:4, :], src[hi:hi + 1, :], writes=[xt_b[i]])
                    npart = 4
                j = s % 2
                act(xn[j][0:npart, :], xt[i][0:npart, :], AF.Square, [xt_b[i]], [xn_b[j], ssq_b], accum_out=ssq[0:npart, s:s + 1])
                ts("pool", rstd[0:npart, s:s + 1], ssq[0:npart, s:s + 1], 1.0 / D, EPS, ALU.mult, ALU.add, [ssq_b], [rstd_b])
                tt("pool", rstd[0:npart, s:s + 1], rstd[0:npart, s:s + 1], nhalf[0:npart, 0:1], ALU.pow, [rstd_b, nhalf_b], [rstd_b])
                ts("pool", xn[j][0:npart, :], xt[i][0:npart, :], rstd[0:npart, s:s + 1], 1.0, ALU.mult, ALU.mult,
                   [xt_b[i], rstd_b], [xn_b[j]])
                ptx, ptxb = (PT, PT_b) if s % 2 == 0 else (PTalt, PL_b)
                for kc in range(KC):
                    tr(ptx[:, kc, 0:npart], xn[j][0:npart, kc * 128:(kc + 1) * 128], identb[0:npart, 0:npart], [xn_b[j], cb16_b], [ptxb])
                for kc in range(KC):
                    if s < nsub:
                        o = hT[:, kc, s * 128:(s + 1) * 128]; ob = hT_b
                    else:
                        o = hTh[:, kc, 0:4]; ob = hTh_b
                    a_ap = Acol3[:, kc, ci:ci + 1]
                    s_ap = mod3[:, kc, ci:ci + 1]
                    if evac_eng() == "act":
                        act(o, ptx[:, kc, 0:npart], AF.Identity, [ptxb, Acol_b, modT_b], [ob], bias=s_ap, scale=a_ap)
                    else:
                        ts("dve", o, ptx[:, kc, 0:npart], a_ap, s_ap, ALU.mult, ALU.add, [ptxb, Acol_b, modT_b], [ob])
            if t0 == 0:
                memset("pool", hTh[:, :, 0:2], 0.0, [hTh_b])
            if t0 + ntok >= seqlen:
                memset("pool", hTh[:, :, 2:4], 0.0, [hTh_b])

        def proj_feat(w, wb, cc, ntok, halo_idx=None):
            pa, pab = next_pa()
            for kc in range(KC):
                mm(pa[:, 0:ntok], w[:, kc, cc * 128:(cc + 1) * 128], hT[:, kc, 0:ntok], kc == 0, kc == KC - 1, [wb, hT_b], [pab])
            if halo_idx is not None:
                for kc in range(KC):
                    mm(PZ[:, halo_idx * 4:halo_idx * 4 + 4], w[:, kc, cc * 128:(cc + 1) * 128], hTh[:, kc, 0:4], kc == 0, kc == KC - 1,
                       [wb, hTh_b], [PZ_b])
            return pa, pab

        cn = [0]

        def xbc_block(blk, ntok, sink):
            w, wb = load_wb(win_s, blk)
            pas = []
            for cc in range(4):
                pa, pab = proj_feat(w, wb, cc, ntok, halo_idx=cc)
                cp(evac_eng(), rawT4[:, cc, 2:2 + ntok], pa[:, 0:ntok], [pab], [rawT4_b])
            PSh = PZ[:, 0:16].rearrange("p (c f) -> p c f", f=4)
            cp("dve", rawT4[:, :, 0:2], PSh[:, :, 0:2], [PZ_b], [rawT4_b])
            cp("dve", rawT4[:, :, 2 + ntok:3 + ntok], PSh[:, :, 2:3], [PZ_b], [rawT4_b])
            for cc in range(4):
                gcc = (blk - 12) * 4 + cc
                i = cn[0] % 2
                cn[0] += 1
                wcol = lambda k: cols[:, C_CONVW + gcc * 4 + k:C_CONVW + gcc * 4 + k + 1]
                ts("dve", cacc[i][:, 0:ntok], rawT4[:, cc, 0:ntok], wcol(0), None, ALU.mult, ALU.bypass, [rawT4_b, cols_b], [cacc_b[i]])
                for k in range(1, 4):
                    stt(cacc[i][:, 0:ntok], rawT4[:, cc, k:k + ntok], wcol(k), cacc[i][:, 0:ntok], ALU.mult, ALU.add,
                        [rawT4_b, cols_b, cacc_b[i]], [cacc_b[i]])
                sink(cc, cacc[i], cacc_b[i], cols[:, C_CONVB + gcc:C_CONVB + gcc + 1])

        def dt_block(ntok):
            w, wb = load_wb(win_s, BLK_DT, 64)
            nchk = ntok // 128
            for c in range(nchk):
                pa, pab = next_pa()
                for kc in range(KC):
                    mm(pa[:, 0:64], hT[:, kc, c * 128:(c + 1) * 128], w[:, kc, 0:64], kc == 0, kc == KC - 1, [wb, hT_b], [pab])
                tt("dve", dtc[:, c, :], pa[:, 0:64], dtb_bc, ALU.add, [pab, rowsb_b], [dtc_b])
            act(dtc[:, 0:nchk, :], dtc[:, 0:nchk, :], AF.Exp, [dtc_b], [dtc_b])
            act(dtc[:, 0:nchk, :], dtc[:, 0:nchk, :], AF.Ln, [dtc_b], [dtc_b], bias=1.0)

        def cd_prep(c, d):
            cd = c * 2 + d
            tt("dve", dtA[:, cd, :], dtc[:, c, d * 32:(d + 1) * 32], A_bc[:, d * 32:(d + 1) * 32], ALU.mult, [dtc_b, rowsb_b], [dtA_b])
            pa, pab = next_pa()
            mm(pa[:, 0:32], Uf32[d], dtA[:, cd, :], True, True, [cf_b, dtA_b], [pab])
            mm(pa[:, 32:64], onesf, dtA[:, cd, :], True, True, [cf_b, dtA_b], [pab])
            mm(pa[:, 64:96], SUf32[d], dtA[:, cd, :], True, True, [cf_b, dtA_b], [pab])
            act(eall[:, cd, :], pa[:, 0:96], AF.Exp, [pab], [eall_b])
            tt("dve", dts[:, cd, :], dtc[:, c, d * 32:(d + 1) * 32], eall[:, cd, 64:96], ALU.mult, [dtc_b, eall_b], [dts_b])

        un_ = [0]
        ch_n = [0]

        def state_unit(g, c, d, xs, xsb, store_ci=None):
            cd = c * 2 + d
            i = un_[0] % 2
            un_[0] += 1
            tt("pool", xdts[i][:].rearrange("p (h q) -> p h q", h=8), xs[:, c, :].rearrange("p (h q) -> p h q", h=8),
               bc_h(dts[:, cd, g * 8:(g + 1) * 8]), ALU.mult, [xsb, dts_b], [xdts_b[i]])
            pa, pab = next_pa()
            mm(pa[:], Btok[:, c, g * 128:(g + 1) * 128], xdts[i][:], True, True, [Btok_b, xdts_b[i]], [pab])
            Sg, Sgb = S[d][g], S_b[d][g]
            if store_ci is not None:
                j = un_[0] % 2
                cp("act", Hb16[j][:], Sg[:], [Sgb], [Hb16_b[j]])
                dma(hb_s[store_ci * 4 + g], Hb16[j][:], reads=[Hb16_b[j]], writes=[HB_B[store_ci * 4 + g]])
            tt("pool", Sg[:].rearrange("p (h q) -> p h q", h=8), Sg[:].rearrange("p (h q) -> p h q", h=8),
               bc_h(eall[:, cd, 32 + g * 8:32 + (g + 1) * 8]), ALU.mult, [Sgb, eall_b], [Sgb])
            tt("dve", Sg[:], Sg[:], pa[:], ALU.add, [Sgb, pab], [Sgb])

        HB_B = [Buf(f"hb{i}") for i in range(NCH * 4)]
        V_B = [Buf(f"v{i}") for i in range(8)]
        D_B = [Buf(f"d{i}") for i in range(8)]

        def xs_sink_factory(slot, ntok):
            xsl, xslb = xs_g[slot], xs_g_b[slot]

            def sink(cc, acc, accb, bias):
                j = cn[0] % 2
                act(xbcT[j][:, 0:ntok], acc[:, 0:ntok], AF.Silu, [accb, cols_b], [xbcT_b[j]], bias=bias)
                for c in range(ntok // 128):
                    tr(PT[:, c, :], xbcT[j][:, c * 128:(c + 1) * 128], identb, [xbcT_b[j], cb16_b], [PT_b])
                nchk = ntok // 128
                cp(evac_eng(), xsl[:, 0:nchk, cc * 128:(cc + 1) * 128], PT[:, 0:nchk, :], [PT_b], [xslb])
            return sink

        def b_sink_factory(ntok):
            def sink(cc, acc, accb, bias):
                act(BT[:, cc, 0:ntok], acc[:, 0:ntok], AF.Silu, [accb, cols_b], [BT_b], bias=bias)
                for c in range(ntok // 128):
                    tr(PT[:, c, :], BT[:, cc, c * 128:(c + 1) * 128], identb, [BT_b, cb16_b], [PT_b])
                nchk = ntok // 128
                cp(evac_eng(), Btok[:, 0:nchk, cc * 128:(cc + 1) * 128], PT[:, 0:nchk, :], [PT_b], [Btok_b])
            return sink

        def c_sink_factory(ntok):
            def sink(cc, acc, accb, bias):
                act(CT[:, cc, 0:ntok], acc[:, 0:ntok], AF.Silu, [accb, cols_b], [CT_b], bias=bias)
            return sink

        def dbg_dump(ap, ncols, rows=128, col0=0):
            if debug:
                cp("dve", osb[0:rows, 0:ncols], ap, [], [*OSB])
                dma(dbg_d[0:rows, col0:col0 + ncols], osb[0:rows, 0:ncols], reads=[*OSB], writes=[DBG_B])
        DBG_B = Buf("dbg")

        def context(b):
            for d in range(2):
                for g in range(4):
                    memset("pool", S[d][g][:], 0.0, [S_b[d][g]])
            front(ctx_d[b], 0, LC, LC, NB)
            xbc_block(BLK_B, LC, b_sink_factory(LC))
            dt_block(LC)
            for c in range(2):
                for d in range(2):
                    cd_prep(c, d)
            for g in range(4):
                slot = g % 2
                xbc_block(BLK_XS[g], LC, xs_sink_factory(slot, LC))
                for c in (0, 1):
                    state_unit(g, c, 0, xs_g[slot], xs_g_b[slot])
                for c in (1, 0):
                    state_unit(g, c, 1, xs_g[slot], xs_g_b[slot])

        def sweep1(b):
            for t in range(NT - 1, -1, -1):
                t0 = t * 512
                front(x_d[b], t0, 512, L, b)
                for bi, blk in enumerate(BLK_V):
                    w, wb = load_wb(win_s, blk)
                    for cc in range(4):
                        pa, pab = proj_feat(w, wb, cc, 512)
                        cp(evac_eng(), dTt[:, bi * 4 + cc, :], pa[:], [pab], [PG[4]])
                for j in range(8):
                    dma(v_s[j][:, t0:t0 + 512], dTt[:, j, :], reads=[PG[4]], writes=[V_B[j]])
                xbc_block(BLK_B, 512, b_sink_factory(512))
                dt_block(512)
                for c in range(4):
                    cd_prep(c, 1)
                xbc_block(BLK_XS[0], 512, xs_sink_factory(0, 512))
                for g in range(4):
                    slot = g % 2
                    if g + 1 < 4:
                        xbc_block(BLK_XS[g + 1], 512, xs_sink_factory((g + 1) % 2, 512))
                    for c in (3, 2, 1, 0):
                        state_unit(g, c, 1, xs_g[slot], xs_g_b[slot], store_ci=t * 4 + c)

        def pool_phase(b):
            R = L // GW
            PADN = 8
            vin = pg16(0)[:, 0:L].rearrange("p (r c) -> p r c", c=GW)
            bufs = [(pgf[:, 2048:2048 + 5120], [PG[1], PG[2], PG[3]]), (pgf[:, 2048 + 5120:2048 + 10240], [PG[3], PG[4], PG[5]])]

            def rview(i):
                return bufs[i][0][:, 0:(R + 2 * PADN) * GW].rearrange("p (r c) -> p r c", c=GW)

            def cview(i):
                return bufs[i][0][:, 0:R * (GW + 2 * PADN)].rearrange("p (r c) -> p r c", c=GW + 2 * PADN)

            eng = "dve"

            def step(axis, n, a, bsh, src, srcb, dst, dstb):
                def sl(ap, lo, hi):
                    return ap[:, lo:hi, :] if axis == 0 else ap[:, :, lo:hi]
                tt(eng, sl(dst, a, n - bsh), sl(src, 0, n - bsh - a), sl(src, a + bsh, n), ALU.add, srcb, dstb)
                if a > 0:
                    cp(eng, sl(dst, 0, a), sl(src, bsh, a + bsh), srcb, dstb)
                if bsh > 0:
                    cp(eng, sl(dst, n - bsh, n), sl(src, n - bsh - a, n - a), srcb, dstb)

            def run_steps(axis, n, k, cur, view):
                wdt = 1
                while wdt < k:
                    a, bsh = (1, 0) if wdt == 1 else (wdt // 2, wdt // 2)
                    step(axis, n, a, bsh, view(cur), bufs[cur][1], view(1 - cur), bufs[1 - cur][1])
                    cur = 1 - cur
                    wdt *= 2
                return cur

            for j in range(8):
                gi = j // 2
                k = POOL_WINDOWS[gi]
                dma(pg16(0)[:, 0:L], v_s[j], reads=[V_B[j]], writes=[PG[0]])
                A0 = rview(0)
                memset("pool", A0[:, 0:PADN, :], 0.0, bufs[0][1])
                memset("pool", A0[:, R + PADN:R + 2 * PADN, :], 0.0, bufs[0][1])
                cp(eng, A0[:, PADN:R + PADN, :], vin, [PG[0]], bufs[0][1])
                cur = run_steps(0, R + 2 * PADN, k, 0, rview)
                oth = 1 - cur
                Cv = cview(oth)
                memset("pool", Cv[:, :, 0:PADN], 0.0, bufs[oth][1])
                memset("pool", Cv[:, :, GW + PADN:GW + 2 * PADN], 0.0, bufs[oth][1])
                tt(eng, Cv[:, :, PADN:GW + PADN], rview(cur)[:, PADN:R + PADN, :],
                   rinvR[:, gi * 64:gi * 64 + R].unsqueeze(2).to_broadcast([128, R, GW]), ALU.mult,
                   bufs[cur][1] + [rinvR_b], bufs[oth][1])
                cur = run_steps(1, GW + 2 * PADN, k, oth, cview)
                oth = 1 - cur
                tmp = bufs[oth][0][:, 0:L].rearrange("p (r c) -> p r c", c=GW)
                tt(eng, tmp, cview(cur)[:, :, PADN:GW + PADN], rinv[:, gi * 64:gi * 64 + GW].unsqueeze(1).to_broadcast([128, R, GW]),
                   ALU.mult, bufs[cur][1] + [rinv_b], bufs[oth][1])
                tt(eng, vin, tmp, vin, ALU.subtract, bufs[oth][1] + [PG[0]], [PG[0]])
                dma(d_s[j], pg16(0)[:, 0:L], reads=[PG[0]], writes=[D_B[j]])

        rinvR = sb("rinvR", [128, 256]); rinvR_b = Buf("rinvR")
        rinvR_d = din("rinvR", [128, 256])
        dma(rinvR[:], rinvR_d, writes=[rinvR_b])

        def ssd_front(g):
            slot = g % 2
            xbc_block(BLK_XS[g], 512, xs_sink_factory(slot, 512))

        def ssd_group(g, t, b, prefetch=None):
            slot = g % 2
            xs, xsb = xs_g[slot], xs_g_b[slot]
            w, wb = load_wb(win_s, BLK_ZS[g])
            for c in range(4):
                pa, pab = next_pa()
                for kc in range(KC):
                    mm(pa[:], hT[:, kc, c * 128:(c + 1) * 128], w[:, kc, :], kc == 0, kc == KC - 1, [wb, hT_b], [pab])
                act(zs_g[:, c, :], pa[:], AF.Silu, [pab], [zs_g_b])
            for hb in range(2):
                cp("pool", dtA2a[:].rearrange("p c (a q) -> p c a q", q=32)[:, :, :, 0:8],
                   dtA[:, hb * 4:(hb + 1) * 4, g * 8:(g + 1) * 8].unsqueeze(2).to_broadcast([128, 4, 4, 8]), [dtA_b], [dtA2a_b])
                for q4 in range(4):
                    cd = hb * 4 + q4
                    mm(PL[:, cd * 128:(cd + 1) * 128], dtA2a[:, q4, :], Uf32[cd % 2], True, True, [dtA2a_b, cf_b], [PL_b])
            act(posAa[:], PL[:], AF.Copy, [PL_b], [posAa_b])
            for r0 in (32, 96):
                stt(posAa[r0:r0 + 32, :], PL[r0:r0 + 32, :], 1.0, posAa[r0:r0 + 32, :], ALU.mult, ALU.subtract, [PL_b, posAa_b], [posAa_b])
            ts("pool", LHSa[0:64, :], posAa[0:64, :], -1.0, 1.0, ALU.mult, ALU.mult, [posAa_b], [LHSa_b])
            cp("act", Sf16[g][:], S[0][g][:], [S_b[0][g]], [Sf16_b[g]])
            if prefetch is not None:
                prefetch()

            units = [(c, d) for c in range(4) for d in range(2)]

            def phaseA(n):
                c, d = units[n]
                cd = c * 2 + d
                i = n % 2
                tt("pool", RHS[i][64:128, :].rearrange("p (h t) -> p h t", h=8), indb[64:128, :].rearrange("p (h t) -> p h t", h=8),
                   posAa[64:128, cd * 128:(cd + 1) * 128].unsqueeze(1).to_broadcast([64, 8, 128]), ALU.mult, [indb_b, posAa_b], [RHS_b[i]])
                for half in range(2):
                    mm(PL[:, half * 512:(half + 1) * 512], LHSa[:, cd * 128:(cd + 1) * 128], RHS[i][:, half * 512:(half + 1) * 512], True, False,
                       [LHSa_b, RHS_b[i]], [PL_b])
                    mm(PL[:, half * 512:(half + 1) * 512], identb, nmb[:, d * 512:(d + 1) * 512], False, True, [cb16_b, nmb_b], [PL_b])
                act(LT[i][:].rearrange("p h t -> p (h t)"), PL[:], AF.Exp, [PL_b], [LT_b[i]])

            def pre(c):
                ci = t * 4 + c
                i2 = c % 2
                pa, pab = next_pa()
                mm(pa[:, 0:128], BT[:, g, c * 128:(c + 1) * 128], CT[:, g, c * 128:(c + 1) * 128], True, True, [BT_b, CT_b], [pab])
                cp("dve", cbm[i2][:, 0, :], pa[:, 0:128], [pab], [cbm_b[i2]])
                tt("pool", xsd[i2][:].rearrange("p (h q) -> p h q", h=8), xs[:, c, :].rearrange("p (h q) -> p h q", h=8),
                   bc_h(dsk_bc[:, g * 8:(g + 1) * 8]), ALU.mult, [xsb, rowsb_b], [xsd_b[i2]])
                mm(PYs[i2][:], identb, xsd[i2][:], True, False, [cb16_b, xsd_b[i2]], [PYs_b[i2]])
                dma(Hb16[i2][:], hb_s[ci * 4 + g], reads=[HB_B[ci * 4 + g]], writes=[Hb16_b[i2]])
                tt("pool", xdt2[i2][:].rearrange("p d (h q) -> p d h q", h=8),
                   xs[:, c, :].rearrange("p (h q) -> p h q", h=8).unsqueeze(1).to_broadcast([128, 2, 8, 64]),
                   dtc[:, c, :].rearrange("p (d h) -> p d h", d=2)[:, :, g * 8:(g + 1) * 8].unsqueeze(3).to_broadcast([128, 2, 8, 64]),
                   ALU.mult, [xsb, dtc_b], [xdt2_b[i2]])

            def phaseB(n):
                c, d = units[n]
                cd = c * 2 + d
                i = n % 2
                i2 = c % 2
                tt("dve", MT[i][:], LT[i][:], cbm[i2][:, 0, :].unsqueeze(1).to_broadcast([128, 8, 128]), ALU.mult, [LT_b[i], cbm_b[i2]], [MT_b[i]])
                for h in range(8):
                    mm(PYs[i2][:, h * 64:(h + 1) * 64], MT[i][:, h, :], xdt2[i2][:, d, h * 64:(h + 1) * 64], False, (d == 1 and h == 7),
                       [MT_b[i], xdt2_b[i2]], [PYs_b[i2]])
                if d == 0:
                    mm(PZ[:], CT[:, g, c * 128:(c + 1) * 128], Sf16[g][:], True, True, [CT_b, Sf16_b[g]], [PZ_b])
                    dst, dstb = ub[i2], ub_b[i2]
                else:
                    mm(PZ[:], CT[:, g, c * 128:(c + 1) * 128], Hb16[i2][:], True, True, [CT_b, Hb16_b[i2]], [PZ_b])
                    dst, dstb = ysb[i2][:], ysb_b[i2]
                tt("dve", dst.rearrange("p (h q) -> p h q", h=8), PZ[:].rearrange("p (h q) -> p h q", h=8),
                   bc_h(eall[:, cd, g * 8:(g + 1) * 8]), ALU.mult, [PZ_b, eall_b], [dstb])
                if d == 0:
                    state_unit(g, c, 0, xs, xsb)
                    cp("act", Sf16[g][:], S[0][g][:], [S_b[0][g]], [Sf16_b[g]])

            def post(c):
                i2 = c % 2
                u_, ubb = ub[i2], ub_b[i2]
                t0c = 16 + 2 * i2
                tt("pool", u_, u_, ysb[i2][:], ALU.add, [ubb, ysb_b[i2]], [ubb])
                tt("dve", u_, u_, PYs[i2][:], ALU.add, [ubb, PYs_b[i2]], [ubb])
                tt("pool", u_, u_, zs_g[:, c, :], ALU.mult, [ubb, zs_g_b], [ubb])
                act(un[i2][:], u_, AF.Square, [ubb], [un_b[i2], tiny_b], accum_out=tiny[:, t0c:t0c + 1])
                ts("pool", tiny[:, t0c + 1:t0c + 2], tiny[:, t0c:t0c + 1], 1.0 / 512, EPS, ALU.mult, ALU.add, [tiny_b], [tiny_b])
                tt("pool", tiny[:, t0c + 1:t0c + 2], tiny[:, t0c + 1:t0c + 2], nhalf[:, 0:1], ALU.pow, [tiny_b, nhalf_b], [tiny_b])
                ts("dve", un[i2][:], u_, tiny[:, t0c + 1:t0c + 2], None, ALU.mult, ALU.bypass, [ubb, tiny_b], [un_b[i2]])
                for q in range(4):
                    tr(PT[:, q, :], un[i2][:, q * 128:(q + 1) * 128], identb, [un_b[i2], cb16_b], [PT_b])
                cp("act", uT[:, g * 4:(g + 1) * 4, c * 128:(c + 1) * 128], PT[:, 0:4, :], [PT_b], [PG[0], PG[1]])

            pre(0)
            phaseA(0)
            pending = None
            for n in range(8):
                if n + 1 < 8:
                    phaseA(n + 1)
                phaseB(n)
                c, d = units[n]
                if d == 0 and pending is not None:
                    post(pending)
                    pending = None
                if d == 1:
                    if c + 1 < 4:
                        pre(c + 1)
                    pending = c
            post(pending)

        class _Stop(Exception):
            pass

        def sweep2(b):
            make_gn(b)
            if stop == "gn":
                raise _Stop()
            for t in range(NT):
                t0 = t * 512
                front(x_d[b], t0, 512, L, b)
                xbc_block(BLK_B, 512, b_sink_factory(512))
                xbc_block(BLK_C, 512, c_sink_factory(512))
                dt_block(512)
                for c in range(4):
                    for d in range(2):
                        cd_prep(c, d)
                if stop == "front2":
                    raise _Stop()
                ssd_front(0)
                pa_wide[0] = False
                for g in range(4):
                    ssd_group(g, t, b, prefetch=(lambda g=g: ssd_front(g + 1)) if g + 1 < 4 else None)
                    if stop == "ssd1":
                        raise _Stop()
                pa_wide[0] = True
                if stop == "ssd":
                    raise _Stop()
                for bi, blk in enumerate(BLK_ZP):
                    w, wb = load_wb(win_s, blk)
                    for cc in range(4):
                        pa, pab = proj_feat(w, wb, cc, 512)
                        act(zpT[:, bi * 4 + cc, :], pa[:], AF.Silu, [pab], [PG[3]])
                for j in range(8):
                    dma(dTt[:, j, :], d_s[j][:, t0:t0 + 512], reads=[D_B[j]], writes=[PG[4]])
                w, wb = load_wb(poolw_s, 0, 256)
                pw = w
                for gi in range(4):
                    for oc in range(2):
                        pa, pab = next_pa()
                        for kc in range(2):
                            mm(pa[:], pw[:, gi * 2 + kc, oc * 128:(oc + 1) * 128], dTt[:, gi * 2 + kc, :], kc == 0, kc == 1, [wb, PG[4]], [pab])
                        j = gi * 2 + oc
                        stt(ypT[:, j, :], pa[:], cols[:, C_PSCALE + j:C_PSCALE + j + 1], zpT[:, j, :], ALU.mult, ALU.mult,
                            [pab, cols_b, PG[3]], [PG[5]])
                if stop == "tail1":
                    raise _Stop()
                for bi in range(2):
                    w, wb = load_wb(win_s, BLK_G[bi])
                    for cc in range(4):
                        pa, pab = proj_feat(w, wb, cc, 512)
                        j = bi * 4 + cc
                        act(gT[:, j, :], pa[:], AF.Sigmoid, [pab, cols_b], [PG[2]], bias=cols[:, C_BMERGE + j:C_BMERGE + j + 1])
                for cb_ in range(2):
                    w, wb = load_wb(wpp_s, cb_)
                    for cc in range(4):
                        pa, pab = next_pa()
                        for kc in range(KC):
                            mm(pa[:], w[:, kc, cc * 128:(cc + 1) * 128], ypT[:, kc, :], kc == 0, kc == KC - 1, [wb, PG[5]], [pab])
                        j = cb_ * 4 + cc
                        tt("dve", p1T[:, j, :], pa[:], gT[:, j, :], ALU.mult, [pab, PG[2]], [PG[3]])
                for bi in range(2):
                    w, wb = load_wb(win_s, BLK_G[2 + bi])
                    for cc in range(4):
                        pa, pab = proj_feat(w, wb, cc, 512)
                        j = bi * 4 + cc
                        act(gT[:, j, :], pa[:], AF.Sigmoid, [pab, cols_b], [PG[2]], bias=cols[:, C_BMERGE + 8 + j:C_BMERGE + 8 + j + 1])
                for cb_ in range(2):
                    w0, wb0 = load_wb(wps_s, 0 * 2 + cb_)
                    w1, wb1 = load_wb(wps_s, 1 * 2 + cb_)
                    for cc in range(4):
                        pa, pab = next_pa()
                        for kc in range(16):
                            w, wb = (w0, wb0) if kc < 8 else (w1, wb1)
                            mm(pa[:], w[:, kc % 8, cc * 128:(cc + 1) * 128], uT[:, kc, :], kc == 0, kc == 15, [wb, PG[0], PG[1]], [pab])
                        j = cb_ * 4 + cc
                        tt("dve", mT[:, j, :], pa[:], gT[:, j, :], ALU.mult, [pab, PG[2]], [PG[6]])
                        tt("pool", mT[:, j, :], mT[:, j, :], p1T[:, j, :], ALU.add, [PG[6], PG[3]], [PG[6]])
                if stop == "tail2":
                    raise _Stop()
                w0, wb0 = load_wb(wout_s, 0)
                w1, wb1 = load_wb(wout_s, 1)
                for s in range(4):
                    for cb_ in range(2):
                        w, wb = (w0, wb0) if cb_ == 0 else (w1, wb1)
                        pa, pab = next_pa()
                        for kc in range(KC):
                            mm(pa[:], mT[:, kc, s * 128:(s + 1) * 128], w[:, kc, :], kc == 0, kc == KC - 1, [wb, PG[6]], [pab])
                        cp(evac_eng(), osb[:, cb_ * 512:(cb_ + 1) * 512], pa[:], [pab], [*OSB])
                    act(xn[0][:], osb[:], AF.Square, [*OSB], [xn_b[0], tiny_b], accum_out=tiny[:, 6:7])
                    ts("pool", tiny[:, 7:8], tiny[:, 6:7], 1.0 / D, EPS, ALU.mult, ALU.add, [tiny_b], [tiny_b])
                    tt("pool", tiny[:, 7:8], tiny[:, 7:8], nhalf[:, 0:1], ALU.pow, [tiny_b, nhalf_b], [tiny_b])
                    i = s % 3
                    dma(xt[i][:], x_d[b][t0 + s * 128:t0 + (s + 1) * 128, :], writes=[xt_b[i]])
                    ts("dve", osb[:], osb[:], tiny[:, 7:8], None, ALU.mult, ALU.bypass, [*OSB, tiny_b], [*OSB])
                    tt("pool", osb[:], osb[:], gn_bc[:], ALU.mult, [*OSB, gn_b], [*OSB])
                    tt("dve", osb[:], osb[:], xt[i][:], ALU.add, [*OSB, xt_b[i]], [*OSB])
                    if stop == "o3":
                        raise _Stop()
                    if stop == "outx":
                        dma(out_d[b][t0 + s * 128:t0 + (s + 1) * 128, :], xt[i][:], reads=[xt_b[i], *OSB], writes=[OUT_B])
                    else:
                        dma(out_d[b][t0 + s * 128:t0 + (s + 1) * 128, :], osb[:], reads=[*OSB], writes=[OUT_B])

        OUT_B = Buf("out")

        for b in range(NB):
            if stop == "setup":
                break
            context(b)
            if stop == "context":
                break
            sweep1(b)
            if stop == "sweep1":
                break
            pool_phase(b)
            if stop == "pool":
                break
            try:
                sweep2(b)
            except _Stop:
                break

        P.final_wait_all("sp")
        P.emit(nc)
    return nc


def _consts(L):
    t = np.arange(128)
    ident = np.eye(128, dtype=np.float32)
    Uf = (t[:, None] <= t[None, :]).astype(np.float32)
    Ub = (t[:, None] >= t[None, :]).astype(np.float32)
    SUf = (t[:, None] > t[None, :]).astype(np.float32)
    SUb = (t[:, None] < t[None, :]).astype(np.float32)
    ones = np.ones((128, 128), np.float32)
    consts = np.concatenate([ident, Uf, Ub, SUf, SUb, ones], axis=1)
    ind = np.zeros((128, 8, 128), np.float32)
    for h in range(8):
        for grp in range(4):
            ind[32 * grp + h, h, :] = 1.0
    ind = ind.reshape(128, 1024)
    BIG = 30000.0
    nm0 = np.tile(-BIG * (t[None, :] < t[:, None]).astype(np.float32), (1, 4))
    nm1 = np.tile(-BIG * (t[None, :] > t[:, None]).astype(np.float32), (1, 4))
    negm = np.concatenate([nm0, nm1], axis=1).astype(np.float32)

    def inv_counts(n):
        out = np.ones((4, 64), np.float32)
        for gi, k in enumerate(POOL_WINDOWS):
            lo, hi = k // 2, k - 1 - k // 2
            i = np.arange(n)
            cnt = np.minimum(i + hi + 1, n) - np.maximum(i - lo, 0)
            out[gi, :n] = 1.0 / cnt
        return np.broadcast_to(out.reshape(1, 256), (128, 256)).astype(np.float32).copy()

    return consts, ind, negm, inv_counts(GW), inv_counts(L // GW)


def _in_maps(inputs, n_cores, NB, L):
    f = lambda a: np.ascontiguousarray(np.asarray(a, dtype=np.float32))
    x = f(inputs["x"]); c = f(inputs["c"]); ctx = f(inputs["ctx"]); c_ctx = f(inputs["c_ctx"])
    consts, ind, negm, rinv, rinvR = _consts(L)
    colv = lambda v: f(v).reshape(-1, 128).T
    conv_w = f(inputs["conv_w"])[0]
    cw = conv_w.reshape(4, 24, 128).transpose(2, 1, 0).reshape(128, 96)
    cols = np.concatenate([
        colv(inputs["norm_pre"][0]), colv(inputs["b_ada"][0]), colv(inputs["b_merge"][0]), colv(inputs["pool_scale"][0]),
        colv(inputs["conv_b"][0]), cw, colv(inputs["ssd_norm"][0])], axis=1)
    assert cols.shape == (128, 192)
    rows = np.concatenate([f(inputs["norm_post"][0]), f(inputs["dt_bias"][0]).reshape(-1), f(inputs["a_log"][0]).reshape(-1),
                           f(inputs["d_skip"][0])]).reshape(1, R_TOT)
    shared = {
        "w_ada": f(inputs["w_ada"][0]), "w_in": f(inputs["w_in"][0]), "pool_w": f(inputs["pool_w"][0]).reshape(1024, 256),
        "w_pp": f(inputs["w_proj_pool"][0]), "w_ps": f(inputs["w_proj_ssd"][0]), "w_out": f(inputs["w_out"][0]),
        "cols": f(cols), "rows": f(rows), "consts": consts, "ind": ind, "negm": negm, "rinv": rinv, "rinvR": rinvR,
    }
    maps = []
    for i in range(n_cores):
        cc = np.concatenate([c[i * NB:(i + 1) * NB], c_ctx[None, :]], axis=0)
        c3T = cc.reshape(NB + 1, 8, 128).transpose(2, 1, 0).reshape(128, 8 * (NB + 1))
        m = dict(shared)
        m["x"] = f(x[i * NB:(i + 1) * NB]); m["ctx"] = f(ctx[i * NB:(i + 1) * NB]); m["c3T"] = f(c3T)
        maps.append(m)
    return maps


def kernel(**inputs):
    n_cores = 8
    x = inputs["x"]
    Bt, L, _ = x.shape
    NB = Bt // n_cores
    nc = build_nc(NB, L)
    maps = _in_maps(inputs, n_cores, NB, L)
    res = run_bass_kernel_spmd(nc, maps, core_ids=list(range(n_cores)))
    out = np.concatenate([r["out"] for r in res.results], axis=0)
    return out.astype(np.float32)
```

```python
import contextlib
import numpy as np
import concourse.bass as bass
import concourse.mybir as mybir
from concourse.bass_utils import run_bass_kernel_spmd

F32 = mybir.dt.float32
BF16 = mybir.dt.bfloat16
AF = mybir.ActivationFunctionType
ALU = mybir.AluOpType

COMPUTE = ("pe", "act", "dve", "pool")
DMA_RING = 8
EPS = 1e-6


class Buf:
    __slots__ = ("name", "w", "r")

    def __init__(self, name):
        self.name = name
        self.w = None
        self.r = {}


class Prog:
    def __init__(self):
        self.streams = {e: [] for e in ("pe", "act", "dve", "pool", "sp")}
        self.count = {e: 0 for e in COMPUTE}
        self.dma_n = {e: 0 for e in self.streams}
        self.known = {e: {} for e in self.streams}
        self.pending = []

    def _need(self, stream, waits, tok):
        if tok is None:
            return
        key, val = tok
        if key == stream and stream == "pe":
            return
        if self.known[stream].get(key, 0) >= val:
            return
        if waits.get(key, 0) < val:
            waits[key] = val

    def op(self, stream, fn, reads=(), writes=(), dma=False, noembed=False, lazy=False):
        if lazy:
            self.pending.append((stream, fn, tuple(reads), tuple(writes), dma, noembed))
            return None
        if self.pending:
            ws, rs = set(writes), set(reads)
            last = -1
            for i, p in enumerate(self.pending):
                pr, pw = set(p[2]), set(p[3])
                if (ws & pr) or (ws & pw) or (rs & pw):
                    last = i
            if last >= 0:
                head, self.pending = self.pending[:last + 1], self.pending[last + 1:]
                for p in head:
                    self._record(*p)
        tok = self._record(stream, fn, reads, writes, dma, noembed)
        if dma and stream == "sp" and self.pending:
            self.flush()
        return tok

    def flush(self):
        head, self.pending = self.pending, []
        for p in head:
            self._record(*p)

    def _record(self, stream, fn, reads=(), writes=(), dma=False, noembed=False):
        waits = {}
        for b in reads:
            self._need(stream, waits, b.w)
        for b in writes:
            self._need(stream, waits, b.w)
            for tok in b.r.values():
                self._need(stream, waits, tok)
        if dma:
            m = self.dma_n[stream]
            self.dma_n[stream] = m + 1
            key = ("dma", stream, m % DMA_RING)
            val = 16 * (m // DMA_RING + 1)
            if m >= DMA_RING:
                self._need(stream, waits, (key, val - 16))
            tok = (key, val)
            inc = 16
        else:
            self.count[stream] += 1
            tok = (stream, self.count[stream])
            inc = 1
        for k, v in waits.items():
            self.known[stream][k] = v
        self.streams[stream].append((list(waits.items()), fn, tok[0], inc, noembed))
        for b in writes:
            b.w = tok
            b.r = {}
        for b in reads:
            if b not in writes:
                b.r[tok[0]] = tok
        return tok

    def final_wait_all(self, stream="sp"):
        self.flush()
        waits = {}
        for e in COMPUTE:
            if self.count[e]:
                waits[e] = self.count[e]
        for s, n in self.dma_n.items():
            for m in range(max(0, n - DMA_RING), n):
                key = ("dma", s, m % DMA_RING)
                val = 16 * (m // DMA_RING + 1)
                if waits.get(key, 0) < val:
                    waits[key] = val
        self.streams[stream].append((list(waits.items()), None, None, 0, True))

    def emit(self, nc):
        keys = set()
        for s, ops in self.streams.items():
            for waits, fn, key, inc, _ne in ops:
                if key is not None:
                    keys.add(key)
                for k, _ in waits:
                    keys.add(k)
        keys = sorted(keys, key=str)
        with contextlib.ExitStack() as st:
            sems = {}
            for i, k in enumerate(keys):
                sems[k] = st.enter_context(nc.semaphore(f"s{i}"))
            block = st.enter_context(nc.Block())

            def runner(stream):
                embed = stream != "pe"

                def body(eng):
                    for waits, fn, key, inc, noembed in self.streams[stream]:
                        if fn is None or not embed or not waits or noembed:
                            for k, v in waits:
                                eng.wait_ge(sems[k], v)
                            if fn is not None:
                                fn(eng).then_inc(sems[key], inc)
                        else:
                            for k, v in waits[:-1]:
                                eng.wait_ge(sems[k], v)
                            ins = fn(eng)
                            k, v = waits[-1]
                            ins._wait_ge(sems[k], v)
                            ins.then_inc(sems[key], inc)
                return body

            block.tensor(runner("pe"))
            block.scalar(runner("act"))
            block.vector(runner("dve"))
            block.gpsimd(runner("pool"))
            block.sync(runner("sp"))


D = 1024
KC = 8
GW = 64
NCOL = 9280
NH = 32
LC = 256
BLK_V = (0, 1)
BLK_ZP = (2, 3)
BLK_ZS = (4, 5, 6, 7)
BLK_G = (8, 9, 10, 11)
BLK_XS = (12, 13, 14, 15)
BLK_B = 16
BLK_C = 17
BLK_DT = 18
POOL_WINDOWS = (2, 4, 8, 16)
C_NPRE, C_BADA, C_BMERGE, C_PSCALE, C_CONVB, C_CONVW, C_SSDN = 0, 8, 32, 48, 56, 80, 176
R_NPOST, R_DTB, R_ALOG, R_DSKIP, R_TOT = 0, 1024, 1088, 1152, 1184


def build_nc(NB, L, debug=False, stop=None):
    NT = L // 512
    NCH = L // 128
    NCOND = NB + 1
    nc = bass.Bass("TRN2", target_bir_lowering=False)
    din = lambda n, s: nc.dram_tensor(n, s, F32, kind="ExternalInput").ap()
    x_d = din("x", [NB, L, D])
    ctx_d = din("ctx", [NB, LC, D])
    c3_d = din("c3T", [128, KC * NCOND])
    wada_d = din("w_ada", [D, 3 * D])
    win_d = din("w_in", [D, NCOL])
    poolw_d = din("pool_w", [D, 256])
    wpp_d = din("w_pp", [D, D])
    wps_d = din("w_ps", [2 * D, D])
    wout_d = din("w_out", [D, D])
    cols_d = din("cols", [128, 192])
    rows_d = din("rows", [1, R_TOT])
    consts_d = din("consts", [128, 768])
    ind_d = din("ind", [128, 1024])
    negm_d = din("negm", [128, 1024])
    rinv_d = din("rinv", [128, 256])
    out_d = nc.dram_tensor("out", [NB, L, D], F32, kind="ExternalOutput").ap()
    dbg_d = nc.dram_tensor("dbg", [128, 4096], F32, kind="ExternalOutput").ap() if debug else None

    def scratch(n, s, dt=BF16):
        return nc.dram_tensor(n, s, dt, kind="Internal").ap()

    win_s = scratch("win_s", [19, 128, 4096])
    wada_s = scratch("wada_s", [6, 128, 4096])
    poolw_s = scratch("poolw_s", [1, 128, 4096])
    wpp_s = scratch("wpp_s", [2, 128, 4096])
    wps_s = scratch("wps_s", [4, 128, 4096])
    wout_s = scratch("wout_s", [2, 128, 4096])
    v_s = scratch("v_s", [8, 128, L])
    d_s = scratch("d_s", [8, 128, L])
    hb_s = scratch("hb_s", [NCH * 4, 128, 512])

    P = Prog()
    with contextlib.ExitStack() as st:
        def sb(name, shape, dt=F32):
            return st.enter_context(nc.sbuf_tensor("sb_" + name, shape, dt))

        def ps(name, shape, dt=F32):
            return st.enter_context(nc.psum_tensor("ps_" + name, shape, dt))

        cf = sb("cf", [128, 768]); cf_b = Buf("cf")
        identf = cf[:, 0:128]; onesf = cf[:, 640:768]
        Uf32 = {0: cf[:, 128:256], 1: cf[:, 256:384]}
        SUf32 = {0: cf[:, 384:512], 1: cf[:, 512:640]}
        cb16 = sb("cb16", [128, 128], BF16); cb16_b = Buf("cb16")
        identb = cb16[:, 0:128]
        indb = sb("indb", [128, 1024], BF16); indb_b = Buf("indb")
        nmb = sb("nmb", [128, 1024], BF16); nmb_b = Buf("nmb")
        rinv = sb("rinv", [128, 256]); rinv_b = Buf("rinv")
        cols = sb("cols", [128, 192]); cols_b = Buf("cols")
        rowsb = sb("rowsb", [128, 160]); rowsb_b = Buf("rowsb")
        dtb_bc = rowsb[:, 0:64]; A_bc = rowsb[:, 64:128]; dsk_bc = rowsb[:, 128:160]
        sc3 = sb("sc3", [128, KC * NCOND], BF16); sc3_b = Buf("sc3")
        modT = sb("modT", [128, 24 * NCOND]); modT_b = Buf("modT")
        Acol = sb("Acol", [128, KC * NCOND]); Acol_b = Buf("Acol")
        gn_bc = sb("gn_bc", [128, D]); gn_b = Buf("gn")
        tiny = sb("tiny", [128, 64]); tiny_b = Buf("tiny")

        wring = [sb(f"wr{i}", [128, KC, 512], BF16) for i in range(3)]
        wring_b = [Buf(f"wr{i}") for i in range(3)]
        wr_n = [0]

        xt = [sb(f"xt{i}", [128, D]) for i in range(3)]; xt_b = [Buf(f"xt{i}") for i in range(3)]
        xn = [sb(f"xn{i}", [128, D], BF16) for i in range(2)]; xn_b = [Buf(f"xn{i}") for i in range(2)]
        hT = sb("hT", [128, KC, 512], BF16); hT_b = Buf("hT")
        hTh = sb("hTh", [128, KC, 4], BF16); hTh_b = Buf("hTh")
        ssq = sb("ssq", [128, 8]); ssq_b = Buf("ssq")
        rstd = sb("rstd", [128, 8]); rstd_b = Buf("rstd")

        rawT4 = sb("rawT4", [128, 4, 516], BF16); rawT4_b = Buf("rawT4")
        cacc = [sb("cacc0", [128, 512])] * 2; cacc_b = [Buf("cacc0")] * 2
        xbcT = [sb("xbcT0", [128, 512], BF16)] * 2; xbcT_b = [Buf("xbcT0")] * 2
        BT = sb("BT", [128, 4, 512], BF16); BT_b = Buf("BT")
        CT = sb("CT", [128, 4, 512], BF16); CT_b = Buf("CT")
        Btok = sb("Btok", [128, 4, 512], BF16); Btok_b = Buf("Btok")
        xs_g = [sb(f"xsg{i}", [128, 4, 512], BF16) for i in range(2)]; xs_g_b = [Buf(f"xsg{i}") for i in range(2)]
        zs_g = sb("zsg", [128, 4, 512], BF16); zs_g_b = Buf("zsg")
        dtc = sb("dtc", [128, 4, 64]); dtc_b = Buf("dtc")
        dtA = sb("dtA", [128, 8, 32]); dtA_b = Buf("dtA")
        eall = sb("eall", [128, 8, 96]); eall_b = Buf("eall")
        dts = sb("dts", [128, 8, 32]); dts_b = Buf("dts")

        dtA2a = sb("dtA2a", [128, 4, 128]); dtA2a_b = Buf("dtA2a")
        posAa = sb("posAa", [128, 1024], BF16); posAa_b = Buf("posAa")
        LHSa = sb("LHSa", [128, 1024], BF16); LHSa_b = Buf("LHSa")
        RHS = [sb(f"RHS{i}", [128, 1024], BF16) for i in range(2)]; RHS_b = [Buf(f"RHS{i}") for i in range(2)]
        nhalf = sb("nhalf", [128, 8]); nhalf_b = Buf("nhalf")
        LT = [sb(f"LT{i}", [128, 8, 128], BF16) for i in range(2)]; LT_b = [Buf(f"LT{i}") for i in range(2)]
        MT = [sb("MT0", [128, 8, 128], BF16)] * 2; MT_b = [Buf("MT0")] * 2
        cbm = [sb(f"cbm{i}", [128, 1, 128], BF16) for i in range(2)]; cbm_b = [Buf(f"cbm{i}") for i in range(2)]
        xdt2 = [sb(f"xdt2{i}", [128, 2, 512], BF16) for i in range(2)]; xdt2_b = [Buf(f"xdt2{i}") for i in range(2)]
        xdts = [sb("xdts0", [128, 512], BF16)] * 2; xdts_b = [Buf("xdts0")] * 2
        xsd = [sb("xsd0", [128, 512], BF16)] * 2; xsd_b = [Buf("xsd0")] * 2
        ysb = [sb(f"ysb{i}", [128, 512], BF16) for i in range(2)]; ysb_b = [Buf(f"ysb{i}") for i in range(2)]
        un = [sb(f"un{i}", [128, 512], BF16) for i in range(2)]; un_b = [Buf(f"un{i}") for i in range(2)]
        S = {0: [sb(f"Sf{g}", [128, 512]) for g in range(4)], 1: [sb(f"Sb{g}", [128, 512]) for g in range(4)]}
        S_b = {0: [Buf(f"Sf{g}") for g in range(4)], 1: [Buf(f"Sb{g}") for g in range(4)]}
        Sf16 = [sb("Sf16s", [128, 512], BF16)] * 4; Sf16_b = [Buf("Sf16s")] * 4
        Hb16 = [sb(f"Hb16{i}", [128, 512], BF16) for i in range(2)]; Hb16_b = [Buf(f"Hb16{i}") for i in range(2)]
        osb = sb("osb", [128, D])
        OSB = [Buf("osb0"), Buf("osb1")]
        ub = [osb[:, 0:512], osb[:, 512:1024]]; ub_b = OSB

        pgf = sb("pgf", [128, 6 * 2048]); PG = [Buf(f"pg{i}") for i in range(7)]
        mTt = sb("mTt", [128, 8, 512], BF16)

        def pg16(page, npages=1):
            return pgf[:, page * 2048:(page + npages) * 2048].bitcast(BF16)

        uT = pg16(0, 2).rearrange("p (k t) -> p k t", k=16)
        gT = pg16(2).rearrange("p (k t) -> p k t", k=8)
        zpT = pg16(3).rearrange("p (k t) -> p k t", k=8)
        p1T = zpT
        dTt = pg16(4).rearrange("p (k t) -> p k t", k=8)
        ypT = pg16(5).rearrange("p (k t) -> p k t", k=8)
        mT = mTt

        PA = [ps(f"PA{i}", [128, 512]) for i in range(2)]; PA_b = [Buf(f"PA{i}") for i in range(2)]
        PT = ps("PT", [128, 8, 128], BF16); PT_b = Buf("PT")
        PL = ps("PL", [128, 1024]); PL_b = Buf("PL")
        PY = ps("PY", [128, 512]); PY_b = Buf("PY")
        PZ = ps("PZ", [128, 512]); PZ_b = Buf("PZ")
        PS = ps("PS", [128, 512]); PS_b = Buf("PS")
        PTalt = PL[:, 0:512].bitcast(BF16).rearrange("p (k t) -> p k t", k=8)
        PYs = [PY, PS]; PYs_b = [PY_b, PS_b]
        pa_n = [0]

        pa_wide = [True]

        def next_pa():
            ring = ((PA[0], PA_b[0]), (PA[1], PA_b[1]), (PY, PY_b), (PS, PS_b)) if pa_wide[0] else ((PA[0], PA_b[0]), (PA[1], PA_b[1]))
            i = pa_n[0] % len(ring)
            pa_n[0] += 1
            return ring[i]

        rr = [0]

        def evac_eng():
            rr[0] += 1
            return "act" if rr[0] % 2 else "dve"

        def dma(out, in_, reads=(), writes=(), stream="sp", lazy=False, **kw):
            return P.op(stream, lambda e: e.dma_start(out=out, in_=in_, **kw), reads=reads, writes=writes, dma=True, lazy=lazy)

        def act(out, in_, func, reads, writes, bias=0.0, scale=1.0, accum_out=None):
            if accum_out is None:
                return P.op("act", lambda e: e.activation(out=out, in_=in_, func=func, bias=bias, scale=scale), reads, writes)
            return P.op("act", lambda e: e.activation(out=out, in_=in_, func=func, bias=bias, scale=scale, accum_out=accum_out), reads, writes,
                        noembed=True)

        def tt(eng, out, in0, in1, op, reads, writes):
            return P.op(eng, lambda e: e.tensor_tensor(out=out, in0=in0, in1=in1, op=op), reads, writes)

        def ts(eng, out, in0, s1, s2, op0, op1, reads, writes):
            return P.op(eng, lambda e: e.tensor_scalar(out=out, in0=in0, scalar1=s1, scalar2=s2, op0=op0, op1=op1), reads, writes)

        def stt(out, in0, scalar, in1, op0, op1, reads, writes):
            return P.op("dve", lambda e: e.scalar_tensor_tensor(out=out, in0=in0, scalar=scalar, in1=in1, op0=op0, op1=op1), reads, writes)

        def cp(eng, out, in_, reads, writes):
            if eng == "act":
                return act(out, in_, AF.Copy, reads, writes)
            return P.op(eng, lambda e: e.tensor_copy(out=out, in_=in_), reads, writes)

        def mm(out, lhsT, rhs, start, stop, reads, writes):
            return P.op("pe", lambda e: e.matmul(out, lhsT=lhsT, rhs=rhs, start=start, stop=stop), reads, writes)

        def tr(out, in_, ident, reads, writes):
            return P.op("pe", lambda e: e.transpose(out=out, in_=in_, identity=ident), reads, writes)

        def memset(eng, ap, val, writes):
            return P.op(eng, lambda e: e.memset(ap, val), (), writes)

        def bc_h(ap8, n=8, q=64):
            return ap8.unsqueeze(2).to_broadcast([128, n, q])

        def load_w(scr, blk):
            i = wr_n[0] % 3
            wr_n[0] += 1
            dma(wring[i][:].rearrange("p k c -> p (k c)"), scr[blk], writes=[wring_b[i]])
            return wring[i], wring_b[i]

        dma(cf[:], consts_d, writes=[cf_b])
        dma(cols[:], cols_d, writes=[cols_b])
        dma(rinv[:], rinv_d, writes=[rinv_b])
        dma(rowsb[:], rows_d[:, R_DTB:R_TOT].partition_broadcast(128), writes=[rowsb_b])
        cp("dve", cb16[:], cf[:, 0:128], [cf_b], [cb16_b])
        act(A_bc, A_bc, AF.Exp, [rowsb_b], [rowsb_b])
        ts("dve", A_bc, A_bc, -1.0, None, ALU.mult, ALU.bypass, [rowsb_b], [rowsb_b])
        dma(pgf[:, 0:1024], ind_d, writes=[PG[0]])
        cp("dve", indb[:], pgf[:, 0:1024], [PG[0]], [indb_b])
        dma(pgf[:, 2048:3072], negm_d, writes=[PG[1]])
        cp("dve", nmb[:], pgf[:, 2048:3072], [PG[1]], [nmb_b])
        memset("pool", dtA2a[:], 0.0, [dtA2a_b])
        memset("pool", LHSa[:], 1.0, [LHSa_b])
        memset("pool", nhalf[:], -0.5, [nhalf_b])
        for i in range(2):
            cp("pool", RHS[i][:], indb[:], [indb_b], [RHS_b[i]])
        dma(tiny[:, 0:KC * NCOND], c3_d, writes=[tiny_b])
        act(sc3[:], tiny[:, 0:KC * NCOND], AF.Silu, [tiny_b], [sc3_b])

        cv_n = [0]

        def convert(src, K, N, scr, scale_col0=None):
            nkh = K // 1024
            ncb = (N + 511) // 512
            for kh in range(nkh):
                for cb_ in range(ncb):
                    w = min(512, N - cb_ * 512)
                    blk = kh * ncb + cb_
                    for half in range(2):
                        i = cv_n[0] % 2
                        cv_n[0] += 1
                        stg = pgf[:, i * 2048:(i + 1) * 2048].rearrange("p (k c) -> p k c", k=4)
                        cst = pg16(2 + i)[:, 0:2048].rearrange("p (k c) -> p k c", k=4)
                        r0 = kh * 1024 + half * 512
                        dma(stg[:, :, 0:w], src[r0:r0 + 512, cb_ * 512:cb_ * 512 + w].rearrange("(k p) c -> p k c", p=128),
                            writes=[PG[i]])
                        eng = ("act", "dve", "pool")[cv_n[0] % 3]
                        if scale_col0 is None:
                            cp(eng, cst[:, :, 0:w], stg[:, :, 0:w], [PG[i]], [PG[2 + i]])
                        else:
                            for k in range(4):
                                c0 = scale_col0 + kh * 8 + half * 4 + k
                                ts("dve", cst[:, k, 0:w], stg[:, k, 0:w], cols[:, c0:c0 + 1], None, ALU.mult, ALU.bypass,
                                   [PG[i], cols_b], [PG[2 + i]])
                        dst = scr[blk].rearrange("p (k c) -> p k c", k=8)[:, half * 4:half * 4 + 4, 0:w]
                        dma(dst, cst[:, :, 0:w], reads=[PG[2 + i]], writes=[SCR_B[id(scr)]], lazy=True)

        SCR_B = {}
        for s_ in (win_s, wada_s, poolw_s, wpp_s, wps_s, wout_s):
            SCR_B[id(s_)] = Buf("scr")
        convert(wada_d, D, 3 * D, wada_s)
        convert(win_d, D, NCOL, win_s)
        convert(poolw_d, D, 256, poolw_s)
        convert(wpp_d, D, D, wpp_s)
        convert(wps_d, 2 * D, D, wps_s, scale_col0=C_SSDN)
        convert(wout_d, D, D, wout_s)
        W_B = lambda scr: SCR_B[id(scr)]

        def load_wb(scr, blk, ncols=512):
            i = wr_n[0] % 3
            wr_n[0] += 1
            if ncols == 512:
                dma(wring[i][:].rearrange("p k c -> p (k c)"), scr[blk], reads=[W_B(scr)], writes=[wring_b[i]])
            else:
                dma(wring[i][:, :, 0:ncols], scr[blk].rearrange("p (k c) -> p k c", k=8)[:, :, 0:ncols], reads=[W_B(scr)],
                    writes=[wring_b[i]])
            return wring[i], wring_b[i]

        for blk in range(6):
            w, wb = load_wb(wada_s, blk)
            for cc in range(4):
                j = blk * 4 + cc
                for kc in range(KC):
                    mm(PL[:, j * 4:j * 4 + NCOND], w[:, kc, cc * 128:(cc + 1) * 128], sc3[:, kc * NCOND:(kc + 1) * NCOND],
                       kc == 0, kc == KC - 1, [wb, sc3_b], [PL_b])
        tt("dve", modT[:].rearrange("p (j i) -> p j i", i=NCOND), PL[:, 0:96].rearrange("p (j i) -> p j i", i=4)[:, :, 0:NCOND],
           cols[:, C_BADA:C_BADA + 24].unsqueeze(2).to_broadcast([128, 24, NCOND]), ALU.add, [PL_b, cols_b], [modT_b])
        mod3 = modT[:].rearrange("p (j i) -> p j i", i=NCOND)
        ts("dve", Acol[:].rearrange("p (k i) -> p k i", i=NCOND), mod3[:, 8:16, :], 1.0, None, ALU.add, ALU.bypass, [modT_b], [Acol_b])
        tt("dve", Acol[:].rearrange("p (k i) -> p k i", i=NCOND), Acol[:].rearrange("p (k i) -> p k i", i=NCOND),
           cols[:, C_NPRE:C_NPRE + 8].unsqueeze(2).to_broadcast([128, 8, NCOND]), ALU.mult, [Acol_b, cols_b], [Acol_b])
        Acol3 = Acol[:].rearrange("p (k i) -> p k i", i=NCOND)

        def make_gn(b):
            dma(osb[:], rows_d[:, R_NPOST:R_NPOST + D].partition_broadcast(128), writes=[*OSB])
            for half in range(2):
                pa, pab = next_pa()
                for k4 in range(4):
                    kc = half * 4 + k4
                    dg = xt[0][:, k4 * 128:(k4 + 1) * 128]
                    ts("dve", dg, identf, mod3[:, 16 + kc, b:b + 1], None, ALU.mult, ALU.bypass, [cf_b, modT_b], [xt_b[0]])
                    mm(pa[:, k4 * 128:(k4 + 1) * 128], onesf, dg, True, True, [cf_b, xt_b[0]], [pab])
                tt("dve", gn_bc[:, half * 512:(half + 1) * 512], pa[:], osb[:, half * 512:(half + 1) * 512], ALU.mult,
                   [pab, *OSB], [gn_b])

        def front(src, t0, ntok, seqlen, ci):
            nsub = ntok // 128
            xts = []
            for s in range(nsub + 1):
                i = s % 3
                if s < nsub:
                    dma(xt[i][:], src[t0 + s * 128:t0 + (s + 1) * 128, :], writes=[xt_b[i]])
                    npart = 128
                else:
                    lo = max(t0 - 2, 0)
                    hi = min(t0 + ntok, seqlen - 1)
                    dma(xt[i][0:2, :], src[lo:lo + 2, :], writes=[xt_b[i]])
                    dma(xt[i][2:3, :], src[hi:hi + 1, :], writes=[xt_b[i]])
                    dma(xt[i][3:4, :], src[hi:hi + 1, :], writes=[xt_b[i]])
                    npart = 4
                j = s % 2
                act(xn[j][0:npart, :], xt[i][0:npart, :], AF.Square, [xt_b[i]], [xn_b[j], ssq_b], accum_out=ssq[0:npart, s:s + 1])
                ts("pool", rstd[0:npart, s:s + 1], ssq[0:npart, s:s + 1], 1.0 / D, EPS, ALU.mult, ALU.add, [ssq_b], [rstd_b])
                tt("pool", rstd[0:npart, s:s + 1], rstd[0:npart, s:s + 1], nhalf[0:npart, 0:1], ALU.pow, [rstd_b, nhalf_b], [rstd_b])
                ts("pool", xn[j][0:npart, :], xt[i][0:npart, :], rstd[0:npart, s:s + 1], 1.0, ALU.mult, ALU.mult,
                   [xt_b[i], rstd_b], [xn_b[j]])
                ptx, ptxb = (PT, PT_b) if s % 2 == 0 else (PTalt, PL_b)
                for kc in range(KC):
                    tr(ptx[:, kc, 0:npart], xn[j][0:npart, kc * 128:(kc + 1) * 128], identb[0:npart, 0:npart], [xn_b[j], cb16_b], [ptxb])
                for kc in range(KC):
                    if s < nsub:
                        o = hT[:, kc, s * 128:(s + 1) * 128]; ob = hT_b
                    else:
                        o = hTh[:, kc, 0:4]; ob = hTh_b
                    a_ap = Acol3[:, kc, ci:ci + 1]
                    s_ap = mod3[:, kc, ci:ci + 1]
                    if evac_eng() == "act":
                        act(o, ptx[:, kc, 0:npart], AF.Identity, [ptxb, Acol_b, modT_b], [ob], bias=s_ap, scale=a_ap)
                    else:
                        ts("dve", o, ptx[:, kc, 0:npart], a_ap, s_ap, ALU.mult, ALU.add, [ptxb, Acol_b, modT_b], [ob])
            if t0 == 0:
                memset("pool", hTh[:, :, 0:2], 0.0, [hTh_b])
            if t0 + ntok >= seqlen:
                memset("pool", hTh[:, :, 2:4], 0.0, [hTh_b])

        def proj_feat(w, wb, cc, ntok, halo_idx=None):
            pa, pab = next_pa()
            for kc in range(KC):
                mm(pa[:, 0:ntok], w[:, kc, cc * 128:(cc + 1) * 128], hT[:, kc, 0:ntok], kc == 0, kc == KC - 1, [wb, hT_b], [pab])
                if halo_idx is not None:
                    mm(PZ[:, halo_idx * 4:halo_idx * 4 + 4], w[:, kc, cc * 128:(cc + 1) * 128], hTh[:, kc, 0:4], kc == 0, kc == KC - 1,
                       [wb, hTh_b], [PZ_b])
            return pa, pab

        cn = [0]

        def xbc_block(blk, ntok, sink):
            w, wb = load_wb(win_s, blk)
            pas = []
            for cc in range(4):
                pa, pab = proj_feat(w, wb, cc, ntok, halo_idx=cc)
                cp(evac_eng(), rawT4[:, cc, 2:2 + ntok], pa[:, 0:ntok], [pab], [rawT4_b])
            PSh = PZ[:, 0:16].rearrange("p (c f) -> p c f", f=4)
            cp("dve", rawT4[:, :, 0:2], PSh[:, :, 0:2], [PZ_b], [rawT4_b])
            cp("dve", rawT4[:, :, 2 + ntok:3 + ntok], PSh[:, :, 2:3], [PZ_b], [rawT4_b])
            for cc in range(4):
                gcc = (blk - 12) * 4 + cc
                i = cn[0] % 2
                cn[0] += 1
                wcol = lambda k: cols[:, C_CONVW + gcc * 4 + k:C_CONVW + gcc * 4 + k + 1]
                ts("dve", cacc[i][:, 0:ntok], rawT4[:, cc, 0:ntok], wcol(0), None, ALU.mult, ALU.bypass, [rawT4_b, cols_b], [cacc_b[i]])
                for k in range(1, 4):
                    stt(cacc[i][:, 0:ntok], rawT4[:, cc, k:k + ntok], wcol(k), cacc[i][:, 0:ntok], ALU.mult, ALU.add,
                        [rawT4_b, cols_b, cacc_b[i]], [cacc_b[i]])
                sink(cc, cacc[i], cacc_b[i], cols[:, C_CONVB + gcc:C_CONVB + gcc + 1])

        def dt_block(ntok):
            w, wb = load_wb(win_s, BLK_DT, 64)
            nchk = ntok // 128
            for c in range(nchk):
                pa, pab = next_pa()
                for kc in range(KC):
                    mm(pa[:, 0:64], hT[:, kc, c * 128:(c + 1) * 128], w[:, kc, 0:64], kc == 0, kc == KC - 1, [wb, hT_b], [pab])
                tt("dve", dtc[:, c, :], pa[:, 0:64], dtb_bc, ALU.add, [pab, rowsb_b], [dtc_b])
            act(dtc[:, 0:nchk, :], dtc[:, 0:nchk, :], AF.Exp, [dtc_b], [dtc_b])
            act(dtc[:, 0:nchk, :], dtc[:, 0:nchk, :], AF.Ln, [dtc_b], [dtc_b], bias=1.0)

        def cd_prep(c, d):
            cd = c * 2 + d
            tt("dve", dtA[:, cd, :], dtc[:, c, d * 32:(d + 1) * 32], A_bc[:, d * 32:(d + 1) * 32], ALU.mult, [dtc_b, rowsb_b], [dtA_b])
            pa, pab = next_pa()
            mm(pa[:, 0:32], Uf32[d], dtA[:, cd, :], True, True, [cf_b, dtA_b], [pab])
            mm(pa[:, 32:64], onesf, dtA[:, cd, :], True, True, [cf_b, dtA_b], [pab])
            mm(pa[:, 64:96], SUf32[d], dtA[:, cd, :], True, True, [cf_b, dtA_b], [pab])
            act(eall[:, cd, :], pa[:, 0:96], AF.Exp, [pab], [eall_b])
            tt("dve", dts[:, cd, :], dtc[:, c, d * 32:(d + 1) * 32], eall[:, cd, 64:96], ALU.mult, [dtc_b, eall_b], [dts_b])

        un_ = [0]
        ch_n = [0]

        def state_unit(g, c, d, xs, xsb, store_ci=None):
            cd = c * 2 + d
            i = un_[0] % 2
            un_[0] += 1
            tt("pool", xdts[i][:].rearrange("p (h q) -> p h q", h=8), xs[:, c, :].rearrange("p (h q) -> p h q", h=8),
               bc_h(dts[:, cd, g * 8:(g + 1) * 8]), ALU.mult, [xsb, dts_b], [xdts_b[i]])
            pa, pab = next_pa()
            mm(pa[:], Btok[:, c, g * 128:(g + 1) * 128], xdts[i][:], True, True, [Btok_b, xdts_b[i]], [pab])
            Sg, Sgb = S[d][g], S_b[d][g]
            if store_ci is not None:
                j = un_[0] % 2
                cp("act", Hb16[j][:], Sg[:], [Sgb], [Hb16_b[j]])
                dma(hb_s[store_ci * 4 + g], Hb16[j][:], reads=[Hb16_b[j]], writes=[HB_B[store_ci * 4 + g]], lazy=True)
            tt("pool", Sg[:].rearrange("p (h q) -> p h q", h=8), Sg[:].rearrange("p (h q) -> p h q", h=8),
               bc_h(eall[:, cd, 32 + g * 8:32 + (g + 1) * 8]), ALU.mult, [Sgb, eall_b], [Sgb])
            tt("dve", Sg[:], Sg[:], pa[:], ALU.add, [Sgb, pab], [Sgb])

        HB_B = [Buf(f"hb{i}") for i in range(NCH * 4)]
        V_B = [Buf(f"v{i}") for i in range(8)]
        D_B = [Buf(f"d{i}") for i in range(8)]

        def xs_sink_factory(slot, ntok):
            xsl, xslb = xs_g[slot], xs_g_b[slot]

            def sink(cc, acc, accb, bias):
                j = cn[0] % 2
                act(xbcT[j][:, 0:ntok], acc[:, 0:ntok], AF.Silu, [accb, cols_b], [xbcT_b[j]], bias=bias)
                for c in range(ntok // 128):
                    tr(PT[:, c, :], xbcT[j][:, c * 128:(c + 1) * 128], identb, [xbcT_b[j], cb16_b], [PT_b])
                nchk = ntok // 128
                cp(evac_eng(), xsl[:, 0:nchk, cc * 128:(cc + 1) * 128], PT[:, 0:nchk, :], [PT_b], [xslb])
            return sink

        def b_sink_factory(ntok):
            def sink(cc, acc, accb, bias):
                act(BT[:, cc, 0:ntok], acc[:, 0:ntok], AF.Silu, [accb, cols_b], [BT_b], bias=bias)
                for c in range(ntok // 128):
                    tr(PT[:, c, :], BT[:, cc, c * 128:(c + 1) * 128], identb, [BT_b, cb16_b], [PT_b])
                nchk = ntok // 128
                cp(evac_eng(), Btok[:, 0:nchk, cc * 128:(cc + 1) * 128], PT[:, 0:nchk, :], [PT_b], [Btok_b])
            return sink

        def c_sink_factory(ntok):
            def sink(cc, acc, accb, bias):
                act(CT[:, cc, 0:ntok], acc[:, 0:ntok], AF.Silu, [accb, cols_b], [CT_b], bias=bias)
            return sink

        def dbg_dump(ap, ncols, rows=128, col0=0):
            if debug:
                cp("dve", osb[0:rows, 0:ncols], ap, [], [*OSB])
                dma(dbg_d[0:rows, col0:col0 + ncols], osb[0:rows, 0:ncols], reads=[*OSB], writes=[DBG_B])
        DBG_B = Buf("dbg")

        def context(b):
            for d in range(2):
                for g in range(4):
                    memset("pool", S[d][g][:], 0.0, [S_b[d][g]])
            front(ctx_d[b], 0, LC, LC, NB)
            xbc_block(BLK_B, LC, b_sink_factory(LC))
            dt_block(LC)
            for c in range(2):
                for d in range(2):
                    cd_prep(c, d)
            for g in range(4):
                slot = g % 2
                xbc_block(BLK_XS[g], LC, xs_sink_factory(slot, LC))
                for c in (0, 1):
                    state_unit(g, c, 0, xs_g[slot], xs_g_b[slot])
                for c in (1, 0):
                    state_unit(g, c, 1, xs_g[slot], xs_g_b[slot])

        def sweep1(b):
            for t in range(NT - 1, -1, -1):
                t0 = t * 512
                front(x_d[b], t0, 512, L, b)
                for bi, blk in enumerate(BLK_V):
                    w, wb = load_wb(win_s, blk)
                    for cc in range(4):
                        pa, pab = proj_feat(w, wb, cc, 512)
                        cp(evac_eng(), dTt[:, bi * 4 + cc, :], pa[:], [pab], [PG[4]])
                for j in range(8):
                    dma(v_s[j][:, t0:t0 + 512], dTt[:, j, :], reads=[PG[4]], writes=[V_B[j]], lazy=True)
                xbc_block(BLK_B, 512, b_sink_factory(512))
                dt_block(512)
                for c in range(4):
                    cd_prep(c, 1)
                xbc_block(BLK_XS[0], 512, xs_sink_factory(0, 512))
                for g in range(4):
                    slot = g % 2
                    if g + 1 < 4:
                        xbc_block(BLK_XS[g + 1], 512, xs_sink_factory((g + 1) % 2, 512))
                    for c in (3, 2, 1, 0):
                        state_unit(g, c, 1, xs_g[slot], xs_g_b[slot], store_ci=t * 4 + c)

        def pool_phase(b):
            R = L // GW
            PADN = 8
            vin = pg16(0)[:, 0:L].rearrange("p (r c) -> p r c", c=GW)
            bufs = [(pgf[:, 2048:2048 + 5120], [PG[1], PG[2], PG[3]]), (pgf[:, 2048 + 5120:2048 + 10240], [PG[3], PG[4], PG[5]])]

            def rview(i):
                return bufs[i][0][:, 0:(R + 2 * PADN) * GW].rearrange("p (r c) -> p r c", c=GW)

            def cview(i):
                return bufs[i][0][:, 0:R * (GW + 2 * PADN)].rearrange("p (r c) -> p r c", c=GW + 2 * PADN)

            eng = "dve"

            def step(axis, n, a, bsh, src, srcb, dst, dstb):
                def sl(ap, lo, hi):
                    return ap[:, lo:hi, :] if axis == 0 else ap[:, :, lo:hi]
                tt(eng, sl(dst, a, n - bsh), sl(src, 0, n - bsh - a), sl(src, a + bsh, n), ALU.add, srcb, dstb)
                if a > 0:
                    cp(eng, sl(dst, 0, a), sl(src, bsh, a + bsh), srcb, dstb)
                if bsh > 0:
                    cp(eng, sl(dst, n - bsh, n), sl(src, n - bsh - a, n - a), srcb, dstb)

            def run_steps(axis, n, k, cur, view):
                wdt = 1
                while wdt < k:
                    a, bsh = (1, 0) if wdt == 1 else (wdt // 2, wdt // 2)
                    step(axis, n, a, bsh, view(cur), bufs[cur][1], view(1 - cur), bufs[1 - cur][1])
                    cur = 1 - cur
                    wdt *= 2
                return cur

            for j in range(8):
                gi = j // 2
                k = POOL_WINDOWS[gi]
                dma(pg16(0)[:, 0:L], v_s[j], reads=[V_B[j]], writes=[PG[0]])
                A0 = rview(0)
                memset("pool", A0[:, 0:PADN, :], 0.0, bufs[0][1])
                memset("pool", A0[:, R + PADN:R + 2 * PADN, :], 0.0, bufs[0][1])
                cp(eng, A0[:, PADN:R + PADN, :], vin, [PG[0]], bufs[0][1])
                cur = run_steps(0, R + 2 * PADN, k, 0, rview)
                oth = 1 - cur
                Cv = cview(oth)
                memset("pool", Cv[:, :, 0:PADN], 0.0, bufs[oth][1])
                memset("pool", Cv[:, :, GW + PADN:GW + 2 * PADN], 0.0, bufs[oth][1])
                tt(eng, Cv[:, :, PADN:GW + PADN], rview(cur)[:, PADN:R + PADN, :],
                   rinvR[:, gi * 64:gi * 64 + R].unsqueeze(2).to_broadcast([128, R, GW]), ALU.mult,
                   bufs[cur][1] + [rinvR_b], bufs[oth][1])
                cur = run_steps(1, GW + 2 * PADN, k, oth, cview)
                oth = 1 - cur
                tmp = bufs[oth][0][:, 0:L].rearrange("p (r c) -> p r c", c=GW)
                tt(eng, tmp, cview(cur)[:, :, PADN:GW + PADN], rinv[:, gi * 64:gi * 64 + GW].unsqueeze(1).to_broadcast([128, R, GW]),
                   ALU.mult, bufs[cur][1] + [rinv_b], bufs[oth][1])
                tt(eng, vin, tmp, vin, ALU.subtract, bufs[oth][1] + [PG[0]], [PG[0]])
                dma(d_s[j], pg16(0)[:, 0:L], reads=[PG[0]], writes=[D_B[j]], lazy=True)

        rinvR = sb("rinvR", [128, 256]); rinvR_b = Buf("rinvR")
        rinvR_d = din("rinvR", [128, 256])
        dma(rinvR[:], rinvR_d, writes=[rinvR_b])

        def ssd_front(g):
            slot = g % 2
            xbc_block(BLK_XS[g], 512, xs_sink_factory(slot, 512))

        def ssd_group(g, t, b, prefetch=None):
            slot = g % 2
            xs, xsb = xs_g[slot], xs_g_b[slot]
            w, wb = load_wb(win_s, BLK_ZS[g])
            for c in range(4):
                pa, pab = next_pa()
                for kc in range(KC):
                    mm(pa[:], hT[:, kc, c * 128:(c + 1) * 128], w[:, kc, :], kc == 0, kc == KC - 1, [wb, hT_b], [pab])
                act(zs_g[:, c, :], pa[:], AF.Silu, [pab], [zs_g_b])
            for hb in range(2):
                cp("pool", dtA2a[:].rearrange("p c (a q) -> p c a q", q=32)[:, :, :, 0:8],
                   dtA[:, hb * 4:(hb + 1) * 4, g * 8:(g + 1) * 8].unsqueeze(2).to_broadcast([128, 4, 4, 8]), [dtA_b], [dtA2a_b])
                for q4 in range(4):
                    cd = hb * 4 + q4
                    mm(PL[:, cd * 128:(cd + 1) * 128], dtA2a[:, q4, :], Uf32[cd % 2], True, True, [dtA2a_b, cf_b], [PL_b])
            act(posAa[:], PL[:], AF.Copy, [PL_b], [posAa_b])
            for r0 in (32, 96):
                stt(posAa[r0:r0 + 32, :], PL[r0:r0 + 32, :], 1.0, posAa[r0:r0 + 32, :], ALU.mult, ALU.subtract, [PL_b, posAa_b], [posAa_b])
            ts("pool", LHSa[0:64, :], posAa[0:64, :], -1.0, 1.0, ALU.mult, ALU.mult, [posAa_b], [LHSa_b])
            cp("act", Sf16[g][:], S[0][g][:], [S_b[0][g]], [Sf16_b[g]])
            if prefetch is not None:
                prefetch()

            units = [(c, d) for c in range(4) for d in range(2)]

            def phaseA(n):
                c, d = units[n]
                cd = c * 2 + d
                i = n % 2
                tt("pool", RHS[i][64:128, :].rearrange("p (h t) -> p h t", h=8), indb[64:128, :].rearrange("p (h t) -> p h t", h=8),
                   posAa[64:128, cd * 128:(cd + 1) * 128].unsqueeze(1).to_broadcast([64, 8, 128]), ALU.mult, [indb_b, posAa_b], [RHS_b[i]])
                for half in range(2):
                    mm(PL[:, half * 512:(half + 1) * 512], LHSa[:, cd * 128:(cd + 1) * 128], RHS[i][:, half * 512:(half + 1) * 512], True, False,
                       [LHSa_b, RHS_b[i]], [PL_b])
                    mm(PL[:, half * 512:(half + 1) * 512], identb, nmb[:, d * 512:(d + 1) * 512], False, True, [cb16_b, nmb_b], [PL_b])
                act(LT[i][:].rearrange("p h t -> p (h t)"), PL[:], AF.Exp, [PL_b], [LT_b[i]])

            def pre(c):
                ci = t * 4 + c
                i2 = c % 2
                pa, pab = next_pa()
                mm(pa[:, 0:128], BT[:, g, c * 128:(c + 1) * 128], CT[:, g, c * 128:(c + 1) * 128], True, True, [BT_b, CT_b], [pab])
                cp("dve", cbm[i2][:, 0, :], pa[:, 0:128], [pab], [cbm_b[i2]])
                tt("pool", xsd[i2][:].rearrange("p (h q) -> p h q", h=8), xs[:, c, :].rearrange("p (h q) -> p h q", h=8),
                   bc_h(dsk_bc[:, g * 8:(g + 1) * 8]), ALU.mult, [xsb, rowsb_b], [xsd_b[i2]])
                mm(PYs[i2][:], identb, xsd[i2][:], True, False, [cb16_b, xsd_b[i2]], [PYs_b[i2]])
                dma(Hb16[i2][:], hb_s[ci * 4 + g], reads=[HB_B[ci * 4 + g]], writes=[Hb16_b[i2]])
                tt("pool", xdt2[i2][:].rearrange("p d (h q) -> p d h q", h=8),
                   xs[:, c, :].rearrange("p (h q) -> p h q", h=8).unsqueeze(1).to_broadcast([128, 2, 8, 64]),
                   dtc[:, c, :].rearrange("p (d h) -> p d h", d=2)[:, :, g * 8:(g + 1) * 8].unsqueeze(3).to_broadcast([128, 2, 8, 64]),
                   ALU.mult, [xsb, dtc_b], [xdt2_b[i2]])

            def phaseB(n):
                c, d = units[n]
                cd = c * 2 + d
                i = n % 2
                i2 = c % 2
                tt("dve", MT[i][:], LT[i][:], cbm[i2][:, 0, :].unsqueeze(1).to_broadcast([128, 8, 128]), ALU.mult, [LT_b[i], cbm_b[i2]], [MT_b[i]])
                for h in range(8):
                    mm(PYs[i2][:, h * 64:(h + 1) * 64], MT[i][:, h, :], xdt2[i2][:, d, h * 64:(h + 1) * 64], False, (d == 1 and h == 7),
                       [MT_b[i], xdt2_b[i2]], [PYs_b[i2]])
                if d == 0:
                    mm(PZ[:], CT[:, g, c * 128:(c + 1) * 128], Sf16[g][:], True, True, [CT_b, Sf16_b[g]], [PZ_b])
                    dst, dstb = ub[i2], ub_b[i2]
                else:
                    mm(PZ[:], CT[:, g, c * 128:(c + 1) * 128], Hb16[i2][:], True, True, [CT_b, Hb16_b[i2]], [PZ_b])
                    dst, dstb = ysb[i2][:], ysb_b[i2]
                tt("dve", dst.rearrange("p (h q) -> p h q", h=8), PZ[:].rearrange("p (h q) -> p h q", h=8),
                   bc_h(eall[:, cd, g * 8:(g + 1) * 8]), ALU.mult, [PZ_b, eall_b], [dstb])
                if d == 0:
                    state_unit(g, c, 0, xs, xsb)
                    cp("act", Sf16[g][:], S[0][g][:], [S_b[0][g]], [Sf16_b[g]])

            def post(c):
                i2 = c % 2
                u_, ubb = ub[i2], ub_b[i2]
                t0c = 16 + 2 * i2
                tt("pool", u_, u_, ysb[i2][:], ALU.add, [ubb, ysb_b[i2]], [ubb])
                tt("dve", u_, u_, PYs[i2][:], ALU.add, [ubb, PYs_b[i2]], [ubb])
                tt("pool", u_, u_, zs_g[:, c, :], ALU.mult, [ubb, zs_g_b], [ubb])
                act(un[i2][:], u_, AF.Square, [ubb], [un_b[i2], tiny_b], accum_out=tiny[:, t0c:t0c + 1])
                ts("pool", tiny[:, t0c + 1:t0c + 2], tiny[:, t0c:t0c + 1], 1.0 / 512, EPS, ALU.mult, ALU.add, [tiny_b], [tiny_b])
                tt("pool", tiny[:, t0c + 1:t0c + 2], tiny[:, t0c + 1:t0c + 2], nhalf[:, 0:1], ALU.pow, [tiny_b, nhalf_b], [tiny_b])
                ts("dve", un[i2][:], u_, tiny[:, t0c + 1:t0c + 2], None, ALU.mult, ALU.bypass, [ubb, tiny_b], [un_b[i2]])
                for q in range(4):
                    tr(PT[:, q, :], un[i2][:, q * 128:(q + 1) * 128], identb, [un_b[i2], cb16_b], [PT_b])
                cp("act", uT[:, g * 4:(g + 1) * 4, c * 128:(c + 1) * 128], PT[:, 0:4, :], [PT_b], [PG[0], PG[1]])

            pre(0)
            phaseA(0)
            pending = None
            for n in range(8):
                if n + 1 < 8:
                    phaseA(n + 1)
                phaseB(n)
                c, d = units[n]
                if d == 0 and pending is not None:
                    post(pending)
                    pending = None
                if d == 1:
                    if c + 1 < 4:
                        pre(c + 1)
                    pending = c
            post(pending)

        class _Stop(Exception):
            pass

        def sweep2(b):
            make_gn(b)
            if stop == "gn":
                raise _Stop()
            for t in range(NT):
                t0 = t * 512
                front(x_d[b], t0, 512, L, b)
                xbc_block(BLK_B, 512, b_sink_factory(512))
                xbc_block(BLK_C, 512, c_sink_factory(512))
                dt_block(512)
                for c in range(4):
                    for d in range(2):
                        cd_prep(c, d)
                if stop == "front2":
                    raise _Stop()
                ssd_front(0)
                pa_wide[0] = False
                for g in range(4):
                    ssd_group(g, t, b, prefetch=(lambda g=g: ssd_front(g + 1)) if g + 1 < 4 else None)
                    if stop == "ssd1":
                        raise _Stop()
                pa_wide[0] = True
                if stop == "ssd":
                    raise _Stop()
                for bi, blk in enumerate(BLK_ZP):
                    w, wb = load_wb(win_s, blk)
                    for cc in range(4):
                        pa, pab = proj_feat(w, wb, cc, 512)
                        act(zpT[:, bi * 4 + cc, :], pa[:], AF.Silu, [pab], [PG[3]])
                for j in range(8):
                    dma(dTt[:, j, :], d_s[j][:, t0:t0 + 512], reads=[D_B[j]], writes=[PG[4]])
                w, wb = load_wb(poolw_s, 0, 256)
                pw = w
                for gi in range(4):
                    for oc in range(2):
                        pa, pab = next_pa()
                        for kc in range(2):
                            mm(pa[:], pw[:, gi * 2 + kc, oc * 128:(oc + 1) * 128], dTt[:, gi * 2 + kc, :], kc == 0, kc == 1, [wb, PG[4]], [pab])
                        j = gi * 2 + oc
                        stt(ypT[:, j, :], pa[:], cols[:, C_PSCALE + j:C_PSCALE + j + 1], zpT[:, j, :], ALU.mult, ALU.mult,
                            [pab, cols_b, PG[3]], [PG[5]])
                if stop == "tail1":
                    raise _Stop()
                for bi in range(2):
                    w, wb = load_wb(win_s, BLK_G[bi])
                    for cc in range(4):
                        pa, pab = proj_feat(w, wb, cc, 512)
                        j = bi * 4 + cc
                        act(gT[:, j, :], pa[:], AF.Sigmoid, [pab, cols_b], [PG[2]], bias=cols[:, C_BMERGE + j:C_BMERGE + j + 1])
                for cb_ in range(2):
                    w, wb = load_wb(wpp_s, cb_)
                    for cc in range(4):
                        pa, pab = next_pa()
                        for kc in range(KC):
                            mm(pa[:], w[:, kc, cc * 128:(cc + 1) * 128], ypT[:, kc, :], kc == 0, kc == KC - 1, [wb, PG[5]], [pab])
                        j = cb_ * 4 + cc
                        tt("dve", p1T[:, j, :], pa[:], gT[:, j, :], ALU.mult, [pab, PG[2]], [PG[3]])
                for bi in range(2):
                    w, wb = load_wb(win_s, BLK_G[2 + bi])
                    for cc in range(4):
                        pa, pab = proj_feat(w, wb, cc, 512)
                        j = bi * 4 + cc
                        act(gT[:, j, :], pa[:], AF.Sigmoid, [pab, cols_b], [PG[2]], bias=cols[:, C_BMERGE + 8 + j:C_BMERGE + 8 + j + 1])
                for cb_ in range(2):
                    w0, wb0 = load_wb(wps_s, 0 * 2 + cb_)
                    w1, wb1 = load_wb(wps_s, 1 * 2 + cb_)
                    for cc in range(4):
                        pa, pab = next_pa()
                        for kc in range(16):
                            w, wb = (w0, wb0) if kc < 8 else (w1, wb1)
                            mm(pa[:], w[:, kc % 8, cc * 128:(cc + 1) * 128], uT[:, kc, :], kc == 0, kc == 15, [wb, PG[0], PG[1]], [pab])
                        j = cb_ * 4 + cc
                        tt("dve", mT[:, j, :], pa[:], gT[:, j, :], ALU.mult, [pab, PG[2]], [PG[6]])
                        tt("pool", mT[:, j, :], mT[:, j, :], p1T[:, j, :], ALU.add, [PG[6], PG[3]], [PG[6]])
                if stop == "tail2":
                    raise _Stop()
                w0, wb0 = load_wb(wout_s, 0)
                w1, wb1 = load_wb(wout_s, 1)
                for s in range(4):
                    for cb_ in range(2):
                        w, wb = (w0, wb0) if cb_ == 0 else (w1, wb1)
                        pa, pab = next_pa()
                        for kc in range(KC):
                            mm(pa[:], mT[:, kc, s * 128:(s + 1) * 128], w[:, kc, :], kc == 0, kc == KC - 1, [wb, PG[6]], [pab])
                        cp(evac_eng(), osb[:, cb_ * 512:(cb_ + 1) * 512], pa[:], [pab], [*OSB])
                    act(xn[0][:], osb[:], AF.Square, [*OSB], [xn_b[0], tiny_b], accum_out=tiny[:, 6:7])
                    ts("pool", tiny[:, 7:8], tiny[:, 6:7], 1.0 / D, EPS, ALU.mult, ALU.add, [tiny_b], [tiny_b])
                    tt("pool", tiny[:, 7:8], tiny[:, 7:8], nhalf[:, 0:1], ALU.pow, [tiny_b, nhalf_b], [tiny_b])
                    i = s % 3
                    dma(xt[i][:], x_d[b][t0 + s * 128:t0 + (s + 1) * 128, :], writes=[xt_b[i]])
                    ts("dve", osb[:], osb[:], tiny[:, 7:8], None, ALU.mult, ALU.bypass, [*OSB, tiny_b], [*OSB])
                    tt("pool", osb[:], osb[:], gn_bc[:], ALU.mult, [*OSB, gn_b], [*OSB])
                    tt("dve", osb[:], osb[:], xt[i][:], ALU.add, [*OSB, xt_b[i]], [*OSB])
                    if stop == "o3":
                        raise _Stop()
                    if stop == "outx":
                        dma(out_d[b][t0 + s * 128:t0 + (s + 1) * 128, :], xt[i][:], reads=[xt_b[i], *OSB], writes=[OUT_B])
                    else:
                        dma(out_d[b][t0 + s * 128:t0 + (s + 1) * 128, :], osb[:], reads=[*OSB], writes=[OUT_B], lazy=True)

        OUT_B = Buf("out")

        for b in range(NB):
            if stop == "setup":
                break
            context(b)
            if stop == "context":
                break
            sweep1(b)
            if stop == "sweep1":
                break
            pool_phase(b)
            if stop == "pool":
                break
            try:
                sweep2(b)
            except _Stop:
                break

        P.final_wait_all("sp")
        P.emit(nc)
    return nc


def _consts(L):
    t = np.arange(128)
    ident = np.eye(128, dtype=np.float32)
    Uf = (t[:, None] <= t[None, :]).astype(np.float32)
    Ub = (t[:, None] >= t[None, :]).astype(np.float32)
    SUf = (t[:, None] > t[None, :]).astype(np.float32)
    SUb = (t[:, None] < t[None, :]).astype(np.float32)
    ones = np.ones((128, 128), np.float32)
    consts = np.concatenate([ident, Uf, Ub, SUf, SUb, ones], axis=1)
    ind = np.zeros((128, 8, 128), np.float32)
    for h in range(8):
        for grp in range(4):
            ind[32 * grp + h, h, :] = 1.0
    ind = ind.reshape(128, 1024)
    BIG = 30000.0
    nm0 = np.tile(-BIG * (t[None, :] < t[:, None]).astype(np.float32), (1, 4))
    nm1 = np.tile(-BIG * (t[None, :] > t[:, None]).astype(np.float32), (1, 4))
    negm = np.concatenate([nm0, nm1], axis=1).astype(np.float32)

    def inv_counts(n):
        out = np.ones((4, 64), np.float32)
        for gi, k in enumerate(POOL_WINDOWS):
            lo, hi = k // 2, k - 1 - k // 2
            i = np.arange(n)
            cnt = np.minimum(i + hi + 1, n) - np.maximum(i - lo, 0)
            out[gi, :n] = 1.0 / cnt
        return np.broadcast_to(out.reshape(1, 256), (128, 256)).astype(np.float32).copy()

    return consts, ind, negm, inv_counts(GW), inv_counts(L // GW)


def _in_maps(inputs, n_cores, NB, L):
    f = lambda a: np.ascontiguousarray(np.asarray(a, dtype=np.float32))
    x = f(inputs["x"]); c = f(inputs["c"]); ctx = f(inputs["ctx"]); c_ctx = f(inputs["c_ctx"])
    consts, ind, negm, rinv, rinvR = _consts(L)
    colv = lambda v: f(v).reshape(-1, 128).T
    conv_w = f(inputs["conv_w"])[0]
    cw = conv_w.reshape(4, 24, 128).transpose(2, 1, 0).reshape(128, 96)
    cols = np.concatenate([
        colv(inputs["norm_pre"][0]), colv(inputs["b_ada"][0]), colv(inputs["b_merge"][0]), colv(inputs["pool_scale"][0]),
        colv(inputs["conv_b"][0]), cw, colv(inputs["ssd_norm"][0])], axis=1)
    assert cols.shape == (128, 192)
    rows = np.concatenate([f(inputs["norm_post"][0]), f(inputs["dt_bias"][0]).reshape(-1), f(inputs["a_log"][0]).reshape(-1),
                           f(inputs["d_skip"][0])]).reshape(1, R_TOT)
    shared = {
        "w_ada": f(inputs["w_ada"][0]), "w_in": f(inputs["w_in"][0]), "pool_w": f(inputs["pool_w"][0]).reshape(1024, 256),
        "w_pp": f(inputs["w_proj_pool"][0]), "w_ps": f(inputs["w_proj_ssd"][0]), "w_out": f(inputs["w_out"][0]),
        "cols": f(cols), "rows": f(rows), "consts": consts, "ind": ind, "negm": negm, "rinv": rinv, "rinvR": rinvR,
    }
    maps = []
    for i in range(n_cores):
        cc = np.concatenate([c[i * NB:(i + 1) * NB], c_ctx[None, :]], axis=0)
        c3T = cc.reshape(NB + 1, 8, 128).transpose(2, 1, 0).reshape(128, 8 * (NB + 1))
        m = dict(shared)
        m["x"] = f(x[i * NB:(i + 1) * NB]); m["ctx"] = f(ctx[i * NB:(i + 1) * NB]); m["c3T"] = f(c3T)
        maps.append(m)
    return maps


def kernel(**inputs):
    n_cores = 8
    x = inputs["x"]
    Bt, L, _ = x.shape
    NB = Bt // n_cores
    nc = build_nc(NB, L)
    maps = _in_maps(inputs, n_cores, NB, L)
    res = run_bass_kernel_spmd(nc, maps, core_ids=list(range(n_cores)))
    out = np.concatenate([r["out"] for r in res.results], axis=0)
    return out.astype(np.float32)
```
